# Optimizing a Trainium2 kernel written in Bass

```python
import math
import jax, jax.numpy as jnp
from jax import lax
import numpy as np

D_MODEL = 1024
BATCH = 8
SEQ = 8192
DEPTH = 1
DEC_BATCH = 32
DEC_SEQ = 2048
PAST_LEN = 128

N_META = 16
ATTN_WIDTH = 512
HYENA_WIDTH = D_MODEL - ATTN_WIDTH
N_HEADS = 4
QK_NOPE = 128
QK_ROPE = 64
V_HEAD = ATTN_WIDTH // N_HEADS
Q_LORA = 256
KV_LORA = 128
ROPE_BASE = 10000.0
Q_BLOCK = 128
HYENA_ORDER = 2
SHORT_CONV = 3
FILTER_EMB = 33
FILTER_BANDS = (FILTER_EMB - 1) // 2
FILTER_HIDDEN = 64
N_FILTERS = HYENA_ORDER * 2 * HYENA_WIDTH
DECAY_TARGET = 1e-2
FAST_DECAY_PCT = 0.3
SLOW_DECAY_PCT = 1.5
D_FF = 4 * D_MODEL
NORM_EPS = 1e-6
IN_COLS = Q_LORA + KV_LORA + QK_ROPE + (HYENA_ORDER + 1) * HYENA_WIDTH

kernel_name = "hymba_mla_hyena_encoder"


def rms_norm(x, g):
    xf = x.astype(jnp.float32)
    y = xf * lax.rsqrt(jnp.mean(xf * xf, axis=-1, keepdims=True) + NORM_EPS)
    return (y * g.astype(jnp.float32)).astype(x.dtype)


def rope_tables(L):
    inv = 1.0 / (ROPE_BASE ** (jnp.arange(0, QK_ROPE, 2, dtype=jnp.float32) / QK_ROPE))
    ang = jnp.arange(L, dtype=jnp.float32)[:, None] * inv[None, :]
    ang = jnp.concatenate([ang, ang], axis=-1)
    return jnp.cos(ang), jnp.sin(ang)


def apply_rope(x, cos, sin):
    xf = x.astype(jnp.float32)
    x1, x2 = jnp.split(xf, 2, axis=-1)
    rot = jnp.concatenate([-x2, x1], axis=-1)
    return (xf * cos + rot * sin).astype(x.dtype)


def mla(c_q, c_kv, k_rope, q_norm_g, w_uq, kv_norm_g, w_ukv):
    B, L, _ = c_q.shape
    q = (rms_norm(c_q, q_norm_g) @ w_uq).reshape(B, L, N_HEADS, QK_NOPE + QK_ROPE)
    q_nope, q_rope = q[..., :QK_NOPE], q[..., QK_NOPE:]
    kv = (rms_norm(c_kv, kv_norm_g) @ w_ukv).reshape(B, L, N_HEADS, QK_NOPE + V_HEAD)
    k_nope, v = kv[..., :QK_NOPE], kv[..., QK_NOPE:]
    cos, sin = rope_tables(L)
    q_rope = apply_rope(q_rope, cos[None, :, None, :], sin[None, :, None, :])
    k_rope = apply_rope(k_rope, cos[None], sin[None])
    scale = (QK_NOPE + QK_ROPE) ** -0.5
    n_blk = -(-L // Q_BLOCK)
    pad = n_blk * Q_BLOCK - L

    def to_blocks(t):
        t = jnp.pad(t, ((0, 0), (0, pad), (0, 0), (0, 0)))
        return t.reshape(B, n_blk, Q_BLOCK, N_HEADS, t.shape[-1]).transpose(1, 0, 2, 3, 4)

    def attend(blk):
        qn, qr = blk
        s = (jnp.einsum('bqhd,bkhd->bhqk', qn, k_nope, preferred_element_type=jnp.float32)
             + jnp.einsum('bqhr,bkr->bhqk', qr, k_rope, preferred_element_type=jnp.float32))
        p = jax.nn.softmax(s * scale, axis=-1)
        return jnp.einsum('bhqk,bkhd->bqhd', p.astype(v.dtype), v)

    o = lax.map(attend, (to_blocks(q_nope), to_blocks(q_rope)))
    o = o.transpose(1, 0, 2, 3, 4).reshape(B, n_blk * Q_BLOCK, N_HEADS * V_HEAD)
    return o[:, :L]


def hyena_filters(L, f_w1, f_b1, f_freq, f_w2, f_b2, f_w3, f_decay):
    f32 = jnp.float32
    t = jnp.linspace(0.0, 1.0, L, dtype=f32)[:, None]
    w = 2.0 * math.pi * jnp.arange(L, dtype=f32)[:, None] / L
    f = jnp.linspace(1e-4, FILTER_BANDS - 1, FILTER_BANDS, dtype=f32)[None, :]
    feats = jnp.concatenate([t, jnp.cos(f * w), -jnp.sin(f * w)], axis=-1)
    freq = f_freq.astype(f32)
    h = jnp.sin(freq * (feats @ f_w1.astype(f32) + f_b1.astype(f32)))
    h = jnp.sin(freq * (h @ f_w2.astype(f32) + f_b2.astype(f32)))
    h = h @ f_w3.astype(f32)
    h = h * jnp.exp(-t * jnp.abs(f_decay.astype(f32)))
    h = h.reshape(L, HYENA_ORDER, 2, HYENA_WIDTH)
    h_fwd, h_bwd = h[:, :, 0], h[:, :, 1]
    zero = jnp.zeros_like(h_fwd[:1])
    k = jnp.concatenate([h_fwd, zero, h_bwd[1:][::-1]], axis=0)
    return jnp.fft.rfft(k, axis=0)


def hyena(u, conv_w, filt_fft, hy_bias):
    B, L, _ = u.shape
    C = HYENA_WIDTH
    up = jnp.pad(u, ((0, 0), (1, 1), (0, 0)))
    u = up[:, :-2] * conv_w[0] + up[:, 1:-1] * conv_w[1] + up[:, 2:] * conv_w[2]
    z = u[..., HYENA_ORDER * C:].astype(jnp.float32)
    for n in range(HYENA_ORDER):
        y = jnp.fft.irfft(jnp.fft.rfft(z, n=2 * L, axis=1) * filt_fft[None, :, n], n=2 * L, axis=1)[:, :L]
        z = u[..., n * C:(n + 1) * C].astype(jnp.float32) * (y + z * hy_bias[n].astype(jnp.float32))
    return z.astype(u.dtype)


def trunk(x, meta_tokens, pre_mix_g, w_in, q_norm_g, w_uq, kv_norm_g, w_ukv, conv_w,
          f_w1, f_b1, f_freq, f_w2, f_b2, f_w3, f_decay, hy_bias, attn_out_g, hy_out_g,
          w_o, post_mix_g, pre_mlp_g, w_ff1, w_ff2, post_mlp_g):
    B = x.shape[0]
    meta = jnp.broadcast_to(meta_tokens.astype(x.dtype)[None], (B, N_META, D_MODEL))
    h = jnp.concatenate([meta, x], axis=1)
    L = h.shape[1]
    s1 = Q_LORA
    s2 = Q_LORA + KV_LORA
    s3 = Q_LORA + KV_LORA + QK_ROPE
    for i in range(DEPTH):
        a = rms_norm(h, pre_mix_g[i])
        p = a @ w_in[i]
        c_q, c_kv, k_rope, u = p[..., :s1], p[..., s1:s2], p[..., s2:s3], p[..., s3:]
        o_attn = mla(c_q, c_kv, k_rope, q_norm_g[i], w_uq[i], kv_norm_g[i], w_ukv[i])
        filt = hyena_filters(L, f_w1[i], f_b1[i], f_freq[i], f_w2[i], f_b2[i], f_w3[i], f_decay[i])
        o_hy = hyena(u, conv_w[i], filt, hy_bias[i])
        mix = jnp.concatenate([rms_norm(o_attn, attn_out_g[i]), rms_norm(o_hy, hy_out_g[i])], axis=-1) @ w_o[i]
        h = h + rms_norm(mix, post_mix_g[i])
        m = rms_norm(h, pre_mlp_g[i]) @ w_ff1[i]
        m = jnp.square(jax.nn.relu(m)) @ w_ff2[i]
        h = h + rms_norm(m, post_mlp_g[i])
    return h[:, N_META:]


def setup_inputs(seed: int = 0) -> dict:
    key = jax.random.key(seed)
    ks = jax.random.split(key, 32)
    f32 = jnp.float32

    def nrm(k, shape, scale):
        return jax.random.normal(k, shape, f32) * scale

    def gain(k, shape):
        return 1.0 + 0.02 * jax.random.normal(k, shape, f32)

    min_decay = math.log(DECAY_TARGET) / SLOW_DECAY_PCT
    max_decay = math.log(DECAY_TARGET) / FAST_DECAY_PCT
    base = jnp.tile(jnp.abs(jnp.linspace(min_decay, max_decay, HYENA_WIDTH, dtype=f32)), HYENA_ORDER * 2)
    f_decay = base[None, :] * (1.0 + 0.05 * jax.random.normal(ks[15], (DEPTH, N_FILTERS), f32))
    return {
        "x_prompt": nrm(ks[0], (BATCH, SEQ, D_MODEL), 1.0),
        "x_sample": nrm(ks[1], (DEC_BATCH, DEC_SEQ, D_MODEL), 1.0),
        "meta_tokens": nrm(ks[2], (N_META, D_MODEL), 1.0),
        "pre_mix_g": gain(ks[3], (DEPTH, D_MODEL)),
        "w_in": nrm(ks[4], (DEPTH, D_MODEL, IN_COLS), D_MODEL ** -0.5),
        "q_norm_g": gain(ks[5], (DEPTH, Q_LORA)),
        "w_uq": nrm(ks[6], (DEPTH, Q_LORA, N_HEADS * (QK_NOPE + QK_ROPE)), Q_LORA ** -0.5),
        "kv_norm_g": gain(ks[7], (DEPTH, KV_LORA)),
        "w_ukv": nrm(ks[8], (DEPTH, KV_LORA, N_HEADS * (QK_NOPE + V_HEAD)), KV_LORA ** -0.5),
        "conv_w": nrm(ks[9], (DEPTH, SHORT_CONV, (HYENA_ORDER + 1) * HYENA_WIDTH), SHORT_CONV ** -0.5),
        "f_w1": nrm(ks[10], (DEPTH, FILTER_EMB, FILTER_HIDDEN), FILTER_EMB ** -0.5),
        "f_b1": nrm(ks[11], (DEPTH, FILTER_HIDDEN), 0.02),
        "f_freq": gain(ks[12], (DEPTH, FILTER_HIDDEN)),
        "f_w2": nrm(ks[13], (DEPTH, FILTER_HIDDEN, FILTER_HIDDEN), FILTER_HIDDEN ** -0.5),
        "f_b2": nrm(ks[14], (DEPTH, FILTER_HIDDEN), 0.02),
        "f_w3": nrm(ks[16], (DEPTH, FILTER_HIDDEN, N_FILTERS), FILTER_HIDDEN ** -0.5),
        "f_decay": f_decay,
        "hy_bias": nrm(ks[17], (DEPTH, HYENA_ORDER, HYENA_WIDTH), 1.0),
        "attn_out_g": gain(ks[18], (DEPTH, ATTN_WIDTH)),
        "hy_out_g": gain(ks[19], (DEPTH, HYENA_WIDTH)),
        "w_o": nrm(ks[20], (DEPTH, D_MODEL, D_MODEL), D_MODEL ** -0.5),
        "post_mix_g": gain(ks[21], (DEPTH, D_MODEL)),
        "pre_mlp_g": gain(ks[22], (DEPTH, D_MODEL)),
        "w_ff1": nrm(ks[23], (DEPTH, D_MODEL, D_FF), D_MODEL ** -0.5),
        "w_ff2": nrm(ks[24], (DEPTH, D_FF, D_MODEL), D_FF ** -0.5),
        "post_mlp_g": gain(ks[25], (DEPTH, D_MODEL)),
    }


def reference(x_prompt, x_sample, meta_tokens, pre_mix_g, w_in, q_norm_g, w_uq, kv_norm_g, w_ukv,
              conv_w, f_w1, f_b1, f_freq, f_w2, f_b2, f_w3, f_decay, hy_bias, attn_out_g, hy_out_g,
              w_o, post_mix_g, pre_mlp_g, w_ff1, w_ff2, post_mlp_g):
    y_prompt = trunk(x_prompt, meta_tokens, pre_mix_g, w_in, q_norm_g, w_uq, kv_norm_g, w_ukv, conv_w,
                     f_w1, f_b1, f_freq, f_w2, f_b2, f_w3, f_decay, hy_bias, attn_out_g, hy_out_g,
                     w_o, post_mix_g, pre_mlp_g, w_ff1, w_ff2, post_mlp_g)
    y_sample = trunk(x_sample, meta_tokens, pre_mix_g, w_in, q_norm_g, w_uq, kv_norm_g, w_ukv, conv_w,
                     f_w1, f_b1, f_freq, f_w2, f_b2, f_w3, f_decay, hy_bias, attn_out_g, hy_out_g,
                     w_o, post_mix_g, pre_mlp_g, w_ff1, w_ff2, post_mlp_g)
    return (y_prompt, y_sample)
```

```python
import contextlib
import numpy as np
import ml_dtypes
import concourse.bass as bass
import concourse.mybir as mybir
from concourse.bass_utils import run_bass_kernel_spmd

F32 = mybir.dt.float32
BF16 = mybir.dt.bfloat16
AF = mybir.ActivationFunctionType
ALU = mybir.AluOpType
AX = mybir.AxisListType
NPBF = ml_dtypes.bfloat16


class Buf:
    __slots__ = ("name", "multi", "w", "r")

    def __init__(self, name, multi=False):
        self.name = name
        self.multi = multi
        self.w = {}
        self.r = {}


class Op:
    __slots__ = ("eng", "fn", "deps", "dma", "chan", "sem", "val", "signal", "selfwait")

    def __init__(self, eng, fn, dma, chan):
        self.eng = eng
        self.fn = fn
        self.dma = dma
        self.chan = chan
        self.deps = []
        self.sem = None
        self.val = 0
        self.signal = dma
        self.selfwait = 0


class Prog:
    ENGS = ("pe", "act", "dve", "pool", "sp")
    NROT = {"sp": 14, "act": 10, "pool": 12}

    def __init__(self, nc):
        self.nc = nc
        self.ops = []
        self.esem = {e: nc.alloc_semaphore(name=f"s_{e}") for e in ("pe", "act", "dve", "pool")}
        self.ecnt = {e: 0 for e in self.esem}
        self.dsem = {}
        self.dval = {}
        for q, n in self.NROT.items():
            for k in range(n):
                s = nc.alloc_semaphore(name=f"d_{q}{k}")
                self.dsem[(q, k)] = s
                self.dval[(q, k)] = 0
        self.dcnt = {q: 0 for q in self.NROT}
        self.seen = {e: {} for e in self.ENGS}

    def add(self, eng, fn, reads=(), writes=(), dma=False):
        if dma:
            k = self.dcnt[eng] % self.NROT[eng]
            self.dcnt[eng] += 1
            chan = (eng, k)
        else:
            chan = eng
        op = Op(eng, fn, dma, chan)
        deps = {}

        def need_raw(p):
            return not (p.eng == "pe" and eng == "pe" and not dma and not p.dma)

        def need_w(p):
            return p.dma or dma or p.eng != eng

        for b in reads:
            for p in b.w.values():
                if need_raw(p):
                    deps[id(p)] = p
        for b in writes:
            for p in b.w.values():
                if need_w(p):
                    deps[id(p)] = p
            for p in b.r.values():
                if need_w(p) and p is not op:
                    deps[id(p)] = p
        for b in reads:
            b.r[chan] = op
        for b in writes:
            if b.multi:
                b.w[chan] = op
            else:
                b.w = {chan: op}
                b.r = {}
        op.deps = list(deps.values())
        for p in op.deps:
            p.signal = True
        self.ops.append(op)
        return op

    def dma(self, q, out, in_, reads=(), writes=()):
        return self.add(q, lambda e: e.dma_start(out=out, in_=in_), reads=reads, writes=writes, dma=True)

    def _eng(self, e):
        nc = self.nc
        return {"pe": nc.tensor, "act": nc.scalar, "dve": nc.vector, "pool": nc.gpsimd, "sp": nc.sync}[e]

    def flush(self, final=False):
        nc = self.nc
        ops = self.ops
        self.ops = []
        last = {}
        for op in ops:
            if not op.dma:
                last[op.eng] = op
        for op in last.values():
            op.signal = True
        for op in ops:
            if op.dma:
                op.sem = self.dsem[op.chan]
                op.selfwait = self.dval[op.chan]
                self.dval[op.chan] += 16
                op.val = self.dval[op.chan]
            elif op.signal:
                op.sem = self.esem[op.eng]
                self.ecnt[op.eng] += 1
                op.val = self.ecnt[op.eng]
        streams = {e: [op for op in ops if op.eng == e] for e in self.ENGS}
        bar = [(self.esem[e], self.ecnt[e], e) for e in self.esem] + \
              [(self.dsem[c], self.dval[c], c) for c in self.dsem]

        def run(ename, eng):
            seen = self.seen[ename]
            for op in streams[ename]:
                for p in op.deps:
                    if seen.get(p.chan, 0) < p.val:
                        eng.wait_ge(p.sem, p.val)
                        seen[p.chan] = p.val
                if op.dma and op.selfwait > 0 and seen.get(op.chan, 0) < op.selfwait:
                    eng.wait_ge(op.sem, op.selfwait)
                    seen[op.chan] = op.selfwait
                inst = op.fn(eng)
                if op.signal:
                    inst.then_inc(op.sem, 16 if op.dma else 1)
            for sem, val, chan in bar:
                if val > 0 and seen.get(chan, 0) < val:
                    eng.wait_ge(sem, val)
                    seen[chan] = val

        with nc.Block() as block:
            @block.tensor
            def _(e):
                run("pe", e)

            @block.scalar
            def _(e):
                run("act", e)

            @block.vector
            def _(e):
                run("dve", e)

            @block.gpsimd
            def _(e):
                run("pool", e)

            @block.sync
            def _(e):
                run("sp", e)


D = 1024
NMETA = 16
QL = 256
KVL = 128
ROPE = 64
NH = 4
HC = 512
INC = 1984
DFF = 4096
EPS = 1e-6
NG = 255
SCALE = float((128 + 64) ** -0.5)
PI = float(np.pi)


def _split(n, m=128):
    k = -(-n // m)
    base, rem = divmod(n, k)
    out, s = [], 0
    for i in range(k):
        c = base + (1 if i < rem else 0)
        out.append((s, c))
        s += c
    return out


def prow(n):
    m = n
    while m <= 128:
        if any(m % d == 0 for d in (16, 15, 14, 13, 12, 11)):
            return m
        m += 1
    return n


def _bf(a):
    return np.ascontiguousarray(np.asarray(a, dtype=np.float64).astype(np.float32).astype(NPBF))


def seq_tables(L):
    B = (L - NMETA) // 128 + 1
    PP = 2 * B - 1
    T = 128 * B
    t = {}
    inv = 1.0 / (10000.0 ** (np.arange(0, ROPE, 2, dtype=np.float64) / ROPE))
    ang = np.arange(L, dtype=np.float64)[:, None] * inv[None, :]
    ang = np.concatenate([ang, ang], -1)
    t["cosT"] = np.ascontiguousarray(np.cos(ang).T.astype(np.float32))
    t["sinT"] = np.ascontiguousarray(np.sin(ang).T.astype(np.float32))
    J = np.arange(B, dtype=np.float64)
    f = np.arange(B, dtype=np.float64)
    th = 2 * np.pi * np.outer(J, f) / PP
    t["A_re"] = _bf(np.cos(th))
    t["A_im"] = _bf(-np.sin(th))
    wf = np.where(f == 0, 1.0, 2.0)[:, None]
    thi = 2 * np.pi * np.outer(f, J) / PP
    t["IA_re"] = _bf(wf * np.cos(thi) / PP)
    t["IA_im"] = _bf(-wf * np.sin(thi) / PP)
    d = np.arange(-(B - 1), B, dtype=np.float64)
    thd = 2 * np.pi * np.outer(d, f) / PP
    t["FA_re"] = _bf(np.cos(thd))
    t["FA_im"] = _bf(-np.sin(thd))
    r = np.arange(2 * T, dtype=np.float64)
    tt = np.abs(r - T)
    tv = (tt / (L - 1)).astype(np.float32).astype(np.float64)
    w = 2.0 * np.pi * tt / L
    fb = np.linspace(1e-4, 15.0, 16)
    feats = np.concatenate([tv[:, None], np.cos(w[:, None] * fb[None]), -np.sin(w[:, None] * fb[None])], -1)
    t["featsT"] = np.ascontiguousarray(feats.T.astype(np.float32))
    t["tneg"] = np.ascontiguousarray((-tv).astype(np.float32)[:, None])
    return t


def common_tables():
    t = {}
    j = np.arange(128, dtype=np.float64)
    g = np.arange(NG, dtype=np.float64)
    th = 2 * np.pi * np.outer(j, g) / NG
    t["Bc"], t["Bs"], t["Bsn"] = _bf(np.cos(th)), _bf(np.sin(th)), _bf(-np.sin(th))
    e = np.arange(-127, 128, dtype=np.float64)
    the = 2 * np.pi * np.outer(e, g) / NG
    t["FBc"], t["FBs"], t["FBsn"] = _bf(np.cos(the)), _bf(np.sin(the)), _bf(-np.sin(the))
    thi = 2 * np.pi * np.outer(g, j) / NG
    t["IBc"], t["IBs"], t["IBsn"] = _bf(np.cos(thi) / NG), _bf(np.sin(thi) / NG), _bf(-np.sin(thi) / NG)
    t["IBcn"] = _bf(-np.cos(thi) / NG)
    t["ident"] = np.eye(128).astype(NPBF)
    t["ones"] = np.ones((128, 128)).astype(NPBF)
    t["zeros"] = np.zeros((128, 1536), NPBF)
    return t


_uid = [0]


def U(name):
    _uid[0] += 1
    return f"{name}_{_uid[0]}"


class RPool:
    def __init__(self, stack, nc, name, shape, dtype, n, psum=False):
        self.tiles = []
        for i in range(n):
            if psum:
                t = stack.enter_context(nc.psum_tensor(U(f"{name}{i}"), shape, dtype))
            else:
                t = stack.enter_context(nc.sbuf_tensor(U(f"{name}{i}"), shape, dtype))
            self.tiles.append((t, Buf(f"{name}{i}")))
        self.i = 0

    def next(self):
        t = self.tiles[self.i % len(self.tiles)]
        self.i += 1
        return t


def MM(ps, lhsT, rhs, start, stop):
    return lambda e: e.matmul(ps, lhsT=lhsT, rhs=rhs, start=start, stop=stop)


def TR(out, in_, ident):
    return lambda e: e.transpose(out=out, in_=in_, identity=ident)


def ACTF(out, in_, func, **kw):
    return lambda e: e.activation(out=out, in_=in_, func=func, **kw)


def TT(out, a, b, op):
    return lambda e: e.tensor_tensor(out=out, in0=a, in1=b, op=op)


def TS(out, in0, s1, s2=None, op0=ALU.mult, op1=None):
    if op1 is None:
        return lambda e: e.tensor_scalar(out=out, in0=in0, scalar1=s1, scalar2=None, op0=op0)
    return lambda e: e.tensor_scalar(out=out, in0=in0, scalar1=s1, scalar2=s2, op0=op0, op1=op1)


def STT(out, in0, scalar, in1, op0, op1):
    return lambda e: e.scalar_tensor_tensor(out=out, in0=in0, scalar=scalar, in1=in1, op0=op0, op1=op1)


def CP(out, in_):
    return lambda e: e.tensor_copy(out=out, in_=in_)


def MSET(ap, v):
    return lambda e: e.memset(ap, v)


def RSUM(out, in_):
    return lambda e: e.reduce_sum(out=out, in_=in_, axis=AX.X)


def RECIP(out, in_):
    return lambda e: e.reciprocal(out=out, in_=in_)


class SeqInfo:
    def __init__(self, kind, idx, Lx, toff):
        self.kind, self.idx, self.Lx = kind, idx, Lx
        self.L = Lx + NMETA
        self.n = Lx // 128
        self.B = self.n + 1
        self.T = 128 * self.B
        self.toff = toff
        self.groups = [(0, NMETA, [(0, NMETA, None)])]
        i = 0
        while i < self.n:
            k = min(4, self.n - i)
            self.groups.append((NMETA + 128 * i, 128 * k,
                                [(NMETA + 128 * (i + j), 128, 128 * (i + j)) for j in range(k)]))
            i += k


def DV(ap, offset, dims):
    return bass.AP(ap.tensor, int(offset), [list(map(int, d)) for d in dims])


def rows2d(ap, ncols, r0, nr, c0, ncl):
    return DV(ap, r0 * ncols + c0, [[ncols, nr], [1, ncl]])


class K:
    pass


def build_program(seq_specs, n_xp, Lxp, n_xs, Lxs, dbg=False, stop_after=None):
    nc = bass.Bass("TRN2", target_bir_lowering=False)
    pr = Prog(nc)
    k = K()
    k.nc, k.pr, k.dbg = nc, pr, dbg

    def din(name, shape, dt=F32):
        return nc.dram_tensor(name, list(shape), dt, kind="ExternalInput").ap()

    def dscr(name, shape, dt):
        return nc.dram_tensor(name, list(shape), dt, kind=("ExternalOutput" if dbg else "Internal")).ap()

    k.din = {}
    k.din["xp"] = din("xp", [max(n_xp, 1), Lxp, D])
    k.din["xs"] = din("xs", [max(n_xs, 1), Lxs, D])
    k.yp = nc.dram_tensor("yp", [max(n_xp, 1), Lxp, D], F32, kind="ExternalOutput").ap()
    k.ys = nc.dram_tensor("ys", [max(n_xs, 1), Lxs, D], F32, kind="ExternalOutput").ap()
    wshapes = dict(meta_tokens=[16, D], pre_mix_g=[1, D], w_in=[1, D, INC], q_norm_g=[1, QL], w_uq=[1, QL, 768],
                   kv_norm_g=[1, KVL], w_ukv=[1, KVL, 1024], conv_w=[1, 3, 1536], f_w1=[1, 33, 64], f_b1=[1, 64],
                   f_freq=[1, 64], f_w2=[1, 64, 64], f_b2=[1, 64], f_w3=[1, 64, 2048], f_decay=[1, 2048],
                   hy_bias=[1, 2, 512], attn_out_g=[1, 512], hy_out_g=[1, 512], w_o=[1, D, D], post_mix_g=[1, D],
                   pre_mlp_g=[1, D], w_ff1=[1, D, DFF], w_ff2=[1, DFF, D], post_mlp_g=[1, D])
    for nme, shp in wshapes.items():
        k.din[nme] = din(nme, shp)
    ct = common_tables()
    k.host = dict(ct)
    for nme, arr in ct.items():
        k.din[nme] = din(nme, arr.shape, BF16 if arr.dtype == NPBF else F32)
    seqs, toff = [], 0
    for kind, idx, Lx in seq_specs:
        s = SeqInfo(kind, idx, Lx, toff)
        toff += s.T
        seqs.append(s)
    k.seqs, k.Ttot = seqs, toff
    k.Ls = sorted({s.L for s in seqs})
    k.tab = {}
    for L in k.Ls:
        tb = seq_tables(L)
        for nme, arr in tb.items():
            key = f"{nme}_{L}"
            k.host[key] = arr
            k.din[key] = din(key, arr.shape, BF16 if arr.dtype == NPBF else F32)
    Tt = k.Ttot
    Bm = max(s.B for s in seqs)
    units = []
    for s_ in seqs:
        if units and units[-1]["L"] == s_.L and (units[-1]["nb"] + 1) * s_.B <= 128:
            units[-1]["nb"] += 1
        else:
            units.append(dict(L=s_.L, B=s_.B, nb=1, toff=s_.toff))
    for ui, u in enumerate(units):
        u["Be"] = u["B"] * u["nb"]
        tb = seq_tables(u["L"])
        for nm in ("A_re", "A_im", "IA_re", "IA_im"):
            arr0 = np.kron(np.eye(u["nb"], dtype=np.float32), tb[nm].astype(np.float32))
            arr = np.zeros((128, 128), np.float32)
            arr[:arr0.shape[0], :arr0.shape[1]] = arr0
            arr = arr.astype(NPBF)
            key = f"u{ui}_{nm}"
            k.host[key] = np.ascontiguousarray(arr)
            k.din[key] = din(key, arr.shape, BF16)
    k.units = units
    sc = {}
    sc["QT"] = dscr("QT", [NH, 192, Tt], BF16)
    sc["KN"] = dscr("KN", [NH, 128, Tt], BF16)
    sc["KR"] = dscr("KR", [64, Tt], BF16)
    sc["V"] = dscr("V", [Tt, 512], BF16)
    PADR = 128 * 16
    k.FD = max(prow(u["Be"]) for u in units)
    sc["UC"] = dscr("UC", [Tt + PADR, 1536], BF16)
    sc["UR"] = dscr("UR", [Tt + PADR, 1536], BF16)
    sc["Z1"] = dscr("Z1", [Tt + PADR, 512], BF16)
    sc["Z2"] = dscr("Z2", [Tt + PADR, 512], BF16)
    sc["Y"] = dscr("Y", [Tt + PADR, 512], F32)
    sc["S1"] = dscr("S1", [2, k.FD, 128, 512], BF16)
    sc["S2"] = dscr("S2", [2, k.FD, 128, 512], BF16)
    sc["AN"] = dscr("AN", [512, Tt], BF16)
    sc["H1"] = dscr("H1", [Tt, D], F32)
    for L in k.Ls:
        B = (L - NMETA) // 128 + 1
        for n in range(2):
            sc[f"KTIME{n}_{L}"] = dscr(f"KTIME{n}_{L}", [256 * B, 512], BF16)
            sc[f"KHAT{n}_{L}"] = dscr(f"KHAT{n}_{L}", [2, 128, 2, B, 512], BF16)
        sc[f"S1F_{L}"] = dscr(f"S1F_{L}", [2 * prow(B) * NG + 16, 512], BF16)
    k.sc = sc
    k.db = {nme: Buf(nme, multi=True) for nme in sc}
    k.dout = Buf("yout", multi=True)
    k.din_buf = Buf("inputs", multi=True)

    phases = [phase_F, phase_A, phase_B, phase_C, phase_D1, phase_D2]
    for ph in phases:
        ph(k)
        if stop_after == ph.__name__:
            break
    return nc, k


def xrows(k, s, xrow, nr):
    ap = k.din["xp"] if s.kind == "p" else k.din["xs"]
    Lx = s.Lx
    return DV(ap, (s.idx * Lx + xrow) * D, [[D, nr], [1, D]])


def yrows(k, s, xrow, nr):
    ap = k.yp if s.kind == "p" else k.ys
    return DV(ap, (s.idx * s.Lx + xrow) * D, [[D, nr], [1, D]])


def col1(ap, off, n=128):
    return DV(ap, off, [[1, n], [1, 1]])


def rstd_act(pr, small, junk_ap, junk_buf, src_ap, src_bufs, n_feat):
    st, sb_ = small.next()
    pr.add("act", lambda e: e.activation(out=junk_ap, in_=src_ap, func=AF.Square, accum_out=st[:, 0:1]), reads=list(src_bufs), writes=[junk_buf, sb_])
    pr.add("act", ACTF(st[:, 1:2], st[:, 0:1], AF.Ln, scale=1.0 / n_feat, bias=EPS), reads=[sb_], writes=[sb_])
    pr.add("act", ACTF(st[:, 2:3], st[:, 1:2], AF.Exp, scale=-0.5), reads=[sb_], writes=[sb_])
    return st[:, 2:3], sb_


def rstd_chain(pr, small, src_sq_ap, src_buf, n_feat, rows=128):
    st, sb_ = small.next()
    pr.add("dve", RSUM(st[0:rows, 0:1], src_sq_ap), reads=[src_buf], writes=[sb_])
    pr.add("act", ACTF(st[0:rows, 1:2], st[0:rows, 0:1], AF.Ln, scale=1.0 / n_feat, bias=EPS), reads=[sb_], writes=[sb_])
    pr.add("act", ACTF(st[0:rows, 2:3], st[0:rows, 1:2], AF.Exp, scale=-0.5), reads=[sb_], writes=[sb_])
    return st[0:rows, 2:3], sb_


def phase_A(k):
    nc, pr, W, sc, db = k.nc, k.pr, k.din, k.sc, k.db
    INB = k.din_buf
    Tt = k.Ttot
    with contextlib.ExitStack() as st:
        def sb(name, shape, dt):
            return st.enter_context(nc.sbuf_tensor(U(name), shape, dt))
        WinA, bWinA = sb("WinA", [128, 8, 512], BF16), Buf("WinA")
        Wu, bWu = sb("Wu", [128, 8, 1536], BF16), Buf("Wu")
        cwb, bcwb = sb("cwb", [128, 3, 1536], BF16), Buf("cwb")
        Wuq, bWuq = sb("Wuq", [128, 2, 768], BF16), Buf("Wuq")
        WuqR, bWuqR = sb("WuqR", [128, 2, 256], BF16), Buf("WuqR")
        Wukv, bWukv = sb("Wukv", [128, 1024], BF16), Buf("Wukv")
        WV, bWV = sb("WV", [128, 512], BF16), Buf("WV")
        gv, bgv = sb("gv", [128, 12], F32), Buf("gv")
        ident, bid = sb("identA", [128, 128], BF16), Buf("ident")
        ones, bon = sb("onesA", [128, 128], BF16), Buf("ones")
        zt, bzt = sb("zerosA", [128, 1536], BF16), Buf("zeros")
        pr.dma("sp", ident[:], W["ident"][:, :], reads=[INB], writes=[bid])
        pr.dma("sp", ones[:], W["ones"][:, :], reads=[INB], writes=[bon])
        pr.dma("sp", zt[:], W["zeros"][:, :], reads=[INB], writes=[bzt])
        for kc in range(8):
            pr.dma("sp", gv[:, kc:kc + 1], col1(W["pre_mix_g"], kc * 128), reads=[INB], writes=[bgv])
        for kc in range(2):
            pr.dma("sp", gv[:, 8 + kc:9 + kc], col1(W["q_norm_g"], kc * 128), reads=[INB], writes=[bgv])
        pr.dma("sp", gv[:, 10:11], col1(W["kv_norm_g"], 0), reads=[INB], writes=[bgv])
        with contextlib.ExitStack() as st2:
            cwB = st2.enter_context(nc.sbuf_tensor("cwB", [128, 3, 1536], F32))
            bcw = Buf("cwB")
            stg = RPool(st2, nc, "stgA", [128, INC], F32, 2)
            for i in range(3):
                pr.dma("sp", cwB[:, i, :], DV(W["conv_w"], i * 1536, [[0, 128], [1, 1536]]), reads=[INB], writes=[bcw])
            for kc in range(8):
                sg, bsg = stg.next()
                g1 = gv[:, kc:kc + 1]
                pr.dma("sp", sg[:], rows2d(W["w_in"], INC, kc * 128, 128, 0, INC), reads=[INB], writes=[bsg])
                pr.add("dve", TS(WinA[:, kc, 0:448], sg[:, 0:448], g1), reads=[bsg, bgv], writes=[bWinA])
                pr.add("dve", TS(WinA[:, kc, 448:480], sg[:, 416:448], g1, -1.0, ALU.mult, ALU.mult), reads=[bsg, bgv], writes=[bWinA])
                pr.add("dve", TS(WinA[:, kc, 480:512], sg[:, 384:416], g1), reads=[bsg, bgv], writes=[bWinA])
                pr.add("dve", TS(Wu[:, kc, :], sg[:, 448:INC], g1), reads=[bsg, bgv], writes=[bWu])
                if kc == 0:
                    pr.add("dve", CP(cwb[:].rearrange("p a c -> p (a c)"), cwB[:].rearrange("p a c -> p (a c)")), reads=[bcw], writes=[bcwb])
            for kc in range(2):
                sg, bsg = stg.next()
                g1 = gv[:, 8 + kc:9 + kc]
                pr.dma("sp", sg[:, 0:768], rows2d(W["w_uq"], 768, kc * 128, 128, 0, 768), reads=[INB], writes=[bsg])
                pr.add("dve", TS(Wuq[:, kc, :], sg[:, 0:768], g1), reads=[bsg, bgv], writes=[bWuq])
                for h in range(NH):
                    b0 = h * 192 + 128
                    pr.add("dve", TS(WuqR[:, kc, h * 64:h * 64 + 32], sg[:, b0 + 32:b0 + 64], g1, -1.0, ALU.mult, ALU.mult),
                           reads=[bsg, bgv], writes=[bWuqR])
                    pr.add("dve", TS(WuqR[:, kc, h * 64 + 32:h * 64 + 64], sg[:, b0:b0 + 32], g1), reads=[bsg, bgv], writes=[bWuqR])
            sg, bsg = stg.next()
            pr.dma("sp", sg[:, 0:1024], rows2d(W["w_ukv"], 1024, 0, 128, 0, 1024), reads=[INB], writes=[bsg])
            pr.add("dve", TS(Wukv[:], sg[:, 0:1024], gv[:, 10:11]), reads=[bsg, bgv], writes=[bWukv])
            for h in range(NH):
                pr.add("dve", TS(WV[:, h * 128:(h + 1) * 128], sg[:, h * 256 + 128:h * 256 + 256], gv[:, 10:11]),
                       reads=[bsg, bgv], writes=[bWV])
            pr.flush()
        xt = RPool(st, nc, "xtA", [128, D], F32, 3)
        sqp = RPool(st, nc, "sqA", [128, D], BF16, 2)
        small = RPool(st, nc, "smA", [128, 4], F32, 10)
        abf = RPool(st, nc, "abfA", [128, D], BF16, 2)
        aTp = RPool(st, nc, "aTA", [128, 8, 514], BF16, 3)
        tpp = RPool(st, nc, "tpA", [128, 1024], BF16, 2, psum=True)
        pm = RPool(st, nc, "pmA", [128, 512], F32, 6, psum=True)
        cqf = RPool(st, nc, "cqfA", [128, 2, 512], F32, 2)
        cqsq = RPool(st, nc, "cqsqA", [128, 2, 512], BF16, 1)
        rsp = RPool(st, nc, "rsA", [128, 512], F32, 3)
        cqn = RPool(st, nc, "cqnA", [128, 2, 512], BF16, 2)
        csp = RPool(st, nc, "csA", [64, 2, 512], F32, 2)
        rtp = RPool(st, nc, "rtA", [64, 512], F32, 4)
        oq = RPool(st, nc, "oqA", [128, 512], BF16, 4)
        orp = RPool(st, nc, "orA", [64, 512], BF16, 3)
        uev = RPool(st, nc, "uevA", [128, 1536], BF16, 2)

        ldp = RPool(st, nc, "ldA", [128, 1536], BF16, 6)
        cop = RPool(st, nc, "coA", [128, 1536], F32, 2)
        ctp = RPool(st, nc, "ctA", [128, 1536], F32, 2)
        cbp = RPool(st, nc, "cbA", [128, 1536], BF16, 2)

        def u_group(s, grp, aT_t, baT):
            t0, ntok, subs = grp
            for si, (t, nr, xr) in enumerate(subs):
                ue, bue = uev.next()
                for cg in range(3):
                    ps, bps = pm.next()
                    for kc in range(8):
                        pr.add("pe", MM(ps[0:nr, :], aT_t[:, kc, 1 + si * 128:1 + si * 128 + nr],
                                        Wu[:, kc, cg * 512:(cg + 1) * 512], kc == 0, kc == 7), reads=[baT, bWu], writes=[bps])
                    pr.add("act", ACTF(ue[0:nr, cg * 512:(cg + 1) * 512], ps[0:nr, :], AF.Copy), reads=[bps], writes=[bue])
                pr.dma("act", rows2d(sc["UR"], 1536, s.toff + t + 1, nr, 0, 1536), ue[0:nr, :], reads=[bue], writes=[db["UR"]])

        def conv_group(s, grp, aT_t=None, baT=None):
            t0, ntok, subs = grp
            for si, (t, nr, xr) in enumerate(subs):
                lds = []
                for i in range(3):
                    ld, bld = ldp.next()
                    pr.dma("sp", ld[0:nr, :], rows2d(sc["UR"], 1536, s.toff + t + i, nr, 0, 1536), reads=[db["UR"]], writes=[bld])
                    lds.append((ld, bld))
                co, bco = cop.next()
                ct, bct = ctp.next()
                cb, bcb = cbp.next()
                pr.add("dve", TT(co[0:nr, :], lds[0][0][0:nr, :], cwb[0:nr, 0, :], ALU.mult), reads=[lds[0][1], bcwb], writes=[bco])
                pr.add("dve", TT(ct[0:nr, :], lds[1][0][0:nr, :], cwb[0:nr, 1, :], ALU.mult), reads=[lds[1][1], bcwb], writes=[bct])
                pr.add("dve", TT(co[0:nr, :], co[0:nr, :], ct[0:nr, :], ALU.add), reads=[bco, bct], writes=[bco])
                pr.add("dve", TT(ct[0:nr, :], lds[2][0][0:nr, :], cwb[0:nr, 2, :], ALU.mult), reads=[lds[2][1], bcwb], writes=[bct])
                pr.add("dve", TT(cb[0:nr, :], co[0:nr, :], ct[0:nr, :], ALU.add), reads=[bco, bct], writes=[bcb])
                pr.dma("pool", rows2d(sc["UC"], 1536, s.toff + t, nr, 0, 1536), cb[0:nr, :], reads=[bcb], writes=[db["UC"]])

        def rope_out(ps1, b1, ps2, b2, cs, bcs, ntok, dst):
            t1, bt1 = rtp.next()
            t2, bt2 = rtp.next()
            pr.add("dve", TT(t1[:, :ntok], ps1[0:64, :ntok], cs[:, 0, :ntok], ALU.mult), reads=[b1, bcs], writes=[bt1])
            pr.add("dve", TT(t2[:, :ntok], ps2[0:64, :ntok], cs[:, 1, :ntok], ALU.mult), reads=[b2, bcs], writes=[bt2])
            o, bo = orp.next()
            pr.add("pool", TT(o[:, :ntok], t1[:, :ntok], t2[:, :ntok], ALU.add), reads=[bt1, bt2], writes=[bo])
            pr.dma("pool", dst, o[:, :ntok], reads=[bo], writes=[db["QT"], db["KR"]])

        def fm_norm(ps_list, nch, ntok, n_feat):
            cf, bcf = cqf.next()
            cs_, bcs_ = cqsq.next()
            for c, (ps, bps) in enumerate(ps_list):
                pr.add("act", ACTF(cf[:, c, :ntok], ps[:, :ntok], AF.Copy), reads=[bps], writes=[bcf])
                pr.add("act", ACTF(cs_[:, c, :ntok], ps[:, :ntok], AF.Square), reads=[bps], writes=[bcs_])
            pss, bpss = pm.next()
            for c in range(nch):
                pr.add("pe", MM(pss[:, :ntok], ones[:], cs_[:, c, :ntok], c == 0, c == nch - 1), reads=[bon, bcs_], writes=[bpss])
            l1, bl1 = rsp.next()
            pr.add("act", ACTF(l1[:, :ntok], pss[:, :ntok], AF.Ln, scale=1.0 / n_feat, bias=EPS), reads=[bpss], writes=[bl1])
            r1, br1 = rsp.next()
            pr.add("act", ACTF(r1[:, :ntok], l1[:, :ntok], AF.Exp, scale=-0.5), reads=[bl1], writes=[br1])
            cn, bcn = cqn.next()
            for c in range(nch):
                pr.add("dve", TT(cn[:, c, :ntok], cf[:, c, :ntok], r1[:, :ntok], ALU.mult), reads=[bcf, br1], writes=[bcn])
            return cn, bcn

        for s in k.seqs:
            L = s.L
            cosT, sinT = W[f"cosT_{L}"], W[f"sinT_{L}"]
            pr.dma("act", rows2d(sc["UC"], 1536, s.toff + L, s.T - L, 0, 1536), zt[0:s.T - L, :], reads=[bzt], writes=[db["UC"]])
            pr.dma("act", rows2d(sc["UR"], 1536, s.toff, 1, 0, 1536), zt[0:1, :], reads=[bzt], writes=[db["UR"]])
            pr.dma("act", rows2d(sc["UR"], 1536, s.toff + L + 1, 1, 0, 1536), zt[0:1, :], reads=[bzt], writes=[db["UR"]])
            ng = len(s.groups)

            def SX(gi):
                t0, ntok, subs = s.groups[gi]
                aT_t, baT = aTp.next()
                for si, (t, nr, xr) in enumerate(subs):
                    x_t, bx = xt.next()
                    if xr is None:
                        pr.add("pool", MSET(x_t[:], 0.0), writes=[bx])
                        pr.dma("sp", x_t[0:NMETA, :], W["meta_tokens"][:, :], reads=[INB], writes=[bx])
                    else:
                        pr.dma("sp", x_t[:], xrows(k, s, xr, 128), reads=[INB], writes=[bx])
                    sq_t, bsq = sqp.next()
                    rstd, brs = rstd_act(pr, small, sq_t[:], bsq, x_t[:], [bx], D)
                    a_t, ba = abf.next()
                    pr.add("act", ACTF(a_t[:], x_t[:], AF.Copy, scale=rstd), reads=[bx, brs], writes=[ba])
                    tp, btp = tpp.next()
                    for kc in range(8):
                        pr.add("pe", TR(tp[:, kc * 128:(kc + 1) * 128], a_t[:, kc * 128:(kc + 1) * 128], ident[:]),
                               reads=[ba, bid], writes=[btp])
                    ncol = nr if xr is None else 128
                    pr.add("dve", CP(aT_t[:, :, 1 + si * 128:1 + si * 128 + ncol],
                                     tp[:].rearrange("p (k t) -> p k t", k=8)[:, :, 0:ncol]), reads=[btp], writes=[baT])
                cs, bcs = csp.next()
                pr.dma("sp", cs[:, 0, :ntok], rows2d(cosT, L, 0, 64, t0, ntok), reads=[INB], writes=[bcs])
                pr.dma("sp", cs[:, 1, :ntok], rows2d(sinT, L, 0, 64, t0, ntok), reads=[INB], writes=[bcs])
                return (aT_t, baT, cs, bcs)

            def SP(gi, X, nextX):
                t0, ntok, subs = s.groups[gi]
                aT_t, baT, cs, bcs = X
                rhsA = [aT_t[:, kc, 1:1 + ntok] for kc in range(8)]
                col0 = s.toff + t0

                def proj(c0, m):
                    ps, bps = pm.next()
                    for kc in range(8):
                        pr.add("pe", MM(ps[0:m, :ntok], WinA[:, kc, c0:c0 + m], rhsA[kc], kc == 0, kc == 7),
                               reads=[bWinA, baT], writes=[bps])
                    return ps, bps
                pcq = [proj(0, 128), proj(128, 128)]
                pkv = [proj(256, 128)]
                pk1, bk1 = proj(384, 64)
                pk2, bk2 = proj(448, 64)
                Xn = nextX() if nextX is not None else None
                cn, bcn = fm_norm(pcq, 2, ntok, QL)
                kn, bkn = fm_norm(pkv, 1, ntok, KVL)
                rope_out(pk1, bk1, pk2, bk2, cs, bcs, ntok, DV(sc["KR"], col0, [[Tt, 64], [1, ntok]]))
                u_group(s, s.groups[gi], aT_t, baT)
                for h in range(NH):
                    ps, bps = pm.next()
                    for kc in range(2):
                        pr.add("pe", MM(ps[:, :ntok], Wuq[:, kc, h * 192:h * 192 + 128], cn[:, kc, :ntok], kc == 0, kc == 1),
                               reads=[bWuq, bcn], writes=[bps])
                    o, bo = oq.next()
                    pr.add("act", ACTF(o[:, :ntok], ps[:, :ntok], AF.Copy), reads=[bps], writes=[bo])
                    pr.dma("act", DV(sc["QT"], (h * 192) * Tt + col0, [[Tt, 128], [1, ntok]]), o[:, :ntok], reads=[bo], writes=[db["QT"]])
                    ps1, b1 = pm.next()
                    for kc in range(2):
                        pr.add("pe", MM(ps1[0:64, :ntok], Wuq[:, kc, h * 192 + 128:h * 192 + 192], cn[:, kc, :ntok], kc == 0, kc == 1),
                               reads=[bWuq, bcn], writes=[b1])
                    ps2, b2 = pm.next()
                    for kc in range(2):
                        pr.add("pe", MM(ps2[0:64, :ntok], WuqR[:, kc, h * 64:(h + 1) * 64], cn[:, kc, :ntok], kc == 0, kc == 1),
                               reads=[bWuqR, bcn], writes=[b2])
                    rope_out(ps1, b1, ps2, b2, cs, bcs, ntok, DV(sc["QT"], (h * 192 + 128) * Tt + col0, [[Tt, 64], [1, ntok]]))
                for h in range(NH):
                    ps, bps = pm.next()
                    pr.add("pe", MM(ps[:, :ntok], Wukv[:, h * 256:h * 256 + 128], kn[:, 0, :ntok], True, True),
                           reads=[bWukv, bkn], writes=[bps])
                    o, bo = oq.next()
                    pr.add("act", ACTF(o[:, :ntok], ps[:, :ntok], AF.Copy), reads=[bps], writes=[bo])
                    pr.dma("act", DV(sc["KN"], (h * 128) * Tt + col0, [[Tt, 128], [1, ntok]]), o[:, :ntok], reads=[bo], writes=[db["KN"]])
                for si, (t, nr, xr) in enumerate(subs):
                    ps, bps = pm.next()
                    pr.add("pe", MM(ps[0:nr, :], kn[:, 0, si * 128:si * 128 + nr], WV[:], True, True), reads=[bWV, bkn], writes=[bps])
                    o, bo = oq.next()
                    pr.add("act", ACTF(o[0:nr, :], ps[0:nr, :], AF.Copy), reads=[bps], writes=[bo])
                    pr.dma("act", rows2d(sc["V"], 512, s.toff + t, nr, 0, 512), o[0:nr, :], reads=[bo], writes=[db["V"]])
                return Xn
            X = SX(0)
            for gi in range(ng):
                X = SP(gi, X, (lambda g=gi: SX(g + 1)) if gi + 1 < ng else None)
                if gi >= 1:
                    conv_group(s, s.groups[gi - 1])
            conv_group(s, s.groups[ng - 1])
        pr.flush()


def phase_B(k):
    nc, pr, W, sc, db = k.nc, k.pr, k.din, k.sc, k.db
    INB = k.din_buf
    Tt = k.Ttot
    Lm = max(s.L for s in k.seqs)
    Bm = max(s.B for s in k.seqs)
    with contextlib.ExitStack() as st:
        def sb(name, shape, dt):
            return st.enter_context(nc.sbuf_tensor(U(name), shape, dt))
        KNt, bKN = sb("KNt", [128, NH, Lm], BF16), Buf("KNt")
        KRt, bKR = sb("KRt", [128, Lm], BF16), Buf("KRt")
        Vt, bV = sb("Vt", [128, Bm, 512], BF16), Buf("Vt")
        ones, bon = sb("onesB", [128, 128], BF16), Buf("ones")
        pr.dma("sp", ones[:], W["ones"][:, :], reads=[INB], writes=[bon])
        onesf, bonf = sb("onesfB", [128, 128], F32), Buf("onesf")
        pr.add("pool", MSET(onesf[:], 1.0), writes=[bonf])
        accp = RPool(st, nc, "accB", [128, 2, 512], F32, 2)
        qnp = RPool(st, nc, "qnB", [128, 512], BF16, 2)
        qrp = RPool(st, nc, "qrB", [128, 512], BF16, 2)
        pr.add("dve", MSET(KRt[64:128, :], 0.0), writes=[bKR])
        for t_, b_ in qrp.tiles:
            pr.add("dve", MSET(t_[64:128, :], 0.0), writes=[b_])
        ptp = RPool(st, nc, "ptB", [128, 2, 512], BF16, 4)
        ptb2 = {id(t_): Buf("ptslot1") for t_, b_ in ptp.tiles}
        rsp = RPool(st, nc, "rsB", [128, 512], F32, 2)
        ohp = RPool(st, nc, "ohB", [128, NH, 512], F32, 2)
        sqp = RPool(st, nc, "sqB", [128, NH, 512], BF16, 1)
        anp = RPool(st, nc, "anB", [128, NH, 512], BF16, 2)
        pss = RPool(st, nc, "pssB", [128, 512], F32, 4, psum=True)
        pop = RPool(st, nc, "poB", [128, 512], F32, 2, psum=True)
        psm = RPool(st, nc, "psmB", [128, 512], F32, 1, psum=True)
        pep = RPool(st, nc, "pepB", [128, 512], F32, 1, psum=True)
        for s in k.seqs:
            L, n = s.L, s.n
            for h in range(NH):
                pr.dma("sp", KNt[:, h, 0:L], DV(sc["KN"], h * 128 * Tt + s.toff, [[Tt, 128], [1, L]]), reads=[db["KN"]], writes=[bKN])
            pr.dma("sp", KRt[0:64, 0:L], DV(sc["KR"], s.toff, [[Tt, 64], [1, L]]), reads=[db["KR"]], writes=[bKR])
            pr.dma("sp", Vt[0:NMETA, 0, :], rows2d(sc["V"], 512, s.toff, NMETA, 0, 512), reads=[db["V"]], writes=[bV])
            i = 0
            while i < n:
                c = min(16, n - i)
                pr.dma("sp", Vt[:, 1 + i:1 + i + c, :], DV(sc["V"], (s.toff + NMETA + 128 * i) * 512, [[512, 128], [128 * 512, c], [1, 512]]),
                       reads=[db["V"]], writes=[bV])
                i += c
            ktiles = [(0, NMETA)] + [(NMETA + 128 * i, 128) for i in range(n)]
            for (t0, ntok, subs) in s.groups[1:]:
                col0 = s.toff + t0
                oh, boh = ohp.next()
                for h in range(NH):
                    qn, bqn = qnp.next()
                    qr, bqr = qrp.next()
                    pr.dma("sp", qn[:, :ntok], DV(sc["QT"], h * 192 * Tt + col0, [[Tt, 128], [1, ntok]]), reads=[db["QT"]], writes=[bqn])
                    pr.dma("sp", qr[0:64, :ntok], DV(sc["QT"], (h * 192 + 128) * Tt + col0, [[Tt, 64], [1, ntok]]), reads=[db["QT"]], writes=[bqr])
                    po, bpo = pop.next()
                    pm_, bpm = psm.next()
                    ac, bac = accp.next()
                    pr.add("pool", MSET(ac[:].rearrange("p a c -> p (a c)"), 0.0), writes=[bac])
                    pstate = {}

                    def s_mm(kt):
                        c0, nk = ktiles[kt]
                        ps, bps = pss.next()
                        pr.add("pe", MM(ps[0:nk, :ntok], KNt[:, h, c0:c0 + nk], qn[:, :ntok], True, False), reads=[bKN, bqn], writes=[bps])
                        pr.add("pe", MM(ps[0:nk, :ntok], KRt[:, c0:c0 + nk], qr[:, :ntok], False, True), reads=[bKR, bqr], writes=[bps])
                        slot = 0 if kt == 0 else (kt - 1) % 2
                        if kt == 0 or slot == 0:
                            pstate["cur"] = ptp.next()
                        ptt, bp0 = pstate["cur"]
                        bpt = bp0 if slot == 0 else ptb2[id(ptt)]
                        pt = ptt[:, slot, :]
                        pr.add("act", ACTF(pt[0:nk, :ntok], ps[0:nk, :ntok], AF.Exp, scale=SCALE), reads=[bps], writes=[bpt])
                        if kt == 0:
                            pr.add("dve", TT(ac[0:nk, 0, :ntok], ac[0:nk, 0, :ntok], pt[0:nk, :ntok], ALU.add), reads=[bac, bpt], writes=[bac])
                        elif slot == 1:
                            pr.add("dve", TT(ac[:, :, :ntok], ac[:, :, :ntok], ptt[:, :, :ntok], ALU.add), reads=[bac, bp0, bpt], writes=[bac])
                        elif kt == nkt - 1:
                            pr.add("dve", TT(ac[:, 0, :ntok], ac[:, 0, :ntok], pt[:, :ntok], ALU.add), reads=[bac, bpt], writes=[bac])
                        return pt, bpt, nk
                    nkt = len(ktiles)
                    LA = 2
                    q_ = [s_mm(i) for i in range(min(LA, nkt))]
                    for kt in range(nkt):
                        if kt + LA < nkt:
                            q_.append(s_mm(kt + LA))
                        pt, bpt, nk = q_.pop(0)
                        pr.add("pe", MM(po[:, :ntok], Vt[0:nk, kt, h * 128:(h + 1) * 128], pt[0:nk, :ntok], kt == 0, kt == nkt - 1),
                               reads=[bV, bpt], writes=[bpo])
                    for ai in range(2):
                        pr.add("pe", MM(pm_[:, :ntok], onesf[:], ac[:, ai, :ntok], ai == 0, ai == 1), reads=[bonf, bac], writes=[bpm])
                    rs, brs = rsp.next()
                    pr.add("dve", RECIP(rs[:, :ntok], pm_[:, :ntok]), reads=[bpm], writes=[brs])
                    pr.add("dve", TT(oh[:, h, :ntok], po[:, :ntok], rs[:, :ntok], ALU.mult), reads=[bpo, brs], writes=[boh])
                sq, bsq = sqp.next()
                pe_, bpe = pep.next()
                for h in range(NH):
                    pr.add("act", ACTF(sq[:, h, :ntok], oh[:, h, :ntok], AF.Square), reads=[boh], writes=[bsq])
                for h in range(NH):
                    pr.add("pe", MM(pe_[:, :ntok], ones[:], sq[:, h, :ntok], h == 0, h == NH - 1), reads=[bon, bsq], writes=[bpe])
                l1, bl1 = rsp.next()
                pr.add("act", ACTF(l1[:, :ntok], pe_[:, :ntok], AF.Ln, scale=1.0 / 512, bias=EPS), reads=[bpe], writes=[bl1])
                r1, br1 = rsp.next()
                pr.add("act", ACTF(r1[:, :ntok], l1[:, :ntok], AF.Exp, scale=-0.5), reads=[bl1], writes=[br1])
                an, ban = anp.next()
                for h in range(NH):
                    pr.add("dve" if h % 2 == 0 else "pool", TT(an[:, h, :ntok], oh[:, h, :ntok], r1[:, :ntok], ALU.mult), reads=[boh, br1], writes=[ban])
                pr.dma("pool", DV(sc["AN"], col0, [[Tt, 128], [128 * Tt, NH], [1, ntok]]), an[:, :, :ntok], reads=[ban], writes=[db["AN"]])
        pr.flush()


def phase_D1(k):
    nc, pr, W, sc, db = k.nc, k.pr, k.din, k.sc, k.db
    INB = k.din_buf
    Tt = k.Ttot
    with contextlib.ExitStack() as st:
        def sb(name, shape, dt):
            return st.enter_context(nc.sbuf_tensor(U(name), shape, dt))
        Wo, bWo = sb("Wo", [128, 8, D], BF16), Buf("Wo")
        gB, bgB = sb("gB1", [128, D], F32), Buf("gB1")
        gv, bgv = sb("gv1", [128, 8], F32), Buf("gv1")
        ident, bid = sb("identD1", [128, 128], BF16), Buf("ident")
        pr.dma("sp", ident[:], W["ident"][:, :], reads=[INB], writes=[bid])
        pr.dma("sp", gB[:], DV(W["post_mix_g"], 0, [[0, 128], [1, D]]), reads=[INB], writes=[bgB])
        for kc in range(4):
            pr.dma("sp", gv[:, kc:kc + 1], col1(W["attn_out_g"], kc * 128), reads=[INB], writes=[bgv])
            pr.dma("sp", gv[:, 4 + kc:5 + kc], col1(W["hy_out_g"], kc * 128), reads=[INB], writes=[bgv])
        stg = RPool(st, nc, "stgD1", [128, D], F32, 2)
        for kc in range(8):
            sg, bsg = stg.next()
            pr.dma("sp", sg[:], rows2d(W["w_o"], D, kc * 128, 128, 0, D), reads=[INB], writes=[bsg])
            pr.add("dve", TS(Wo[:, kc, :], sg[:], gv[:, kc:kc + 1]), reads=[bsg, bgv], writes=[bWo])
        anp = RPool(st, nc, "anD1", [128, NH, 512], BF16, 3)
        zp = RPool(st, nc, "zD1", [128, 512], BF16, 4)
        xp = RPool(st, nc, "xD1", [128, D], F32, 5)
        sqp = RPool(st, nc, "sqD1", [128, D], BF16, 3)
        small = RPool(st, nc, "smD1", [128, 4], F32, 12)
        hnp = RPool(st, nc, "hnD1", [128, 512], BF16, 3)
        hTp = RPool(st, nc, "hTD1", [128, 4, 128], BF16, 5)
        tpp = RPool(st, nc, "tpD1", [128, 512], BF16, 2, psum=True)
        pmm = RPool(st, nc, "pmD1", [128, 1024], F32, 3, psum=True)
        tp_ = RPool(st, nc, "tD1", [128, D], F32, 2)
        hp = RPool(st, nc, "hD1", [128, D], F32, 2)
        work = []
        for s in k.seqs:
            for (t0, ntok, subs) in s.groups[1:]:
                for si, sub in enumerate(subs):
                    work.append((s, t0, ntok, si, sub))

        def stX(w):
            s, t0, ntok, si, (t, nr, xr) = w
            col0 = s.toff + t0
            if si == 0:
                an, ban = anp.next()
                pr.dma("sp", an[:, :, :ntok], DV(sc["AN"], col0, [[Tt, 128], [128 * Tt, NH], [1, ntok]]), reads=[db["AN"]], writes=[ban])
                stX.an = (an, ban)
            an, ban = stX.an
            z, bz = zp.next()
            pr.dma("sp", z[:], rows2d(sc["Z2"], 512, s.toff + t, 128, 0, 512), reads=[db["Z2"]], writes=[bz])
            x_t, bx = xp.next()
            pr.dma("sp", x_t[:], xrows(k, s, xr, 128), reads=[INB], writes=[bx])
            sq, bsq = sqp.next()
            rstd, brs = rstd_act(pr, small, sq[:, 0:512], bsq, z[:], [bz], 512)
            hn, bhn = hnp.next()
            pr.add("act", ACTF(hn[:], z[:], AF.Copy, scale=rstd), reads=[bz, brs], writes=[bhn])
            tp, btp = tpp.next()
            for kc in range(4):
                pr.add("pe", TR(tp[:, kc * 128:(kc + 1) * 128], hn[:, kc * 128:(kc + 1) * 128], ident[:]), reads=[bhn, bid], writes=[btp])
            hT, bhT = hTp.next()
            pr.add("dve", CP(hT[:], tp[:].rearrange("p (k t) -> p k t", k=4)), reads=[btp], writes=[bhT])
            return (an, ban, hT, bhT, x_t, bx)

        def stYm(w, xs_):
            s, t0, ntok, si, (t, nr, xr) = w
            an, ban, hT, bhT, x_t, bx = xs_
            ps, bps = pmm.next()
            for half in range(2):
                for kc in range(4):
                    pr.add("pe", MM(ps[:, half * 512:(half + 1) * 512], an[:, kc, si * 128:(si + 1) * 128], Wo[:, kc, half * 512:(half + 1) * 512], kc == 0, False),
                           reads=[ban, bWo], writes=[bps])
                for kc in range(4):
                    pr.add("pe", MM(ps[:, half * 512:(half + 1) * 512], hT[:, kc, :], Wo[:, 4 + kc, half * 512:(half + 1) * 512], False, kc == 3),
                           reads=[bhT, bWo], writes=[bps])
            return ps, bps

        def stYe(w, xs_, pp_):
            s, t0, ntok, si, (t, nr, xr) = w
            an, ban, hT, bhT, x_t, bx = xs_
            ps, bps = pp_
            sq2, bsq2 = sqp.next()
            rstd2, brs2 = rstd_act(pr, small, sq2[:], bsq2, ps[:], [bps], D)
            tt, btt = tp_.next()
            pr.add("dve", STT(tt[:], ps[:], rstd2, gB[:], ALU.mult, ALU.mult), reads=[bps, brs2, bgB], writes=[btt])
            h1, bh1 = hp.next()
            pr.add("dve", TT(h1[:], tt[:], x_t[:], ALU.add), reads=[btt, bx], writes=[bh1])
            pr.dma("pool", rows2d(sc["H1"], D, s.toff + t, 128, 0, D), h1[:], reads=[bh1], writes=[db["H1"]])
        LOOK = 2
        pend = {}
        for i in range(min(LOOK, len(work))):
            pend[i] = stX(work[i])
        for i in range(len(work)):
            pp_ = stYm(work[i], pend[i])
            if i + LOOK < len(work):
                pend[i + LOOK] = stX(work[i + LOOK])
            stYe(work[i], pend.pop(i), pp_)
        pr.flush()


GD2 = 2


def phase_D2(k):
    nc, pr, W, sc, db = k.nc, k.pr, k.din, k.sc, k.db
    INB = k.din_buf
    NT = GD2 * 128
    with contextlib.ExitStack() as st:
        def sb(name, shape, dt):
            return st.enter_context(nc.sbuf_tensor(U(name), shape, dt))
        W1, bW1 = sb("W1", [128, 8, DFF], BF16), Buf("W1")
        W2, bW2 = sb("W2", [128, 32, D], BF16), Buf("W2")
        gB, bgB = sb("gB2", [128, D], F32), Buf("gB2")
        gv, bgv = sb("gv2", [128, 8], F32), Buf("gv2")
        ident, bid = sb("identD2", [128, 128], BF16), Buf("ident")
        pr.dma("sp", ident[:], W["ident"][:, :], reads=[INB], writes=[bid])
        pr.dma("sp", gB[:], DV(W["post_mlp_g"], 0, [[0, 128], [1, D]]), reads=[INB], writes=[bgB])
        for kc in range(8):
            pr.dma("sp", gv[:, kc:kc + 1], col1(W["pre_mlp_g"], kc * 128), reads=[INB], writes=[bgv])
        with contextlib.ExitStack() as st2:
            stg = RPool(st2, nc, "stgD2", [128, DFF], F32, 3)
            for kc in range(8):
                sg, bsg = stg.next()
                pr.dma("sp", sg[:], rows2d(W["w_ff1"], DFF, kc * 128, 128, 0, DFF), reads=[INB], writes=[bsg])
                if kc % 2 == 0:
                    pr.add("dve", TS(W1[:, kc, :], sg[:], gv[:, kc:kc + 1]), reads=[bsg, bgv], writes=[bW1])
                else:
                    pr.add("act", ACTF(W1[:, kc, :], sg[:], AF.Copy, scale=gv[:, kc:kc + 1]), reads=[bsg, bgv], writes=[bW1])
            for f4 in range(8):
                sg, bsg = stg.next()
                pr.dma("sp", sg[:].rearrange("p (a c) -> p a c", a=4),
                       DV(W["w_ff2"], f4 * 4 * 128 * D, [[D, 128], [128 * D, 4], [1, D]]), reads=[INB], writes=[bsg])
                eng = ("dve", "act")[f4 % 2]
                if eng == "act":
                    pr.add("act", ACTF(W2[:, f4 * 4:(f4 + 1) * 4, :], sg[:].rearrange("p (a c) -> p a c", a=4), AF.Copy), reads=[bsg], writes=[bW2])
                else:
                    pr.add(eng, CP(W2[:, f4 * 4:(f4 + 1) * 4, :], sg[:].rearrange("p (a c) -> p a c", a=4)), reads=[bsg], writes=[bW2])
            pr.flush()
        hp = RPool(st, nc, "hD2", [128, D], F32, 3)
        h2p = RPool(st, nc, "h2D2", [128, D], F32, 2)
        sqp = RPool(st, nc, "sqD2", [128, D], BF16, 2)
        small = RPool(st, nc, "smD2", [128, 4], F32, 12)
        abf = RPool(st, nc, "abfD2", [128, D], BF16, 4)
        aTp = RPool(st, nc, "aTD2", [128, 8, NT], BF16, 2)
        mT = sb("mT", [128, 32, NT], BF16)
        bmT = [Buf(f"mT{i}") for i in range(32)]
        rp = RPool(st, nc, "rD2", [128, NT], F32, 3)
        tpp = RPool(st, nc, "tpD2", [128, 1024], BF16, 1, psum=True)
        pmm = RPool(st, nc, "pmD2", [128, 512], F32, 3, psum=True)
        pm2 = RPool(st, nc, "pm2D2", [128, 1024], F32, 2, psum=True)
        tp_ = RPool(st, nc, "tD2", [128, D], F32, 2)
        yp_ = RPool(st, nc, "yD2", [128, D], F32, 2)
        work = []
        for s in k.seqs:
            subs_all = [sub for g in s.groups[1:] for sub in g[2]]
            for g0 in range(0, len(subs_all), GD2):
                work.append((s, subs_all[g0:g0 + GD2]))

        def stXa(w, sis=None):
            s, subs = w
            outs = []
            for si, (t, nr, xr) in enumerate(subs):
                if sis is not None and si not in sis:
                    continue
                h1, bh1 = hp.next()
                pr.dma("sp", h1[:], rows2d(sc["H1"], D, s.toff + t, 128, 0, D), reads=[db["H1"]], writes=[bh1])
                sq, bsq = sqp.next()
                rstd, brs = rstd_act(pr, small, sq[:], bsq, h1[:], [bh1], D)
                a_t, ba = abf.next()
                pr.add("act", ACTF(a_t[:], h1[:], AF.Copy, scale=rstd), reads=[bh1, brs], writes=[ba])
                outs.append((a_t, ba))
            return outs

        def stXb(w, outs):
            s, subs = w
            aT, baT = aTp.next()
            for si, (t, nr, xr) in enumerate(subs):
                a_t, ba = outs[si]
                tp, btp = tpp.next()
                for kc in range(8):
                    pr.add("pe", TR(tp[:, kc * 128:(kc + 1) * 128], a_t[:, kc * 128:(kc + 1) * 128], ident[:]), reads=[ba, bid], writes=[btp])
                pr.add("dve", CP(aT[:, :, si * 128:(si + 1) * 128], tp[:].rearrange("p (k t) -> p k t", k=8)), reads=[btp], writes=[baT])
            return aT, baT

        def stF1(w, aTs, fcs):
            s, subs = w
            aT, baT = aTs
            ntok = 128 * len(subs)
            for fc in fcs:
                ps, bps = pmm.next()
                for kc in range(8):
                    pr.add("pe", MM(ps[:, :ntok], W1[:, kc, fc * 128:(fc + 1) * 128], aT[:, kc, :ntok], kc == 0, kc == 7),
                           reads=[bW1, baT], writes=[bps])
                r, br = rp.next()
                pr.add("act", ACTF(r[:, :ntok], ps[:, :ntok], AF.Relu), reads=[bps], writes=[br])
                pr.add("dve" if fc % 4 != 3 else "pool", TT(mT[:, fc, :ntok], r[:, :ntok], r[:, :ntok], ALU.mult), reads=[br], writes=[bmT[fc]])

        def stF2(w):
            s, subs = w
            for si, (t, nr, xr) in enumerate(subs):
                ps, bps = pm2.next()
                for half in range(2):
                    for fc in range(32):
                        pr.add("pe", MM(ps[:, half * 512:(half + 1) * 512], mT[:, fc, si * 128:(si + 1) * 128], W2[:, fc, half * 512:(half + 1) * 512], fc == 0, fc == 31),
                               reads=[bmT[fc], bW2], writes=[bps])
                sq2, bsq2 = sqp.next()
                rstd2, brs2 = rstd_act(pr, small, sq2[:], bsq2, ps[:], [bps], D)
                tt, btt = tp_.next()
                pr.add("dve", STT(tt[:], ps[:], rstd2, gB[:], ALU.mult, ALU.mult), reads=[bps, brs2, bgB], writes=[btt])
                h2, bh2 = h2p.next()
                pr.dma("sp", h2[:], rows2d(sc["H1"], D, s.toff + t, 128, 0, D), reads=[db["H1"]], writes=[bh2])
                y, by = yp_.next()
                pr.add("dve", TT(y[:], tt[:], h2[:], ALU.add), reads=[btt, bh2], writes=[by])
                pr.dma("pool", yrows(k, s, xr, 128), y[:], reads=[by], writes=[k.dout])
        cur = stXb(work[0], stXa(work[0]))
        for i, w in enumerate(work):
            has_next = i + 1 < len(work)
            nsub = len(work[i + 1][1]) if has_next else 0
            stF1(w, cur, range(0, 5))
            nxa = stXa(work[i + 1], [0]) if has_next else []
            stF1(w, cur, range(5, 11))
            if has_next and nsub > 1:
                nxa = nxa + stXa(work[i + 1], list(range(1, nsub)))
            stF1(w, cur, range(11, 20))
            nxt = stXb(work[i + 1], nxa) if has_next else None
            stF1(w, cur, range(20, 32))
            stF2(w)
            cur = nxt
        pr.flush()


GCH = [(0, 128), (128, 127)]


def load_chunked(pr, st, nc, name, src, nrows, ncols, chunks, INB):
    out = []
    for ci, (r0, rn) in enumerate(chunks):
        t = st.enter_context(nc.sbuf_tensor(U(f"{name}{ci}"), [128, ncols], BF16))
        b = Buf(f"{name}{ci}")
        pr.dma("sp", t[0:rn, :], rows2d(src, ncols, r0, rn, 0, ncols), reads=[INB], writes=[b])
        out.append((t, b, rn))
    return out


def phase_F(k):
    nc, pr, W, sc, db = k.nc, k.pr, k.din, k.sc, k.db
    INB = k.din_buf
    with contextlib.ExitStack() as st:
        def sb(name, shape, dt):
            return st.enter_context(nc.sbuf_tensor(U(name), shape, dt))
        fw1, fw2, fw3 = sb("fw1", [33, 64], F32), sb("fw2", [64, 64], F32), sb("fw3", [64, 2048], F32)
        fv, dB = sb("fv", [64, 8], F32), sb("dB", [128, 2048], F32)
        bw, bfv, bdB = Buf("fw"), Buf("fv"), Buf("dB")
        pr.dma("sp", fw1[:], rows2d(W["f_w1"], 64, 0, 33, 0, 64), reads=[INB], writes=[bw])
        pr.dma("sp", fw2[:], rows2d(W["f_w2"], 64, 0, 64, 0, 64), reads=[INB], writes=[bw])
        pr.dma("sp", fw3[:], rows2d(W["f_w3"], 2048, 0, 64, 0, 2048), reads=[INB], writes=[bw])
        fw3b, bw3b = sb("fw3b", [64, 2048], BF16), Buf("fw3b")
        pr.add("dve", CP(fw3b[:], fw3[:]), reads=[bw], writes=[bw3b])
        pr.dma("sp", fv[:, 0:1], col1(W["f_freq"], 0, 64), reads=[INB], writes=[bfv])
        pr.dma("sp", fv[:, 1:2], col1(W["f_b1"], 0, 64), reads=[INB], writes=[bfv])
        pr.dma("sp", fv[:, 2:3], col1(W["f_b2"], 0, 64), reads=[INB], writes=[bfv])
        pr.add("dve", TT(fv[:, 3:4], fv[:, 1:2], fv[:, 0:1], ALU.mult), reads=[bfv], writes=[bfv])
        pr.add("dve", TT(fv[:, 4:5], fv[:, 2:3], fv[:, 0:1], ALU.mult), reads=[bfv], writes=[bfv])
        pr.dma("sp", dB[:], DV(W["f_decay"], 0, [[0, 128], [1, 2048]]), reads=[INB], writes=[bdB])
        pr.add("act", ACTF(dB[:], dB[:], AF.Abs), reads=[bdB], writes=[bdB])
        hb, bhb = sb("hbias", [1, 2, 512], F32), Buf("hbias")
        pr.dma("sp", hb[:], DV(W["hy_bias"], 0, [[0, 1], [512, 2], [1, 512]]), reads=[INB], writes=[bhb])
        FB_ = {nm: load_chunked(pr, st, nc, f"F{nm}", W[nm], NG, NG, GCH, INB) for nm in ("FBc", "FBs", "FBsn")}
        for L in k.Ls:
            B = (L - NMETA) // 128 + 1
            BP = prow(B)
            T = 128 * B
            PP = 2 * B - 1
            featsT, tneg = W[f"featsT_{L}"], W[f"tneg_{L}"]
            with contextlib.ExitStack() as s2:
                CH = 512
                ftp = RPool(s2, nc, "ftF", [33, CH], F32, 3)
                tnp_ = RPool(s2, nc, "tnF", [128, 4], F32, 5)
                ap_ = RPool(s2, nc, "aF", [64, CH], F32, 6)
                tq = RPool(s2, nc, "tF", [64, CH], F32, 4)
                hp_ = RPool(s2, nc, "hF", [64, CH], F32, 4)
                hbp = RPool(s2, nc, "hbF", [64, CH], BF16, 4)
                ep = RPool(s2, nc, "eF", [128, 512], F32, 3)
                kp = RPool(s2, nc, "kF", [128, 512], BF16, 4)
                p1 = RPool(s2, nc, "p1F", [64, CH], F32, 3, psum=True)
                p3 = RPool(s2, nc, "p3F", [128, 512], F32, 4, psum=True)

                def sin_layer(ps, bps, bias_col, n_, outp=None):
                    a, ba = ap_.next()
                    pr.add("dve", TS(a[:, :n_], ps[:, :n_], fv[:, 0:1], fv[:, bias_col:bias_col + 1], ALU.mult, ALU.add), reads=[bps, bfv], writes=[ba])
                    t, bt = tq.next()
                    pr.add("dve", TS(t[:, :n_], a[:, :n_], PI, -2.0 * PI, ALU.is_gt, ALU.mult), reads=[ba], writes=[bt])
                    pr.add("dve", TT(a[:, :n_], a[:, :n_], t[:, :n_], ALU.add), reads=[ba, bt], writes=[ba])
                    t, bt = tq.next()
                    pr.add("dve", TS(t[:, :n_], a[:, :n_], -PI, 2.0 * PI, ALU.is_lt, ALU.mult), reads=[ba], writes=[bt])
                    pr.add("dve", TT(a[:, :n_], a[:, :n_], t[:, :n_], ALU.add), reads=[ba, bt], writes=[ba])
                    h, bh = (outp or hp_).next()
                    pr.add("act", ACTF(h[:, :n_], a[:, :n_], AF.Sin), reads=[ba], writes=[bh])
                    return h, bh
                chunks = [(r0, min(CH, 2 * T - r0)) for r0 in range(0, 2 * T, CH)]

                def st1(c):
                    r0, n_ = chunks[c]
                    ft, bft = ftp.next()
                    pr.dma("sp", ft[:, :n_], rows2d(featsT, 2 * T, 0, 33, r0, n_), reads=[INB], writes=[bft])
                    tn, btn = tnp_.next()
                    for j in range(n_ // 128):
                        pr.dma("sp", tn[:, j:j + 1], col1(tneg, r0 + 128 * j), reads=[INB], writes=[btn])
                    ps, bps = p1.next()
                    pr.add("pe", MM(ps[:, :n_], fw1[:], ft[:, :n_], True, True), reads=[bw, bft], writes=[bps])
                    h1, bh1 = sin_layer(ps, bps, 3, n_)
                    return (h1, bh1, tn, btn)

                def st2(c, s1_):
                    r0, n_ = chunks[c]
                    h1, bh1, tn, btn = s1_
                    ps, bps = p1.next()
                    pr.add("pe", MM(ps[:, :n_], fw2[:], h1[:, :n_], True, True), reads=[bw, bh1], writes=[bps])
                    h2, bh2 = sin_layer(ps, bps, 4, n_, hbp)
                    return (h2, bh2, tn, btn)

                def st3(c, s2_):
                    r0, n_ = chunks[c]
                    h2, bh2, tn, btn = s2_
                    for j in range(n_ // 128):
                        dirn = 0 if (r0 + 128 * j - T) >= 0 else 1
                        for n in range(2):
                            cols = n * 1024 + dirn * 512
                            ps3, bp3 = p3.next()
                            pr.add("pe", MM(ps3[:], h2[:, j * 128:(j + 1) * 128], fw3b[:, cols:cols + 512], True, True), reads=[bw3b, bh2], writes=[bp3])
                            E, bE = ep.next()
                            pr.add("act", ACTF(E[:], dB[:, cols:cols + 512], AF.Exp, scale=tn[:, j:j + 1]), reads=[bdB, btn], writes=[bE])
                            kt, bkt = kp.next()
                            if r0 + 128 * j == T:
                                pr.add("dve", TT(E[:], ps3[:], E[:], ALU.mult), reads=[bp3, bE], writes=[bE])
                                pr.add("dve", TT(E[0:1, :], E[0:1, :], hb[0:1, n, :], ALU.add), reads=[bE, bhb], writes=[bE])
                                pr.add("dve", CP(kt[:], E[:]), reads=[bE], writes=[bkt])
                            else:
                                pr.add("dve", TT(kt[:], ps3[:], E[:], ALU.mult), reads=[bp3, bE], writes=[bkt])
                            nm = f"KTIME{n}_{L}"
                            pr.dma("pool", rows2d(sc[nm], 512, r0 + 128 * j, 128, 0, 512), kt[:], reads=[bkt], writes=[db[nm]])
                NCK = len(chunks)
                r1, r2 = {}, {}
                for i in range(NCK + 2):
                    if i < NCK:
                        r1[i] = st1(i)
                    if 0 <= i - 1 < NCK:
                        r2[i - 1] = st2(i - 1, r1.pop(i - 1))
                    if 0 <= i - 2 < NCK:
                        st3(i - 2, r2.pop(i - 2))
                pr.flush()
            S1F, bS1F = sc[f"S1F_{L}"], db[f"S1F_{L}"]
            for n in range(2):
                KT, bKT = sc[f"KTIME{n}_{L}"], db[f"KTIME{n}_{L}"]
                KH, bKH = sc[f"KHAT{n}_{L}"], db[f"KHAT{n}_{L}"]
                with contextlib.ExitStack() as s2:
                    kch = _split(PP)
                    fa = []
                    for p, nm in enumerate(("FA_re", "FA_im")):
                        lst = []
                        for ci, (r0_, rn_) in enumerate(kch):
                            t_ = s2.enter_context(nc.sbuf_tensor(U(f"FA{p}{ci}"), [128, 128], BF16))
                            b_ = Buf(f"FA{p}{ci}")
                            pr.add("dve", MSET(t_[:], 0.0), writes=[b_])
                            pr.dma("sp", t_[0:rn_, 0:B], rows2d(W[f"{nm}_{L}"], B, r0_, rn_, 0, B), reads=[INB], writes=[b_])
                            lst.append((t_, b_, rn_))
                        fa.append(lst)
                    EB = 8
                    xin = [RPool(s2, nc, f"xinF{ci}", [128, EB, 512], BF16, 2) for ci in range(len(kch))]
                    for xp_ in xin:
                        for t_, b_ in xp_.tiles:
                            pr.add("dve", MSET(t_[:].rearrange("p a c -> p (a c)"), 0.0), writes=[b_])
                    sop = [RPool(s2, nc, f"soF{p_}", [128, EB, 512], BF16, 2) for p_ in range(2)]
                    pp = RPool(s2, nc, "ppF", [128, 512], F32, 8, psum=True)
                    for e0 in range(0, NG, EB):
                        ne = min(EB, NG - e0)
                        xs = []
                        for ci, (d0, dn) in enumerate(kch):
                            xt_, bx = xin[ci].next()
                            pr.dma("sp", xt_[0:dn, 0:ne, :], DV(KT, (128 * d0 + e0 + 1) * 512, [[128 * 512, dn], [512, ne], [1, 512]]), reads=[bKT], writes=[bx])
                            xs.append((xt_, bx, dn))
                        sos = [sop[0].next(), sop[1].next()]
                        for ee in range(ne):
                            for part in range(2):
                                so, bso = sos[part]
                                ps, bps = pp.next()
                                for ci, (xt_, bx, dn) in enumerate(xs):
                                    ft_, bft_, rn = fa[part][ci]
                                    pr.add("pe", MM(ps[:, :], ft_[:, :], xt_[:, ee, :], ci == 0, ci == len(xs) - 1), reads=[bft_, bx], writes=[bps])
                                if part == 0:
                                    pr.add("act", ACTF(so[0:B, ee, :], ps[0:B, :], AF.Copy), reads=[bps], writes=[bso])
                                else:
                                    pr.add("dve", CP(so[0:B, ee, :], ps[0:B, :]), reads=[bps], writes=[bso])
                        for part in range(2):
                            so, bso = sos[part]
                            pr.dma("act" if part == 0 else "pool", DV(S1F, (part * BP * NG + e0) * 512, [[NG * 512, BP], [512, ne], [1, 512]]), so[0:BP, 0:ne, :], reads=[bso], writes=[bS1F])
                    pr.flush()
                with contextlib.ExitStack() as s2:
                    FBK = 4
                    xin = [RPool(s2, nc, f"xinG{ci}", [128, 2, FBK, 512], BF16, 2) for ci in range(2)]
                    kop = [RPool(s2, nc, f"koG{p_}", [128, FBK, 512], BF16, 3) for p_ in range(2)]
                    pp = RPool(s2, nc, "ppG", [128, 512], F32, 8, psum=True)
                    for f0 in range(0, B, FBK):
                        nf = min(FBK, B - f0)
                        xs = []
                        for ci, (e0c, en) in enumerate(GCH):
                            xt_, bx = xin[ci].next()
                            for p_ in range(2):
                                pr.dma("sp", xt_[0:128, p_, 0:nf, :], DV(S1F, (p_ * BP * NG + f0 * NG + e0c) * 512, [[512, 128], [NG * 512, nf], [1, 512]]),
                                       reads=[bS1F], writes=[bx])
                            xs.append((xt_, bx, en))
                        for gc, (g0, gn) in enumerate(GCH):
                            kos = [kop[0].next(), kop[1].next()]
                            for ff in range(nf):
                                for part in range(2):
                                    ko, bko = kos[part]
                                    ps, bps = pp.next()
                                    terms = []
                                    for ci, (xt_, bx, en) in enumerate(xs):
                                        if part == 0:
                                            terms += [("FBc", ci, 0), ("FBs", ci, 1)]
                                        else:
                                            terms += [("FBc", ci, 1), ("FBsn", ci, 0)]
                                    for ti, (nm, ci, src) in enumerate(terms):
                                        tb, btb, rn = FB_[nm][ci]
                                        xt_, bx, en = xs[ci]
                                        pr.add("pe", MM(ps[0:gn, :], tb[0:en, g0:g0 + gn], xt_[0:en, src, ff, :], ti == 0, ti == len(terms) - 1),
                                               reads=[btb, bx], writes=[bps])
                                    if part == 0:
                                        pr.add("act", ACTF(ko[0:gn, ff, :], ps[0:gn, :], AF.Copy), reads=[bps], writes=[bko])
                                    else:
                                        pr.add("dve", CP(ko[0:gn, ff, :], ps[0:gn, :]), reads=[bps], writes=[bko])
                            for p_ in range(2):
                                ko, bko = kos[p_]
                                pr.dma("act" if p_ == 0 else "pool", DV(KH, (gc * 128 * 2 * B + p_ * B + f0) * 512, [[2 * B * 512, 128], [512, nf], [1, 512]]),
                                       ko[0:128, 0:nf, :], reads=[bko], writes=[bKH])
                    pr.flush()


def phase_C(k):
    nc, pr, W, sc, db = k.nc, k.pr, k.din, k.sc, k.db
    INB = k.din_buf
    Bm = k.FD
    S1, S2 = sc["S1"], sc["S2"]
    with contextlib.ExitStack() as st:
        def sb(name, shape, dt):
            return st.enter_context(nc.sbuf_tensor(U(name), shape, dt))
        Bt = {}
        for nm in ("Bc", "Bs", "Bsn"):
            t = sb(f"C{nm}", [128, NG], BF16)
            b = Buf(f"C{nm}")
            pr.dma("sp", t[:], W[nm][:, :], reads=[INB], writes=[b])
            Bt[nm] = (t, b)
        IB = {nm: load_chunked(pr, st, nc, f"C{nm}", W[nm], NG, 128, GCH, INB) for nm in ("IBc", "IBs", "IBsn", "IBcn")}
        JB3 = 8
        zc, bzc = sb("zeroC", [128, 1536], BF16), Buf("zeroC")
        pr.add("dve", MSET(zc[:], 0.0), writes=[bzc])
        Tt_ = k.Ttot
        for i_ in range(16):
            pr.dma("pool", rows2d(sc["UC"], 1536, Tt_ + 128 * i_, 128, 0, 1536), zc[:, :], reads=[bzc], writes=[db["UC"]])
            pr.dma("pool", rows2d(sc["Z1"], 512, Tt_ + 128 * i_, 128, 0, 512), zc[:, 0:512], reads=[bzc], writes=[db["Z1"]])
        bemin = min(u["Be"] for u in k.units)
        for p_ in range(2):
            for f_ in range(bemin, Bm):
                pr.dma("pool", rows2d(S2, 512, (p_ * Bm + f_) * 128, 128, 0, 512), zc[:, 0:512], reads=[bzc], writes=[db["S2"]])
        At = {}
        for ui, u in enumerate(k.units):
            Be = u["Be"]
            for nm in ("A_re", "A_im", "IA_re", "IA_im"):
                t = sb(f"C{nm}u{ui}", [128, 128], BF16)
                b = Buf(f"C{nm}u{ui}")
                pr.dma("sp", t[:, :], W[f"u{ui}_{nm}"][:, :], reads=[INB], writes=[b])
                At[(nm, ui)] = (t, b)
        pr.flush()
        for ui, u in enumerate(k.units):
            L, B, nb, toff, Be = u["L"], u["B"], u["nb"], u["toff"], u["Be"]
            BP = prow(Be)
            for n in range(2):
                if n == 0:
                    zsrc, bz, zrs, zco = sc["UC"], db["UC"], 1536, 1024
                    zdst, bzd = sc["Z1"], db["Z1"]
                else:
                    zsrc, bz, zrs, zco = sc["Z1"], db["Z1"], 512, 0
                    zdst, bzd = sc["Z2"], db["Z2"]
                KH, bKH = sc[f"KHAT{n}_{L}"], db[f"KHAT{n}_{L}"]
                with contextlib.ExitStack() as s2:
                    JB = 16
                    zbp = RPool(s2, nc, "zbC", [128, JB, 512], BF16, 3)
                    for ti_, (t_, b_) in enumerate(zbp.tiles):
                        pr.add("dve", MSET(t_[:].rearrange("p a c -> p (a c)"), 0.0), writes=[b_])
                    sop = [RPool(s2, nc, f"soC{p_}", [128, JB, 512], BF16, 2) for p_ in range(2)]
                    pp = RPool(s2, nc, "ppC", [128, 512], F32, 8, psum=True)
                    for j0 in range(0, 128, JB):
                        zb, bzb = zbp.next()
                        pr.dma("sp", zb[0:BP, :, :], DV(zsrc, (toff + j0) * zrs + zco, [[128 * zrs, BP], [zrs, JB], [1, 512]]), reads=[bz], writes=[bzb])
                        sos = [sop[0].next(), sop[1].next()]
                        for jj in range(JB):
                            for part, nm in enumerate(("A_re", "A_im")):
                                at, bat = At[(nm, ui)]
                                so, bso = sos[part]
                                ps, bps = pp.next()
                                pr.add("pe", MM(ps[:, :], at[:, :], zb[:, jj, :], True, True), reads=[bat, bzb], writes=[bps])
                                if part == 0:
                                    pr.add("act", ACTF(so[0:Be, jj, :], ps[0:Be, :], AF.Copy), reads=[bps], writes=[bso])
                                else:
                                    pr.add("dve", CP(so[0:Be, jj, :], ps[0:Be, :]), reads=[bps], writes=[bso])
                        for part in range(2):
                            so, bso = sos[part]
                            pr.dma("act" if part == 0 else "pool", DV(S1, (part * Bm * 128 + j0) * 512, [[128 * 512, BP], [512, JB], [1, 512]]), so[0:BP, :, :], reads=[bso], writes=[db["S1"]])
                    pr.flush()
                with contextlib.ExitStack() as s2:
                    FBK = 4
                    xin = RPool(s2, nc, "xinC2", [128, 2, FBK, 512], BF16, 3)
                    khp = [RPool(s2, nc, f"khC2{gc}", [128, 2, FBK, 512], BF16, 3) for gc in range(2)]
                    sop = RPool(s2, nc, "soC2", [128, 2, FBK, 512], BF16, 2)
                    abp = RPool(s2, nc, "abC2", [128, 2, 512], F32, 5)
                    t12p = RPool(s2, nc, "t13C2", [128, 2, 512], BF16, 5)
                    t34p = RPool(s2, nc, "t24C2", [128, 2, 512], BF16, 5)
                    yrp = RPool(s2, nc, "yrC2", [128, 512], BF16, 5)
                    yip = RPool(s2, nc, "yiC2", [128, 512], BF16, 5)
                    pab = RPool(s2, nc, "pabC2", [128, 512], F32, 4, psum=True)
                    por = RPool(s2, nc, "porC2", [128, 512], F32, 4, psum=True)
                    for sbi in range(nb):
                        for f0 in range(0, B, FBK):
                            nf = min(FBK, B - f0)
                            fe0 = sbi * B + f0
                            xi, bxi = xin.next()
                            for p_ in range(2):
                                pr.dma("sp", xi[:, p_, 0:nf, :], DV(S1, (p_ * Bm * 128 + fe0 * 128) * 512, [[512, 128], [128 * 512, nf], [1, 512]]),
                                       reads=[db["S1"]], writes=[bxi])
                            khs = []
                            for gc, (g0, gn) in enumerate(GCH):
                                kh, bkh = khp[gc].next()
                                for p_ in range(2):
                                    pr.dma("sp", kh[0:128, p_, 0:nf, :], DV(KH, (gc * 128 * 2 * B + p_ * B + f0) * 512, [[2 * B * 512, 128], [512, nf], [1, 512]]),
                                           reads=[bKH], writes=[bkh])
                                khs.append((kh, bkh))
                            so, bso = sop.next()
                            items = [(ff, gc) for ff in range(nf) for gc in range(2)]

                            def fwd(item):
                                ff, gc = item
                                g0, gn = GCH[gc]
                                pa, bpa = pab.next()
                                pb, bpb = pab.next()
                                pr.add("pe", MM(pa[0:gn, :], Bt["Bc"][0][:, g0:g0 + gn], xi[:, 0, ff, :], True, False), reads=[Bt["Bc"][1], bxi], writes=[bpa])
                                pr.add("pe", MM(pa[0:gn, :], Bt["Bs"][0][:, g0:g0 + gn], xi[:, 1, ff, :], False, True), reads=[Bt["Bs"][1], bxi], writes=[bpa])
                                pr.add("pe", MM(pb[0:gn, :], Bt["Bc"][0][:, g0:g0 + gn], xi[:, 1, ff, :], True, False), reads=[Bt["Bc"][1], bxi], writes=[bpb])
                                pr.add("pe", MM(pb[0:gn, :], Bt["Bsn"][0][:, g0:g0 + gn], xi[:, 0, ff, :], False, True), reads=[Bt["Bsn"][1], bxi], writes=[bpb])
                                ab, bab = abp.next()
                                pr.add("act", ACTF(ab[0:gn, 0, :], pa[0:gn, :], AF.Copy), reads=[bpa], writes=[bab])
                                pr.add("act", ACTF(ab[0:gn, 1, :], pb[0:gn, :], AF.Copy), reads=[bpb], writes=[bab])
                                kh, bkh = khs[gc]
                                t13, bt13 = t12p.next()
                                t24, bt24 = t34p.next()
                                khv = kh[0:gn, :, ff, :]
                                pr.add("dve", TT(t13[0:gn, :, :], ab[0:gn, 0:1, :].broadcast_to([gn, 2, 512]), khv, ALU.mult), reads=[bab, bkh], writes=[bt13])
                                pr.add("dve", TT(t24[0:gn, :, :], ab[0:gn, 1:2, :].broadcast_to([gn, 2, 512]), khv, ALU.mult), reads=[bab, bkh], writes=[bt24])
                                return (t13, bt13, t24, bt24)
                            state = {}

                            def inv(item, fw):
                                ff, gc = item
                                g0, gn = GCH[gc]
                                t13, bt13, t24, bt24 = fw
                                if gc == 0:
                                    state["re"] = por.next()
                                    state["im"] = por.next()
                                pre, bre = state["re"]
                                pim, bim = state["im"]
                                ic, ibc, _ = IB["IBc"][gc]
                                icn, ibcn, _ = IB["IBcn"][gc]
                                is_, ibs, _ = IB["IBs"][gc]
                                isn, ibsn, _ = IB["IBsn"][gc]
                                ap_, aq_, bp_, bq_ = t13[0:gn, 0, :], t13[0:gn, 1, :], t24[0:gn, 0, :], t24[0:gn, 1, :]
                                pr.add("pe", MM(pre[:], ic[0:gn, :], ap_, gc == 0, False), reads=[ibc, bt13], writes=[bre])
                                pr.add("pe", MM(pre[:], icn[0:gn, :], bq_, False, False), reads=[ibcn, bt24], writes=[bre])
                                pr.add("pe", MM(pre[:], isn[0:gn, :], aq_, False, False), reads=[ibsn, bt13], writes=[bre])
                                pr.add("pe", MM(pre[:], isn[0:gn, :], bp_, False, gc == 1), reads=[ibsn, bt24], writes=[bre])
                                pr.add("pe", MM(pim[:], is_[0:gn, :], ap_, gc == 0, False), reads=[ibs, bt13], writes=[bim])
                                pr.add("pe", MM(pim[:], isn[0:gn, :], bq_, False, False), reads=[ibsn, bt24], writes=[bim])
                                pr.add("pe", MM(pim[:], ic[0:gn, :], aq_, False, False), reads=[ibc, bt13], writes=[bim])
                                pr.add("pe", MM(pim[:], ic[0:gn, :], bp_, False, gc == 1), reads=[ibc, bt24], writes=[bim])
                                if gc == 1:
                                    pr.add("act", ACTF(so[:, 0, ff, :], pre[:], AF.Copy), reads=[bre], writes=[bso])
                                    pr.add("act", ACTF(so[:, 1, ff, :], pim[:], AF.Copy), reads=[bim], writes=[bso])
                            LA = 2
                            fq = [fwd(items[i_]) for i_ in range(min(LA, len(items)))]
                            for ii, item in enumerate(items):
                                if ii + LA < len(items):
                                    fq.append(fwd(items[ii + LA]))
                                inv(item, fq.pop(0))
                            for p_ in range(2):
                                pr.dma("act", DV(S2, (p_ * Bm * 128 + fe0 * 128) * 512, [[512, 128], [128 * 512, nf], [1, 512]]), so[:, p_, 0:nf, :],
                                       reads=[bso], writes=[db["S2"]])
                    pr.flush()
                with contextlib.ExitStack() as s2:
                    JB = JB3
                    xin = RPool(s2, nc, "xinC3", [128, 2, JB, 512], BF16, 2)
                    for ti_, (t_, b_) in enumerate(xin.tiles):
                        pr.add("dve", MSET(t_[:].rearrange("p a b c -> p (a b c)"), 0.0), writes=[b_])
                    xp3 = RPool(s2, nc, "xC3", [128, JB, 512], BF16, 3)
                    op3 = RPool(s2, nc, "oC3", [128, JB, 512], BF16, 3)
                    for ti_, (t_, b_) in enumerate(op3.tiles):
                        pr.add("dve", MSET(t_[:].rearrange("p a c -> p (a c)"), 0.0), writes=[b_])
                    pp = RPool(s2, nc, "ppC3", [128, 512], F32, 8, psum=True)
                    are, bare = At[("IA_re", ui)]
                    aim, baim = At[("IA_im", ui)]
                    for j0 in range(0, 128, JB):
                        xi, bxi = xin.next()
                        for p_ in range(2):
                            pr.dma("sp", xi[0:BP, p_, :, :], DV(S2, (p_ * Bm * 128 + j0) * 512, [[128 * 512, BP], [512, JB], [1, 512]]), reads=[db["S2"]], writes=[bxi])
                        xt_, bxt = xp3.next()
                        pr.dma("sp", xt_[0:BP, :, :], DV(sc["UC"], (toff + j0) * 1536 + n * 512, [[128 * 1536, BP], [1536, JB], [1, 512]]), reads=[db["UC"]], writes=[bxt])
                        o, bo = op3.next()
                        for jj in range(JB):
                            ps, bps = pp.next()
                            pr.add("pe", MM(ps[:, :], are[:, :], xi[:, 0, jj, :], True, False), reads=[bare, bxi], writes=[bps])
                            pr.add("pe", MM(ps[:, :], aim[:, :], xi[:, 1, jj, :], False, True), reads=[baim, bxi], writes=[bps])
                            pr.add("dve", TT(o[0:Be, jj, :], xt_[0:Be, jj, :], ps[0:Be, :], ALU.mult), reads=[bxt, bps], writes=[bo])
                        pr.dma("pool", DV(zdst, (toff + j0) * 512, [[128 * 512, BP], [512, JB], [1, 512]]), o[0:BP, :, :], reads=[bo], writes=[bzd])
                    pr.flush()


NCORES = 8
_CACHE = {}


def kernel(**inputs):
    xp = np.ascontiguousarray(np.asarray(inputs["x_prompt"], dtype=np.float32))
    xs = np.ascontiguousarray(np.asarray(inputs["x_sample"], dtype=np.float32))
    nbp, Lxp = xp.shape[0], xp.shape[1]
    nbs, Lxs = xs.shape[0], xs.shape[1]
    assert nbp % NCORES == 0 and nbs % NCORES == 0
    npp, nps = nbp // NCORES, nbs // NCORES
    key = (npp, Lxp, nps, Lxs)
    if key not in _CACHE:
        specs = [("p", i, Lxp) for i in range(npp)] + [("s", i, Lxs) for i in range(nps)]
        _CACHE[key] = build_program(specs, npp, Lxp, nps, Lxs)
    nc, k = _CACHE[key]
    shared = {}
    for nme in ("meta_tokens", "pre_mix_g", "w_in", "q_norm_g", "w_uq", "kv_norm_g", "w_ukv", "conv_w", "f_w1", "f_b1",
                "f_freq", "f_w2", "f_b2", "f_w3", "f_decay", "hy_bias", "attn_out_g", "hy_out_g", "w_o", "post_mix_g",
                "pre_mlp_g", "w_ff1", "w_ff2", "post_mlp_g"):
        shared[nme] = np.ascontiguousarray(np.asarray(inputs[nme], dtype=np.float32))
    for nme, arr in k.host.items():
        shared[nme] = arr
    in_maps = []
    for c in range(NCORES):
        m = dict(shared)
        m["xp"] = xp[c * npp:(c + 1) * npp]
        m["xs"] = xs[c * nps:(c + 1) * nps]
        in_maps.append(m)
    res = run_bass_kernel_spmd(nc, in_maps, core_ids=list(range(NCORES)))
    yp = np.concatenate([np.asarray(r["yp"], dtype=np.float32) for r in res.results], axis=0)
    ys = np.concatenate([np.asarray(r["ys"], dtype=np.float32) for r in res.results], axis=0)
    return (yp, ys)
```

```python
import contextlib
import numpy as np
import ml_dtypes
import concourse.bass as bass
import concourse.mybir as mybir
from concourse.bass_utils import run_bass_kernel_spmd

F32 = mybir.dt.float32
BF16 = mybir.dt.bfloat16
AF = mybir.ActivationFunctionType
ALU = mybir.AluOpType
AX = mybir.AxisListType
NPBF = ml_dtypes.bfloat16


class Buf:
    __slots__ = ("name", "multi", "w", "r")

    def __init__(self, name, multi=False):
        self.name = name
        self.multi = multi
        self.w = {}
        self.r = {}


class Op:
    __slots__ = ("eng", "fn", "deps", "dma", "chan", "sem", "val", "signal", "selfwait")

    def __init__(self, eng, fn, dma, chan):
        self.eng = eng
        self.fn = fn
        self.dma = dma
        self.chan = chan
        self.deps = []
        self.sem = None
        self.val = 0
        self.signal = dma
        self.selfwait = 0


class Prog:
    ENGS = ("pe", "act", "dve", "pool", "sp")
    NROT = {"sp": 14, "act": 10, "pool": 12}

    def __init__(self, nc):
        self.nc = nc
        self.ops = []
        self.esem = {e: nc.alloc_semaphore(name=f"s_{e}") for e in ("pe", "act", "dve", "pool")}
        self.ecnt = {e: 0 for e in self.esem}
        self.dsem = {}
        self.dval = {}
        for q, n in self.NROT.items():
            for k in range(n):
                s = nc.alloc_semaphore(name=f"d_{q}{k}")
                self.dsem[(q, k)] = s
                self.dval[(q, k)] = 0
        self.dcnt = {q: 0 for q in self.NROT}
        self.seen = {e: {} for e in self.ENGS}

    def add(self, eng, fn, reads=(), writes=(), dma=False):
        if dma:
            k = self.dcnt[eng] % self.NROT[eng]
            self.dcnt[eng] += 1
            chan = (eng, k)
        else:
            chan = eng
        op = Op(eng, fn, dma, chan)
        deps = {}

        def need_raw(p):
            return not (p.eng == "pe" and eng == "pe" and not dma and not p.dma)

        def need_w(p):
            return p.dma or dma or p.eng != eng

        for b in reads:
            for p in b.w.values():
                if need_raw(p):
                    deps[id(p)] = p
        for b in writes:
            for p in b.w.values():
                if need_w(p):
                    deps[id(p)] = p
            for p in b.r.values():
                if need_w(p) and p is not op:
                    deps[id(p)] = p
        for b in reads:
            b.r[chan] = op
        for b in writes:
            if b.multi:
                b.w[chan] = op
            else:
                b.w = {chan: op}
                b.r = {}
        op.deps = list(deps.values())
        for p in op.deps:
            p.signal = True
        self.ops.append(op)
        return op

    def dma(self, q, out, in_, reads=(), writes=()):
        return self.add(q, lambda e: e.dma_start(out=out, in_=in_), reads=reads, writes=writes, dma=True)

    def _eng(self, e):
        nc = self.nc
        return {"pe": nc.tensor, "act": nc.scalar, "dve": nc.vector, "pool": nc.gpsimd, "sp": nc.sync}[e]

    def flush(self, final=False):
        nc = self.nc
        ops = self.ops
        self.ops = []
        last = {}
        for op in ops:
            if not op.dma:
                last[op.eng] = op
        for op in last.values():
            op.signal = True
        for op in ops:
            if op.dma:
                op.sem = self.dsem[op.chan]
                op.selfwait = self.dval[op.chan]
                self.dval[op.chan] += 16
                op.val = self.dval[op.chan]
            elif op.signal:
                op.sem = self.esem[op.eng]
                self.ecnt[op.eng] += 1
                op.val = self.ecnt[op.eng]
        streams = {e: [op for op in ops if op.eng == e] for e in self.ENGS}
        bar = [(self.esem[e], self.ecnt[e], e) for e in self.esem] + \
              [(self.dsem[c], self.dval[c], c) for c in self.dsem]

        def run(ename, eng):
            seen = self.seen[ename]
            for op in streams[ename]:
                for p in op.deps:
                    if seen.get(p.chan, 0) < p.val:
                        eng.wait_ge(p.sem, p.val)
                        seen[p.chan] = p.val
                if op.dma and op.selfwait > 0 and seen.get(op.chan, 0) < op.selfwait:
                    eng.wait_ge(op.sem, op.selfwait)
                    seen[op.chan] = op.selfwait
                inst = op.fn(eng)
                if op.signal:
                    inst.then_inc(op.sem, 16 if op.dma else 1)
            for sem, val, chan in bar:
                if val > 0 and seen.get(chan, 0) < val:
                    eng.wait_ge(sem, val)
                    seen[chan] = val

        with nc.Block() as block:
            @block.tensor
            def _(e):
                run("pe", e)

            @block.scalar
            def _(e):
                run("act", e)

            @block.vector
            def _(e):
                run("dve", e)

            @block.gpsimd
            def _(e):
                run("pool", e)

            @block.sync
            def _(e):
                run("sp", e)


D = 1024
NMETA = 16
QL = 256
KVL = 128
ROPE = 64
NH = 4
HC = 512
INC = 1984
DFF = 4096
EPS = 1e-6
NG = 255
SCALE = float((128 + 64) ** -0.5)
PI = float(np.pi)


def _split(n, m=128):
    k = -(-n // m)
    base, rem = divmod(n, k)
    out, s = [], 0
    for i in range(k):
        c = base + (1 if i < rem else 0)
        out.append((s, c))
        s += c
    return out


def prow(n):
    m = n
    while m <= 128:
        if any(m % d == 0 for d in (16, 15, 14, 13, 12, 11)):
            return m
        m += 1
    return n


def _bf(a):
    return np.ascontiguousarray(np.asarray(a, dtype=np.float64).astype(np.float32).astype(NPBF))


def seq_tables(L):
    B = (L - NMETA) // 128 + 1
    PP = 2 * B - 1
    T = 128 * B
    t = {}
    inv = 1.0 / (10000.0 ** (np.arange(0, ROPE, 2, dtype=np.float64) / ROPE))
    ang = np.arange(L, dtype=np.float64)[:, None] * inv[None, :]
    ang = np.concatenate([ang, ang], -1)
    t["cosT"] = np.ascontiguousarray(np.cos(ang).T.astype(np.float32))
    t["sinT"] = np.ascontiguousarray(np.sin(ang).T.astype(np.float32))
    J = np.arange(B, dtype=np.float64)
    f = np.arange(B, dtype=np.float64)
    th = 2 * np.pi * np.outer(J, f) / PP
    t["A_re"] = _bf(np.cos(th))
    t["A_im"] = _bf(-np.sin(th))
    wf = np.where(f == 0, 1.0, 2.0)[:, None]
    thi = 2 * np.pi * np.outer(f, J) / PP
    t["IA_re"] = _bf(wf * np.cos(thi) / PP)
    t["IA_im"] = _bf(-wf * np.sin(thi) / PP)
    d = np.arange(-(B - 1), B, dtype=np.float64)
    thd = 2 * np.pi * np.outer(d, f) / PP
    t["FA_re"] = _bf(np.cos(thd))
    t["FA_im"] = _bf(-np.sin(thd))
    r = np.arange(2 * T, dtype=np.float64)
    tt = np.abs(r - T)
    tv = (tt / (L - 1)).astype(np.float32).astype(np.float64)
    w = 2.0 * np.pi * tt / L
    fb = np.linspace(1e-4, 15.0, 16)
    feats = np.concatenate([tv[:, None], np.cos(w[:, None] * fb[None]), -np.sin(w[:, None] * fb[None])], -1)
    t["featsT"] = np.ascontiguousarray(feats.T.astype(np.float32))
    t["tneg"] = np.ascontiguousarray((-tv).astype(np.float32)[:, None])
    return t


def common_tables():
    t = {}
    j = np.arange(128, dtype=np.float64)
    g = np.arange(NG, dtype=np.float64)
    th = 2 * np.pi * np.outer(j, g) / NG
    t["Bc"], t["Bs"], t["Bsn"] = _bf(np.cos(th)), _bf(np.sin(th)), _bf(-np.sin(th))
    e = np.arange(-127, 128, dtype=np.float64)
    the = 2 * np.pi * np.outer(e, g) / NG
    t["FBc"], t["FBs"], t["FBsn"] = _bf(np.cos(the)), _bf(np.sin(the)), _bf(-np.sin(the))
    thi = 2 * np.pi * np.outer(g, j) / NG
    t["IBc"], t["IBs"], t["IBsn"] = _bf(np.cos(thi) / NG), _bf(np.sin(thi) / NG), _bf(-np.sin(thi) / NG)
    t["IBcn"] = _bf(-np.cos(thi) / NG)
    t["ident"] = np.eye(128).astype(NPBF)
    t["ones"] = np.ones((128, 128)).astype(NPBF)
    t["zeros"] = np.zeros((128, 1536), NPBF)
    return t


_uid = [0]


def U(name):
    _uid[0] += 1
    return f"{name}_{_uid[0]}"


class RPool:
    def __init__(self, stack, nc, name, shape, dtype, n, psum=False):
        self.tiles = []
        for i in range(n):
            if psum:
                t = stack.enter_context(nc.psum_tensor(U(f"{name}{i}"), shape, dtype))
            else:
                t = stack.enter_context(nc.sbuf_tensor(U(f"{name}{i}"), shape, dtype))
            self.tiles.append((t, Buf(f"{name}{i}")))
        self.i = 0

    def next(self):
        t = self.tiles[self.i % len(self.tiles)]
        self.i += 1
        return t


def MM(ps, lhsT, rhs, start, stop):
    return lambda e: e.matmul(ps, lhsT=lhsT, rhs=rhs, start=start, stop=stop)


def TR(out, in_, ident):
    return lambda e: e.transpose(out=out, in_=in_, identity=ident)


def ACTF(out, in_, func, **kw):
    return lambda e: e.activation(out=out, in_=in_, func=func, **kw)


def TT(out, a, b, op):
    return lambda e: e.tensor_tensor(out=out, in0=a, in1=b, op=op)


def TS(out, in0, s1, s2=None, op0=ALU.mult, op1=None):
    if op1 is None:
        return lambda e: e.tensor_scalar(out=out, in0=in0, scalar1=s1, scalar2=None, op0=op0)
    return lambda e: e.tensor_scalar(out=out, in0=in0, scalar1=s1, scalar2=s2, op0=op0, op1=op1)


def STT(out, in0, scalar, in1, op0, op1):
    return lambda e: e.scalar_tensor_tensor(out=out, in0=in0, scalar=scalar, in1=in1, op0=op0, op1=op1)


def CP(out, in_):
    return lambda e: e.tensor_copy(out=out, in_=in_)


def MSET(ap, v):
    return lambda e: e.memset(ap, v)


def RSUM(out, in_):
    return lambda e: e.reduce_sum(out=out, in_=in_, axis=AX.X)


def RECIP(out, in_):
    return lambda e: e.reciprocal(out=out, in_=in_)


class SeqInfo:
    def __init__(self, kind, idx, Lx, toff):
        self.kind, self.idx, self.Lx = kind, idx, Lx
        self.L = Lx + NMETA
        self.n = Lx // 128
        self.B = self.n + 1
        self.T = 128 * self.B
        self.toff = toff
        self.groups = [(0, NMETA, [(0, NMETA, None)])]
        i = 0
        while i < self.n:
            k = min(4, self.n - i)
            self.groups.append((NMETA + 128 * i, 128 * k,
                                [(NMETA + 128 * (i + j), 128, 128 * (i + j)) for j in range(k)]))
            i += k


def DV(ap, offset, dims):
    return bass.AP(ap.tensor, int(offset), [list(map(int, d)) for d in dims])


def rows2d(ap, ncols, r0, nr, c0, ncl):
    return DV(ap, r0 * ncols + c0, [[ncols, nr], [1, ncl]])


class K:
    pass


def build_program(seq_specs, n_xp, Lxp, n_xs, Lxs, dbg=False, stop_after=None):
    nc = bass.Bass("TRN2", target_bir_lowering=False)
    pr = Prog(nc)
    k = K()
    k.nc, k.pr, k.dbg = nc, pr, dbg

    def din(name, shape, dt=F32):
        return nc.dram_tensor(name, list(shape), dt, kind="ExternalInput").ap()

    def dscr(name, shape, dt):
        return nc.dram_tensor(name, list(shape), dt, kind=("ExternalOutput" if dbg else "Internal")).ap()

    k.din = {}
    k.din["xp"] = din("xp", [max(n_xp, 1), Lxp, D])
    k.din["xs"] = din("xs", [max(n_xs, 1), Lxs, D])
    k.yp = nc.dram_tensor("yp", [max(n_xp, 1), Lxp, D], F32, kind="ExternalOutput").ap()
    k.ys = nc.dram_tensor("ys", [max(n_xs, 1), Lxs, D], F32, kind="ExternalOutput").ap()
    wshapes = dict(meta_tokens=[16, D], pre_mix_g=[1, D], w_in=[1, D, INC], q_norm_g=[1, QL], w_uq=[1, QL, 768],
                   kv_norm_g=[1, KVL], w_ukv=[1, KVL, 1024], conv_w=[1, 3, 1536], f_w1=[1, 33, 64], f_b1=[1, 64],
                   f_freq=[1, 64], f_w2=[1, 64, 64], f_b2=[1, 64], f_w3=[1, 64, 2048], f_decay=[1, 2048],
                   hy_bias=[1, 2, 512], attn_out_g=[1, 512], hy_out_g=[1, 512], w_o=[1, D, D], post_mix_g=[1, D],
                   pre_mlp_g=[1, D], w_ff1=[1, D, DFF], w_ff2=[1, DFF, D], post_mlp_g=[1, D])
    for nme, shp in wshapes.items():
        k.din[nme] = din(nme, shp)
    ct = common_tables()
    k.host = dict(ct)
    for nme, arr in ct.items():
        k.din[nme] = din(nme, arr.shape, BF16 if arr.dtype == NPBF else F32)
    seqs, toff = [], 0
    for kind, idx, Lx in seq_specs:
        s = SeqInfo(kind, idx, Lx, toff)
        toff += s.T
        seqs.append(s)
    k.seqs, k.Ttot = seqs, toff
    k.Ls = sorted({s.L for s in seqs})
    k.tab = {}
    for L in k.Ls:
        tb = seq_tables(L)
        for nme, arr in tb.items():
            key = f"{nme}_{L}"
            k.host[key] = arr
            k.din[key] = din(key, arr.shape, BF16 if arr.dtype == NPBF else F32)
    Tt = k.Ttot
    Bm = max(s.B for s in seqs)
    units = []
    for s_ in seqs:
        if units and units[-1]["L"] == s_.L and (units[-1]["nb"] + 1) * s_.B <= 128:
            units[-1]["nb"] += 1
        else:
            units.append(dict(L=s_.L, B=s_.B, nb=1, toff=s_.toff))
    for ui, u in enumerate(units):
        u["Be"] = u["B"] * u["nb"]
        tb = seq_tables(u["L"])
        for nm in ("A_re", "A_im", "IA_re", "IA_im"):
            arr0 = np.kron(np.eye(u["nb"], dtype=np.float32), tb[nm].astype(np.float32))
            arr = np.zeros((128, 128), np.float32)
            arr[:arr0.shape[0], :arr0.shape[1]] = arr0
            arr = arr.astype(NPBF)
            key = f"u{ui}_{nm}"
            k.host[key] = np.ascontiguousarray(arr)
            k.din[key] = din(key, arr.shape, BF16)
    k.units = units
    sc = {}
    sc["QT"] = dscr("QT", [NH, 192, Tt], BF16)
    sc["KN"] = dscr("KN", [NH, 128, Tt], BF16)
    sc["KR"] = dscr("KR", [64, Tt], BF16)
    sc["V"] = dscr("V", [Tt, 512], BF16)
    PADR = 128 * 16
    k.FD = max(prow(u["Be"]) for u in units)
    sc["UC"] = dscr("UC", [Tt + PADR, 1536], BF16)
    sc["UR"] = dscr("UR", [Tt + PADR, 1536], BF16)
    sc["Z1"] = dscr("Z1", [Tt + PADR, 512], BF16)
    sc["Z2"] = dscr("Z2", [Tt + PADR, 512], BF16)
    sc["Y"] = dscr("Y", [Tt + PADR, 512], F32)
    sc["S1"] = dscr("S1", [2, k.FD, 128, 512], BF16)
    sc["S2"] = dscr("S2", [2, k.FD, 128, 512], BF16)
    sc["AN"] = dscr("AN", [512, Tt], BF16)
    sc["H1"] = dscr("H1", [Tt, D], F32)
    for L in k.Ls:
        B = (L - NMETA) // 128 + 1
        for n in range(2):
            sc[f"KTIME{n}_{L}"] = dscr(f"KTIME{n}_{L}", [256 * B, 512], BF16)
            sc[f"KHAT{n}_{L}"] = dscr(f"KHAT{n}_{L}", [2, 128, 2, B, 512], BF16)
        sc[f"S1F_{L}"] = dscr(f"S1F_{L}", [2 * prow(B) * NG + 16, 512], BF16)
    k.sc = sc
    k.db = {nme: Buf(nme, multi=True) for nme in sc}
    k.dout = Buf("yout", multi=True)
    k.din_buf = Buf("inputs", multi=True)

    phases = [phase_F, phase_A, phase_B, phase_C, phase_D1, phase_D2]
    for ph in phases:
        ph(k)
        if stop_after == ph.__name__:
            break
    return nc, k


def xrows(k, s, xrow, nr):
    ap = k.din["xp"] if s.kind == "p" else k.din["xs"]
    Lx = s.Lx
    return DV(ap, (s.idx * Lx + xrow) * D, [[D, nr], [1, D]])


def yrows(k, s, xrow, nr):
    ap = k.yp if s.kind == "p" else k.ys
    return DV(ap, (s.idx * s.Lx + xrow) * D, [[D, nr], [1, D]])


def col1(ap, off, n=128):
    return DV(ap, off, [[1, n], [1, 1]])


def rstd_act(pr, small, junk_ap, junk_buf, src_ap, src_bufs, n_feat):
    st, sb_ = small.next()
    pr.add("act", lambda e: e.activation(out=junk_ap, in_=src_ap, func=AF.Square, accum_out=st[:, 0:1]), reads=list(src_bufs), writes=[junk_buf, sb_])
    pr.add("act", ACTF(st[:, 1:2], st[:, 0:1], AF.Ln, scale=1.0 / n_feat, bias=EPS), reads=[sb_], writes=[sb_])
    pr.add("act", ACTF(st[:, 2:3], st[:, 1:2], AF.Exp, scale=-0.5), reads=[sb_], writes=[sb_])
    return st[:, 2:3], sb_


def rstd_chain(pr, small, src_sq_ap, src_buf, n_feat, rows=128):
    st, sb_ = small.next()
    pr.add("dve", RSUM(st[0:rows, 0:1], src_sq_ap), reads=[src_buf], writes=[sb_])
    pr.add("act", ACTF(st[0:rows, 1:2], st[0:rows, 0:1], AF.Ln, scale=1.0 / n_feat, bias=EPS), reads=[sb_], writes=[sb_])
    pr.add("act", ACTF(st[0:rows, 2:3], st[0:rows, 1:2], AF.Exp, scale=-0.5), reads=[sb_], writes=[sb_])
    return st[0:rows, 2:3], sb_


def phase_A(k):
    nc, pr, W, sc, db = k.nc, k.pr, k.din, k.sc, k.db
    INB = k.din_buf
    Tt = k.Ttot
    with contextlib.ExitStack() as st:
        def sb(name, shape, dt):
            return st.enter_context(nc.sbuf_tensor(U(name), shape, dt))
        WinA, bWinA = sb("WinA", [128, 8, 512], BF16), Buf("WinA")
        Wu, bWu = sb("Wu", [128, 8, 1536], BF16), Buf("Wu")
        cwb, bcwb = sb("cwb", [128, 3, 1536], BF16), Buf("cwb")
        Wuq, bWuq = sb("Wuq", [128, 2, 768], BF16), Buf("Wuq")
        WuqR, bWuqR = sb("WuqR", [128, 2, 256], BF16), Buf("WuqR")
        Wukv, bWukv = sb("Wukv", [128, 1024], BF16), Buf("Wukv")
        WV, bWV = sb("WV", [128, 512], BF16), Buf("WV")
        gv, bgv = sb("gv", [128, 12], F32), Buf("gv")
        ident, bid = sb("identA", [128, 128], BF16), Buf("ident")
        ones, bon = sb("onesA", [128, 128], BF16), Buf("ones")
        zt, bzt = sb("zerosA", [128, 1536], BF16), Buf("zeros")
        pr.dma("sp", ident[:], W["ident"][:, :], reads=[INB], writes=[bid])
        pr.dma("sp", ones[:], W["ones"][:, :], reads=[INB], writes=[bon])
        pr.dma("sp", zt[:], W["zeros"][:, :], reads=[INB], writes=[bzt])
        for kc in range(8):
            pr.dma("sp", gv[:, kc:kc + 1], col1(W["pre_mix_g"], kc * 128), reads=[INB], writes=[bgv])
        for kc in range(2):
            pr.dma("sp", gv[:, 8 + kc:9 + kc], col1(W["q_norm_g"], kc * 128), reads=[INB], writes=[bgv])
        pr.dma("sp", gv[:, 10:11], col1(W["kv_norm_g"], 0), reads=[INB], writes=[bgv])
        with contextlib.ExitStack() as st2:
            cwB = st2.enter_context(nc.sbuf_tensor("cwB", [128, 3, 1536], F32))
            bcw = Buf("cwB")
            stg = RPool(st2, nc, "stgA", [128, INC], F32, 2)
            for i in range(3):
                pr.dma("sp", cwB[:, i, :], DV(W["conv_w"], i * 1536, [[0, 128], [1, 1536]]), reads=[INB], writes=[bcw])
            for kc in range(8):
                sg, bsg = stg.next()
                g1 = gv[:, kc:kc + 1]
                pr.dma("sp", sg[:], rows2d(W["w_in"], INC, kc * 128, 128, 0, INC), reads=[INB], writes=[bsg])
                pr.add("dve", TS(WinA[:, kc, 0:448], sg[:, 0:448], g1), reads=[bsg, bgv], writes=[bWinA])
                pr.add("dve", TS(WinA[:, kc, 448:480], sg[:, 416:448], g1, -1.0, ALU.mult, ALU.mult), reads=[bsg, bgv], writes=[bWinA])
                pr.add("dve", TS(WinA[:, kc, 480:512], sg[:, 384:416], g1), reads=[bsg, bgv], writes=[bWinA])
                pr.add("dve", TS(Wu[:, kc, :], sg[:, 448:INC], g1), reads=[bsg, bgv], writes=[bWu])
                if kc == 0:
                    pr.add("dve", CP(cwb[:].rearrange("p a c -> p (a c)"), cwB[:].rearrange("p a c -> p (a c)")), reads=[bcw], writes=[bcwb])
            for kc in range(2):
                sg, bsg = stg.next()
                g1 = gv[:, 8 + kc:9 + kc]
                pr.dma("sp", sg[:, 0:768], rows2d(W["w_uq"], 768, kc * 128, 128, 0, 768), reads=[INB], writes=[bsg])
                pr.add("dve", TS(Wuq[:, kc, :], sg[:, 0:768], g1), reads=[bsg, bgv], writes=[bWuq])
                for h in range(NH):
                    b0 = h * 192 + 128
                    pr.add("dve", TS(WuqR[:, kc, h * 64:h * 64 + 32], sg[:, b0 + 32:b0 + 64], g1, -1.0, ALU.mult, ALU.mult),
                           reads=[bsg, bgv], writes=[bWuqR])
                    pr.add("dve", TS(WuqR[:, kc, h * 64 + 32:h * 64 + 64], sg[:, b0:b0 + 32], g1), reads=[bsg, bgv], writes=[bWuqR])
            sg, bsg = stg.next()
            pr.dma("sp", sg[:, 0:1024], rows2d(W["w_ukv"], 1024, 0, 128, 0, 1024), reads=[INB], writes=[bsg])
            pr.add("dve", TS(Wukv[:], sg[:, 0:1024], gv[:, 10:11]), reads=[bsg, bgv], writes=[bWukv])
            for h in range(NH):
                pr.add("dve", TS(WV[:, h * 128:(h + 1) * 128], sg[:, h * 256 + 128:h * 256 + 256], gv[:, 10:11]),
                       reads=[bsg, bgv], writes=[bWV])
            pr.flush()
        xt = RPool(st, nc, "xtA", [128, D], F32, 3)
        sqp = RPool(st, nc, "sqA", [128, D], BF16, 2)
        small = RPool(st, nc, "smA", [128, 4], F32, 10)
        abf = RPool(st, nc, "abfA", [128, D], BF16, 2)
        aTp = RPool(st, nc, "aTA", [128, 8, 514], BF16, 3)
        tpp = RPool(st, nc, "tpA", [128, 1024], BF16, 2, psum=True)
        pm = RPool(st, nc, "pmA", [128, 512], F32, 6, psum=True)
        cqf = RPool(st, nc, "cqfA", [128, 2, 512], F32, 2)
        cqsq = RPool(st, nc, "cqsqA", [128, 2, 512], BF16, 1)
        rsp = RPool(st, nc, "rsA", [128, 512], F32, 3)
        cqn = RPool(st, nc, "cqnA", [128, 2, 512], BF16, 2)
        csp = RPool(st, nc, "csA", [64, 2, 512], F32, 2)
        rtp = RPool(st, nc, "rtA", [64, 512], F32, 4)
        oq = RPool(st, nc, "oqA", [128, 512], BF16, 4)
        orp = RPool(st, nc, "orA", [64, 512], BF16, 3)
        uev = RPool(st, nc, "uevA", [128, 1536], BF16, 2)

        ldp = RPool(st, nc, "ldA", [128, 1536], BF16, 6)
        cop = RPool(st, nc, "coA", [128, 1536], F32, 2)
        ctp = RPool(st, nc, "ctA", [128, 1536], F32, 2)
        cbp = RPool(st, nc, "cbA", [128, 1536], BF16, 2)

        def u_group(s, grp, aT_t, baT):
            t0, ntok, subs = grp
            for si, (t, nr, xr) in enumerate(subs):
                ue, bue = uev.next()
                for cg in range(3):
                    ps, bps = pm.next()
                    for kc in range(8):
                        pr.add("pe", MM(ps[0:nr, :], aT_t[:, kc, 1 + si * 128:1 + si * 128 + nr],
                                        Wu[:, kc, cg * 512:(cg + 1) * 512], kc == 0, kc == 7), reads=[baT, bWu], writes=[bps])
                    pr.add("act", ACTF(ue[0:nr, cg * 512:(cg + 1) * 512], ps[0:nr, :], AF.Copy), reads=[bps], writes=[bue])
                pr.dma("act", rows2d(sc["UR"], 1536, s.toff + t + 1, nr, 0, 1536), ue[0:nr, :], reads=[bue], writes=[db["UR"]])

        def conv_group(s, grp, aT_t=None, baT=None):
            t0, ntok, subs = grp
            for si, (t, nr, xr) in enumerate(subs):
                lds = []
                for i in range(3):
                    ld, bld = ldp.next()
                    pr.dma("sp", ld[0:nr, :], rows2d(sc["UR"], 1536, s.toff + t + i, nr, 0, 1536), reads=[db["UR"]], writes=[bld])
                    lds.append((ld, bld))
                co, bco = cop.next()
                ct, bct = ctp.next()
                cb, bcb = cbp.next()
                pr.add("dve", TT(co[0:nr, :], lds[0][0][0:nr, :], cwb[0:nr, 0, :], ALU.mult), reads=[lds[0][1], bcwb], writes=[bco])
                pr.add("dve", TT(ct[0:nr, :], lds[1][0][0:nr, :], cwb[0:nr, 1, :], ALU.mult), reads=[lds[1][1], bcwb], writes=[bct])
                pr.add("dve", TT(co[0:nr, :], co[0:nr, :], ct[0:nr, :], ALU.add), reads=[bco, bct], writes=[bco])
                pr.add("dve", TT(ct[0:nr, :], lds[2][0][0:nr, :], cwb[0:nr, 2, :], ALU.mult), reads=[lds[2][1], bcwb], writes=[bct])
                pr.add("dve", TT(cb[0:nr, :], co[0:nr, :], ct[0:nr, :], ALU.add), reads=[bco, bct], writes=[bcb])
                pr.dma("pool", rows2d(sc["UC"], 1536, s.toff + t, nr, 0, 1536), cb[0:nr, :], reads=[bcb], writes=[db["UC"]])

        def rope_out(ps1, b1, ps2, b2, cs, bcs, ntok, dst):
            t1, bt1 = rtp.next()
            t2, bt2 = rtp.next()
            pr.add("dve", TT(t1[:, :ntok], ps1[0:64, :ntok], cs[:, 0, :ntok], ALU.mult), reads=[b1, bcs], writes=[bt1])
            pr.add("dve", TT(t2[:, :ntok], ps2[0:64, :ntok], cs[:, 1, :ntok], ALU.mult), reads=[b2, bcs], writes=[bt2])
            o, bo = orp.next()
            pr.add("pool", TT(o[:, :ntok], t1[:, :ntok], t2[:, :ntok], ALU.add), reads=[bt1, bt2], writes=[bo])
            pr.dma("pool", dst, o[:, :ntok], reads=[bo], writes=[db["QT"], db["KR"]])

        def fm_norm(ps_list, nch, ntok, n_feat):
            cf, bcf = cqf.next()
            cs_, bcs_ = cqsq.next()
            for c, (ps, bps) in enumerate(ps_list):
                pr.add("act", ACTF(cf[:, c, :ntok], ps[:, :ntok], AF.Copy), reads=[bps], writes=[bcf])
                pr.add("act", ACTF(cs_[:, c, :ntok], ps[:, :ntok], AF.Square), reads=[bps], writes=[bcs_])
            pss, bpss = pm.next()
            for c in range(nch):
                pr.add("pe", MM(pss[:, :ntok], ones[:], cs_[:, c, :ntok], c == 0, c == nch - 1), reads=[bon, bcs_], writes=[bpss])
            l1, bl1 = rsp.next()
            pr.add("act", ACTF(l1[:, :ntok], pss[:, :ntok], AF.Ln, scale=1.0 / n_feat, bias=EPS), reads=[bpss], writes=[bl1])
            r1, br1 = rsp.next()
            pr.add("act", ACTF(r1[:, :ntok], l1[:, :ntok], AF.Exp, scale=-0.5), reads=[bl1], writes=[br1])
            cn, bcn = cqn.next()
            for c in range(nch):
                pr.add("dve", TT(cn[:, c, :ntok], cf[:, c, :ntok], r1[:, :ntok], ALU.mult), reads=[bcf, br1], writes=[bcn])
            return cn, bcn

        for s in k.seqs:
            L = s.L
            cosT, sinT = W[f"cosT_{L}"], W[f"sinT_{L}"]
            pr.dma("act", rows2d(sc["UC"], 1536, s.toff + L, s.T - L, 0, 1536), zt[0:s.T - L, :], reads=[bzt], writes=[db["UC"]])
            pr.dma("act", rows2d(sc["UR"], 1536, s.toff, 1, 0, 1536), zt[0:1, :], reads=[bzt], writes=[db["UR"]])
            pr.dma("act", rows2d(sc["UR"], 1536, s.toff + L + 1, 1, 0, 1536), zt[0:1, :], reads=[bzt], writes=[db["UR"]])
            ng = len(s.groups)

            def SX(gi):
                t0, ntok, subs = s.groups[gi]
                aT_t, baT = aTp.next()
                for si, (t, nr, xr) in enumerate(subs):
                    x_t, bx = xt.next()
                    if xr is None:
                        pr.add("pool", MSET(x_t[:], 0.0), writes=[bx])
                        pr.dma("sp", x_t[0:NMETA, :], W["meta_tokens"][:, :], reads=[INB], writes=[bx])
                    else:
                        pr.dma("sp", x_t[:], xrows(k, s, xr, 128), reads=[INB], writes=[bx])
                    sq_t, bsq = sqp.next()
                    rstd, brs = rstd_act(pr, small, sq_t[:], bsq, x_t[:], [bx], D)
                    a_t, ba = abf.next()
                    pr.add("act", ACTF(a_t[:], x_t[:], AF.Copy, scale=rstd), reads=[bx, brs], writes=[ba])
                    tp, btp = tpp.next()
                    for kc in range(8):
                        pr.add("pe", TR(tp[:, kc * 128:(kc + 1) * 128], a_t[:, kc * 128:(kc + 1) * 128], ident[:]),
                               reads=[ba, bid], writes=[btp])
                    ncol = nr if xr is None else 128
                    pr.add("dve", CP(aT_t[:, :, 1 + si * 128:1 + si * 128 + ncol],
                                     tp[:].rearrange("p (k t) -> p k t", k=8)[:, :, 0:ncol]), reads=[btp], writes=[baT])
                cs, bcs = csp.next()
                pr.dma("sp", cs[:, 0, :ntok], rows2d(cosT, L, 0, 64, t0, ntok), reads=[INB], writes=[bcs])
                pr.dma("sp", cs[:, 1, :ntok], rows2d(sinT, L, 0, 64, t0, ntok), reads=[INB], writes=[bcs])
                return (aT_t, baT, cs, bcs)

            def SP(gi, X, nextX):
                t0, ntok, subs = s.groups[gi]
                aT_t, baT, cs, bcs = X
                rhsA = [aT_t[:, kc, 1:1 + ntok] for kc in range(8)]
                col0 = s.toff + t0

                def proj(c0, m):
                    ps, bps = pm.next()
                    for kc in range(8):
                        pr.add("pe", MM(ps[0:m, :ntok], WinA[:, kc, c0:c0 + m], rhsA[kc], kc == 0, kc == 7),
                               reads=[bWinA, baT], writes=[bps])
                    return ps, bps
                pcq = [proj(0, 128), proj(128, 128)]
                pkv = [proj(256, 128)]
                pk1, bk1 = proj(384, 64)
                pk2, bk2 = proj(448, 64)
                Xn = nextX() if nextX is not None else None
                cn, bcn = fm_norm(pcq, 2, ntok, QL)
                kn, bkn = fm_norm(pkv, 1, ntok, KVL)
                rope_out(pk1, bk1, pk2, bk2, cs, bcs, ntok, DV(sc["KR"], col0, [[Tt, 64], [1, ntok]]))
                u_group(s, s.groups[gi], aT_t, baT)
                for h in range(NH):
                    ps, bps = pm.next()
                    for kc in range(2):
                        pr.add("pe", MM(ps[:, :ntok], Wuq[:, kc, h * 192:h * 192 + 128], cn[:, kc, :ntok], kc == 0, kc == 1),
                               reads=[bWuq, bcn], writes=[bps])
                    o, bo = oq.next()
                    pr.add("act", ACTF(o[:, :ntok], ps[:, :ntok], AF.Copy), reads=[bps], writes=[bo])
                    pr.dma("act", DV(sc["QT"], (h * 192) * Tt + col0, [[Tt, 128], [1, ntok]]), o[:, :ntok], reads=[bo], writes=[db["QT"]])
                    ps1, b1 = pm.next()
                    for kc in range(2):
                        pr.add("pe", MM(ps1[0:64, :ntok], Wuq[:, kc, h * 192 + 128:h * 192 + 192], cn[:, kc, :ntok], kc == 0, kc == 1),
                               reads=[bWuq, bcn], writes=[b1])
                    ps2, b2 = pm.next()
                    for kc in range(2):
                        pr.add("pe", MM(ps2[0:64, :ntok], WuqR[:, kc, h * 64:(h + 1) * 64], cn[:, kc, :ntok], kc == 0, kc == 1),
                               reads=[bWuqR, bcn], writes=[b2])
                    rope_out(ps1, b1, ps2, b2, cs, bcs, ntok, DV(sc["QT"], (h * 192 + 128) * Tt + col0, [[Tt, 64], [1, ntok]]))
                for h in range(NH):
                    ps, bps = pm.next()
                    pr.add("pe", MM(ps[:, :ntok], Wukv[:, h * 256:h * 256 + 128], kn[:, 0, :ntok], True, True),
                           reads=[bWukv, bkn], writes=[bps])
                    o, bo = oq.next()
                    pr.add("act", ACTF(o[:, :ntok], ps[:, :ntok], AF.Copy), reads=[bps], writes=[bo])
                    pr.dma("act", DV(sc["KN"], (h * 128) * Tt + col0, [[Tt, 128], [1, ntok]]), o[:, :ntok], reads=[bo], writes=[db["KN"]])
                for si, (t, nr, xr) in enumerate(subs):
                    ps, bps = pm.next()
                    pr.add("pe", MM(ps[0:nr, :], kn[:, 0, si * 128:si * 128 + nr], WV[:], True, True), reads=[bWV, bkn], writes=[bps])
                    o, bo = oq.next()
                    pr.add("act", ACTF(o[0:nr, :], ps[0:nr, :], AF.Copy), reads=[bps], writes=[bo])
                    pr.dma("act", rows2d(sc["V"], 512, s.toff + t, nr, 0, 512), o[0:nr, :], reads=[bo], writes=[db["V"]])
                return Xn
            X = SX(0)
            for gi in range(ng):
                X = SP(gi, X, (lambda g=gi: SX(g + 1)) if gi + 1 < ng else None)
                if gi >= 1:
                    conv_group(s, s.groups[gi - 1])
            conv_group(s, s.groups[ng - 1])
        pr.flush()


def phase_B(k):
    nc, pr, W, sc, db = k.nc, k.pr, k.din, k.sc, k.db
    INB = k.din_buf
    Tt = k.Ttot
    Lm = max(s.L for s in k.seqs)
    Bm = max(s.B for s in k.seqs)
    with contextlib.ExitStack() as st:
        def sb(name, shape, dt):
            return st.enter_context(nc.sbuf_tensor(U(name), shape, dt))
        KNt, bKN = sb("KNt", [128, NH, Lm], BF16), Buf("KNt")
        KRt, bKR = sb("KRt", [128, Lm], BF16), Buf("KRt")
        Vt, bV = sb("Vt", [128, Bm, 512], BF16), Buf("Vt")
        ones, bon = sb("onesB", [128, 128], BF16), Buf("ones")
        pr.dma("sp", ones[:], W["ones"][:, :], reads=[INB], writes=[bon])
        onesf, bonf = sb("onesfB", [128, 128], F32), Buf("onesf")
        pr.add("pool", MSET(onesf[:], 1.0), writes=[bonf])
        accp = RPool(st, nc, "accB", [128, 2, 512], F32, 2)
        qnp = RPool(st, nc, "qnB", [128, 512], BF16, 2)
        qrp = RPool(st, nc, "qrB", [128, 512], BF16, 2)
        pr.add("dve", MSET(KRt[64:128, :], 0.0), writes=[bKR])
        for t_, b_ in qrp.tiles:
            pr.add("dve", MSET(t_[64:128, :], 0.0), writes=[b_])
        ptp = RPool(st, nc, "ptB", [128, 2, 512], BF16, 4)
        ptb2 = {id(t_): Buf("ptslot1") for t_, b_ in ptp.tiles}
        rsp = RPool(st, nc, "rsB", [128, 512], F32, 2)
        ohp = RPool(st, nc, "ohB", [128, NH, 512], F32, 2)
        sqp = RPool(st, nc, "sqB", [128, NH, 512], BF16, 1)
        anp = RPool(st, nc, "anB", [128, NH, 512], BF16, 2)
        pss = RPool(st, nc, "pssB", [128, 512], F32, 4, psum=True)
        pop = RPool(st, nc, "poB", [128, 512], F32, 2, psum=True)
        psm = RPool(st, nc, "psmB", [128, 512], F32, 1, psum=True)
        pep = RPool(st, nc, "pepB", [128, 512], F32, 1, psum=True)
        for s in k.seqs:
            L, n = s.L, s.n
            for h in range(NH):
                pr.dma("sp", KNt[:, h, 0:L], DV(sc["KN"], h * 128 * Tt + s.toff, [[Tt, 128], [1, L]]), reads=[db["KN"]], writes=[bKN])
            pr.dma("sp", KRt[0:64, 0:L], DV(sc["KR"], s.toff, [[Tt, 64], [1, L]]), reads=[db["KR"]], writes=[bKR])
            pr.dma("sp", Vt[0:NMETA, 0, :], rows2d(sc["V"], 512, s.toff, NMETA, 0, 512), reads=[db["V"]], writes=[bV])
            i = 0
            while i < n:
                c = min(16, n - i)
                pr.dma("sp", Vt[:, 1 + i:1 + i + c, :], DV(sc["V"], (s.toff + NMETA + 128 * i) * 512, [[512, 128], [128 * 512, c], [1, 512]]),
                       reads=[db["V"]], writes=[bV])
                i += c
            ktiles = [(0, NMETA)] + [(NMETA + 128 * i, 128) for i in range(n)]
            for (t0, ntok, subs) in s.groups[1:]:
                col0 = s.toff + t0
                oh, boh = ohp.next()
                for h in range(NH):
                    qn, bqn = qnp.next()
                    qr, bqr = qrp.next()
                    pr.dma("sp", qn[:, :ntok], DV(sc["QT"], h * 192 * Tt + col0, [[Tt, 128], [1, ntok]]), reads=[db["QT"]], writes=[bqn])
                    pr.dma("sp", qr[0:64, :ntok], DV(sc["QT"], (h * 192 + 128) * Tt + col0, [[Tt, 64], [1, ntok]]), reads=[db["QT"]], writes=[bqr])
                    po, bpo = pop.next()
                    pm_, bpm = psm.next()
                    ac, bac = accp.next()
                    pr.add("pool", MSET(ac[:].rearrange("p a c -> p (a c)"), 0.0), writes=[bac])
                    pstate = {}

                    def s_mm(kt):
                        c0, nk = ktiles[kt]
                        ps, bps = pss.next()
                        pr.add("pe", MM(ps[0:nk, :ntok], KNt[:, h, c0:c0 + nk], qn[:, :ntok], True, False), reads=[bKN, bqn], writes=[bps])
                        pr.add("pe", MM(ps[0:nk, :ntok], KRt[:, c0:c0 + nk], qr[:, :ntok], False, True), reads=[bKR, bqr], writes=[bps])
                        slot = 0 if kt == 0 else (kt - 1) % 2
                        if kt == 0 or slot == 0:
                            pstate["cur"] = ptp.next()
                        ptt, bp0 = pstate["cur"]
                        bpt = bp0 if slot == 0 else ptb2[id(ptt)]
                        pt = ptt[:, slot, :]
                        pr.add("act", ACTF(pt[0:nk, :ntok], ps[0:nk, :ntok], AF.Exp, scale=SCALE), reads=[bps], writes=[bpt])
                        if kt == 0:
                            pr.add("dve", TT(ac[0:nk, 0, :ntok], ac[0:nk, 0, :ntok], pt[0:nk, :ntok], ALU.add), reads=[bac, bpt], writes=[bac])
                        elif slot == 1:
                            pr.add("dve", TT(ac[:, :, :ntok], ac[:, :, :ntok], ptt[:, :, :ntok], ALU.add), reads=[bac, bp0, bpt], writes=[bac])
                        elif kt == nkt - 1:
                            pr.add("dve", TT(ac[:, 0, :ntok], ac[:, 0, :ntok], pt[:, :ntok], ALU.add), reads=[bac, bpt], writes=[bac])
                        return pt, bpt, nk
                    nkt = len(ktiles)
                    LA = 2
                    q_ = [s_mm(i) for i in range(min(LA, nkt))]
                    for kt in range(nkt):
                        if kt + LA < nkt:
                            q_.append(s_mm(kt + LA))
                        pt, bpt, nk = q_.pop(0)
                        pr.add("pe", MM(po[:, :ntok], Vt[0:nk, kt, h * 128:(h + 1) * 128], pt[0:nk, :ntok], kt == 0, kt == nkt - 1),
                               reads=[bV, bpt], writes=[bpo])
                    for ai in range(2):
                        pr.add("pe", MM(pm_[:, :ntok], onesf[:], ac[:, ai, :ntok], ai == 0, ai == 1), reads=[bonf, bac], writes=[bpm])
                    rs, brs = rsp.next()
                    pr.add("dve", RECIP(rs[:, :ntok], pm_[:, :ntok]), reads=[bpm], writes=[brs])
                    pr.add("dve", TT(oh[:, h, :ntok], po[:, :ntok], rs[:, :ntok], ALU.mult), reads=[bpo, brs], writes=[boh])
                sq, bsq = sqp.next()
                pe_, bpe = pep.next()
                for h in range(NH):
                    pr.add("act", ACTF(sq[:, h, :ntok], oh[:, h, :ntok], AF.Square), reads=[boh], writes=[bsq])
                for h in range(NH):
                    pr.add("pe", MM(pe_[:, :ntok], ones[:], sq[:, h, :ntok], h == 0, h == NH - 1), reads=[bon, bsq], writes=[bpe])
                l1, bl1 = rsp.next()
                pr.add("act", ACTF(l1[:, :ntok], pe_[:, :ntok], AF.Ln, scale=1.0 / 512, bias=EPS), reads=[bpe], writes=[bl1])
                r1, br1 = rsp.next()
                pr.add("act", ACTF(r1[:, :ntok], l1[:, :ntok], AF.Exp, scale=-0.5), reads=[bl1], writes=[br1])
                an, ban = anp.next()
                for h in range(NH):
                    pr.add("dve" if h % 2 == 0 else "pool", TT(an[:, h, :ntok], oh[:, h, :ntok], r1[:, :ntok], ALU.mult), reads=[boh, br1], writes=[ban])
                pr.dma("pool", DV(sc["AN"], col0, [[Tt, 128], [128 * Tt, NH], [1, ntok]]), an[:, :, :ntok], reads=[ban], writes=[db["AN"]])
        pr.flush()


def phase_D1(k):
    nc, pr, W, sc, db = k.nc, k.pr, k.din, k.sc, k.db
    INB = k.din_buf
    Tt = k.Ttot
    with contextlib.ExitStack() as st:
        def sb(name, shape, dt):
            return st.enter_context(nc.sbuf_tensor(U(name), shape, dt))
        Wo, bWo = sb("Wo", [128, 8, D], BF16), Buf("Wo")
        gB, bgB = sb("gB1", [128, D], F32), Buf("gB1")
        gv, bgv = sb("gv1", [128, 8], F32), Buf("gv1")
        ident, bid = sb("identD1", [128, 128], BF16), Buf("ident")
        pr.dma("sp", ident[:], W["ident"][:, :], reads=[INB], writes=[bid])
        pr.dma("sp", gB[:], DV(W["post_mix_g"], 0, [[0, 128], [1, D]]), reads=[INB], writes=[bgB])
        for kc in range(4):
            pr.dma("sp", gv[:, kc:kc + 1], col1(W["attn_out_g"], kc * 128), reads=[INB], writes=[bgv])
            pr.dma("sp", gv[:, 4 + kc:5 + kc], col1(W["hy_out_g"], kc * 128), reads=[INB], writes=[bgv])
        stg = RPool(st, nc, "stgD1", [128, D], F32, 2)
        for kc in range(8):
            sg, bsg = stg.next()
            pr.dma("sp", sg[:], rows2d(W["w_o"], D, kc * 128, 128, 0, D), reads=[INB], writes=[bsg])
            pr.add("dve", TS(Wo[:, kc, :], sg[:], gv[:, kc:kc + 1]), reads=[bsg, bgv], writes=[bWo])
        anp = RPool(st, nc, "anD1", [128, NH, 512], BF16, 3)
        zp = RPool(st, nc, "zD1", [128, 512], BF16, 4)
        xp = RPool(st, nc, "xD1", [128, D], F32, 5)
        sqp = RPool(st, nc, "sqD1", [128, D], BF16, 3)
        small = RPool(st, nc, "smD1", [128, 4], F32, 12)
        hnp = RPool(st, nc, "hnD1", [128, 512], BF16, 3)
        hTp = RPool(st, nc, "hTD1", [128, 4, 128], BF16, 5)
        tpp = RPool(st, nc, "tpD1", [128, 512], BF16, 2, psum=True)
        pmm = RPool(st, nc, "pmD1", [128, 1024], F32, 3, psum=True)
        tp_ = RPool(st, nc, "tD1", [128, D], F32, 2)
        hp = RPool(st, nc, "hD1", [128, D], F32, 2)
        work = []
        for s in k.seqs:
            for (t0, ntok, subs) in s.groups[1:]:
                for si, sub in enumerate(subs):
                    work.append((s, t0, ntok, si, sub))

        def stX(w):
            s, t0, ntok, si, (t, nr, xr) = w
            col0 = s.toff + t0
            if si == 0:
                an, ban = anp.next()
                pr.dma("sp", an[:, :, :ntok], DV(sc["AN"], col0, [[Tt, 128], [128 * Tt, NH], [1, ntok]]), reads=[db["AN"]], writes=[ban])
                stX.an = (an, ban)
            an, ban = stX.an
            z, bz = zp.next()
            pr.dma("sp", z[:], rows2d(sc["Z2"], 512, s.toff + t, 128, 0, 512), reads=[db["Z2"]], writes=[bz])
            x_t, bx = xp.next()
            pr.dma("sp", x_t[:], xrows(k, s, xr, 128), reads=[INB], writes=[bx])
            sq, bsq = sqp.next()
            rstd, brs = rstd_act(pr, small, sq[:, 0:512], bsq, z[:], [bz], 512)
            hn, bhn = hnp.next()
            pr.add("act", ACTF(hn[:], z[:], AF.Copy, scale=rstd), reads=[bz, brs], writes=[bhn])
            tp, btp = tpp.next()
            for kc in range(4):
                pr.add("pe", TR(tp[:, kc * 128:(kc + 1) * 128], hn[:, kc * 128:(kc + 1) * 128], ident[:]), reads=[bhn, bid], writes=[btp])
            hT, bhT = hTp.next()
            pr.add("dve", CP(hT[:], tp[:].rearrange("p (k t) -> p k t", k=4)), reads=[btp], writes=[bhT])
            return (an, ban, hT, bhT, x_t, bx)

        def stYm(w, xs_):
            s, t0, ntok, si, (t, nr, xr) = w
            an, ban, hT, bhT, x_t, bx = xs_
            ps, bps = pmm.next()
            for half in range(2):
                for kc in range(4):
                    pr.add("pe", MM(ps[:, half * 512:(half + 1) * 512], an[:, kc, si * 128:(si + 1) * 128], Wo[:, kc, half * 512:(half + 1) * 512], kc == 0, False),
                           reads=[ban, bWo], writes=[bps])
                for kc in range(4):
                    pr.add("pe", MM(ps[:, half * 512:(half + 1) * 512], hT[:, kc, :], Wo[:, 4 + kc, half * 512:(half + 1) * 512], False, kc == 3),
                           reads=[bhT, bWo], writes=[bps])
            return ps, bps

        def stYe(w, xs_, pp_):
            s, t0, ntok, si, (t, nr, xr) = w
            an, ban, hT, bhT, x_t, bx = xs_
            ps, bps = pp_
            sq2, bsq2 = sqp.next()
            rstd2, brs2 = rstd_act(pr, small, sq2[:], bsq2, ps[:], [bps], D)
            tt, btt = tp_.next()
            pr.add("dve", STT(tt[:], ps[:], rstd2, gB[:], ALU.mult, ALU.mult), reads=[bps, brs2, bgB], writes=[btt])
            h1, bh1 = hp.next()
            pr.add("dve", TT(h1[:], tt[:], x_t[:], ALU.add), reads=[btt, bx], writes=[bh1])
            pr.dma("pool", rows2d(sc["H1"], D, s.toff + t, 128, 0, D), h1[:], reads=[bh1], writes=[db["H1"]])
        LOOK = 2
        pend = {}
        for i in range(min(LOOK, len(work))):
            pend[i] = stX(work[i])
        for i in range(len(work)):
            pp_ = stYm(work[i], pend[i])
            if i + LOOK < len(work):
                pend[i + LOOK] = stX(work[i + LOOK])
            stYe(work[i], pend.pop(i), pp_)
        pr.flush()


GD2 = 2


def phase_D2(k):
    nc, pr, W, sc, db = k.nc, k.pr, k.din, k.sc, k.db
    INB = k.din_buf
    NT = GD2 * 128
    with contextlib.ExitStack() as st:
        def sb(name, shape, dt):
            return st.enter_context(nc.sbuf_tensor(U(name), shape, dt))
        W1, bW1 = sb("W1", [128, 8, DFF], BF16), Buf("W1")
        W2, bW2 = sb("W2", [128, 32, D], BF16), Buf("W2")
        gB, bgB = sb("gB2", [128, D], F32), Buf("gB2")
        gv, bgv = sb("gv2", [128, 8], F32), Buf("gv2")
        ident, bid = sb("identD2", [128, 128], BF16), Buf("ident")
        pr.dma("sp", ident[:], W["ident"][:, :], reads=[INB], writes=[bid])
        pr.dma("sp", gB[:], DV(W["post_mlp_g"], 0, [[0, 128], [1, D]]), reads=[INB], writes=[bgB])
        for kc in range(8):
            pr.dma("sp", gv[:, kc:kc + 1], col1(W["pre_mlp_g"], kc * 128), reads=[INB], writes=[bgv])
        with contextlib.ExitStack() as st2:
            stg = RPool(st2, nc, "stgD2", [128, DFF], F32, 3)
            for kc in range(8):
                sg, bsg = stg.next()
                pr.dma("sp", sg[:], rows2d(W["w_ff1"], DFF, kc * 128, 128, 0, DFF), reads=[INB], writes=[bsg])
                if kc % 2 == 0:
                    pr.add("dve", TS(W1[:, kc, :], sg[:], gv[:, kc:kc + 1]), reads=[bsg, bgv], writes=[bW1])
                else:
                    pr.add("act", ACTF(W1[:, kc, :], sg[:], AF.Copy, scale=gv[:, kc:kc + 1]), reads=[bsg, bgv], writes=[bW1])
            for f4 in range(8):
                sg, bsg = stg.next()
                pr.dma("sp", sg[:].rearrange("p (a c) -> p a c", a=4),
                       DV(W["w_ff2"], f4 * 4 * 128 * D, [[D, 128], [128 * D, 4], [1, D]]), reads=[INB], writes=[bsg])
                eng = ("dve", "act")[f4 % 2]
                if eng == "act":
                    pr.add("act", ACTF(W2[:, f4 * 4:(f4 + 1) * 4, :], sg[:].rearrange("p (a c) -> p a c", a=4), AF.Copy), reads=[bsg], writes=[bW2])
                else:
                    pr.add(eng, CP(W2[:, f4 * 4:(f4 + 1) * 4, :], sg[:].rearrange("p (a c) -> p a c", a=4)), reads=[bsg], writes=[bW2])
            pr.flush()
        hp = RPool(st, nc, "hD2", [128, D], F32, 3)
        h2p = RPool(st, nc, "h2D2", [128, D], F32, 2)
        sqp = RPool(st, nc, "sqD2", [128, D], BF16, 2)
        small = RPool(st, nc, "smD2", [128, 4], F32, 12)
        abf = RPool(st, nc, "abfD2", [128, D], BF16, 4)
        aTp = RPool(st, nc, "aTD2", [128, 8, NT], BF16, 2)
        mT = sb("mT", [128, 32, NT], BF16)
        bmT = [Buf(f"mT{i}") for i in range(32)]
        rp = RPool(st, nc, "rD2", [128, NT], F32, 3)
        tpp = RPool(st, nc, "tpD2", [128, 1024], BF16, 1, psum=True)
        pmm = RPool(st, nc, "pmD2", [128, 512], F32, 3, psum=True)
        pm2 = RPool(st, nc, "pm2D2", [128, 1024], F32, 2, psum=True)
        tp_ = RPool(st, nc, "tD2", [128, D], F32, 2)
        yp_ = RPool(st, nc, "yD2", [128, D], F32, 2)
        work = []
        for s in k.seqs:
            subs_all = [sub for g in s.groups[1:] for sub in g[2]]
            for g0 in range(0, len(subs_all), GD2):
                work.append((s, subs_all[g0:g0 + GD2]))

        def stXa(w, sis=None):
            s, subs = w
            outs = []
            for si, (t, nr, xr) in enumerate(subs):
                if sis is not None and si not in sis:
                    continue
                h1, bh1 = hp.next()
                pr.dma("sp", h1[:], rows2d(sc["H1"], D, s.toff + t, 128, 0, D), reads=[db["H1"]], writes=[bh1])
                sq, bsq = sqp.next()
                rstd, brs = rstd_act(pr, small, sq[:], bsq, h1[:], [bh1], D)
                a_t, ba = abf.next()
                pr.add("act", ACTF(a_t[:], h1[:], AF.Copy, scale=rstd), reads=[bh1, brs], writes=[ba])
                outs.append((a_t, ba))
            return outs

        def stXb(w, outs):
            s, subs = w
            aT, baT = aTp.next()
            for si, (t, nr, xr) in enumerate(subs):
                a_t, ba = outs[si]
                tp, btp = tpp.next()
                for kc in range(8):
                    pr.add("pe", TR(tp[:, kc * 128:(kc + 1) * 128], a_t[:, kc * 128:(kc + 1) * 128], ident[:]), reads=[ba, bid], writes=[btp])
                pr.add("dve", CP(aT[:, :, si * 128:(si + 1) * 128], tp[:].rearrange("p (k t) -> p k t", k=8)), reads=[btp], writes=[baT])
            return aT, baT

        def stF1(w, aTs, fcs):
            s, subs = w
            aT, baT = aTs
            ntok = 128 * len(subs)
            for fc in fcs:
                ps, bps = pmm.next()
                for kc in range(8):
                    pr.add("pe", MM(ps[:, :ntok], W1[:, kc, fc * 128:(fc + 1) * 128], aT[:, kc, :ntok], kc == 0, kc == 7),
                           reads=[bW1, baT], writes=[bps])
                r, br = rp.next()
                pr.add("act", ACTF(r[:, :ntok], ps[:, :ntok], AF.Relu), reads=[bps], writes=[br])
                pr.add("dve" if fc % 4 != 3 else "pool", TT(mT[:, fc, :ntok], r[:, :ntok], r[:, :ntok], ALU.mult), reads=[br], writes=[bmT[fc]])

        def stF2(w):
            s, subs = w
            for si, (t, nr, xr) in enumerate(subs):
                ps, bps = pm2.next()
                for half in range(2):
                    for fc in range(32):
                        pr.add("pe", MM(ps[:, half * 512:(half + 1) * 512], mT[:, fc, si * 128:(si + 1) * 128], W2[:, fc, half * 512:(half + 1) * 512], fc == 0, fc == 31),
                               reads=[bmT[fc], bW2], writes=[bps])
                sq2, bsq2 = sqp.next()
                rstd2, brs2 = rstd_act(pr, small, sq2[:], bsq2, ps[:], [bps], D)
                tt, btt = tp_.next()
                pr.add("dve", STT(tt[:], ps[:], rstd2, gB[:], ALU.mult, ALU.mult), reads=[bps, brs2, bgB], writes=[btt])
                h2, bh2 = h2p.next()
                pr.dma("sp", h2[:], rows2d(sc["H1"], D, s.toff + t, 128, 0, D), reads=[db["H1"]], writes=[bh2])
                y, by = yp_.next()
                pr.add("dve", TT(y[:], tt[:], h2[:], ALU.add), reads=[btt, bh2], writes=[by])
                pr.dma("pool", yrows(k, s, xr, 128), y[:], reads=[by], writes=[k.dout])
        cur = stXb(work[0], stXa(work[0]))
        for i, w in enumerate(work):
            has_next = i + 1 < len(work)
            nsub = len(work[i + 1][1]) if has_next else 0
            stF1(w, cur, range(0, 5))
            nxa = stXa(work[i + 1], [0]) if has_next else []
            stF1(w, cur, range(5, 11))
            if has_next and nsub > 1:
                nxa = nxa + stXa(work[i + 1], list(range(1, nsub)))
            stF1(w, cur, range(11, 20))
            nxt = stXb(work[i + 1], nxa) if has_next else None
            stF1(w, cur, range(20, 32))
            stF2(w)
            cur = nxt
        pr.flush()


GCH = [(0, 128), (128, 127)]


def load_chunked(pr, st, nc, name, src, nrows, ncols, chunks, INB):
    out = []
    for ci, (r0, rn) in enumerate(chunks):
        t = st.enter_context(nc.sbuf_tensor(U(f"{name}{ci}"), [128, ncols], BF16))
        b = Buf(f"{name}{ci}")
        pr.dma("sp", t[0:rn, :], rows2d(src, ncols, r0, rn, 0, ncols), reads=[INB], writes=[b])
        out.append((t, b, rn))
    return out


def phase_F(k):
    nc, pr, W, sc, db = k.nc, k.pr, k.din, k.sc, k.db
    INB = k.din_buf
    with contextlib.ExitStack() as st:
        def sb(name, shape, dt):
            return st.enter_context(nc.sbuf_tensor(U(name), shape, dt))
        fw1, fw2, fw3 = sb("fw1", [33, 64], F32), sb("fw2", [64, 64], F32), sb("fw3", [64, 2048], F32)
        fv, dB = sb("fv", [64, 8], F32), sb("dB", [128, 2048], F32)
        bw, bfv, bdB = Buf("fw"), Buf("fv"), Buf("dB")
        pr.dma("sp", fw1[:], rows2d(W["f_w1"], 64, 0, 33, 0, 64), reads=[INB], writes=[bw])
        pr.dma("sp", fw2[:], rows2d(W["f_w2"], 64, 0, 64, 0, 64), reads=[INB], writes=[bw])
        pr.dma("sp", fw3[:], rows2d(W["f_w3"], 2048, 0, 64, 0, 2048), reads=[INB], writes=[bw])
        fw3b, bw3b = sb("fw3b", [64, 2048], BF16), Buf("fw3b")
        pr.add("dve", CP(fw3b[:], fw3[:]), reads=[bw], writes=[bw3b])
        pr.dma("sp", fv[:, 0:1], col1(W["f_freq"], 0, 64), reads=[INB], writes=[bfv])
        pr.dma("sp", fv[:, 1:2], col1(W["f_b1"], 0, 64), reads=[INB], writes=[bfv])
        pr.dma("sp", fv[:, 2:3], col1(W["f_b2"], 0, 64), reads=[INB], writes=[bfv])
        pr.add("dve", TT(fv[:, 3:4], fv[:, 1:2], fv[:, 0:1], ALU.mult), reads=[bfv], writes=[bfv])
        pr.add("dve", TT(fv[:, 4:5], fv[:, 2:3], fv[:, 0:1], ALU.mult), reads=[bfv], writes=[bfv])
        pr.dma("sp", dB[:], DV(W["f_decay"], 0, [[0, 128], [1, 2048]]), reads=[INB], writes=[bdB])
        pr.add("act", ACTF(dB[:], dB[:], AF.Abs), reads=[bdB], writes=[bdB])
        hb, bhb = sb("hbias", [1, 2, 512], F32), Buf("hbias")
        pr.dma("sp", hb[:], DV(W["hy_bias"], 0, [[0, 1], [512, 2], [1, 512]]), reads=[INB], writes=[bhb])
        FB_ = {nm: load_chunked(pr, st, nc, f"F{nm}", W[nm], NG, NG, GCH, INB) for nm in ("FBc", "FBs", "FBsn")}
        for L in k.Ls:
            B = (L - NMETA) // 128 + 1
            BP = prow(B)
            T = 128 * B
            PP = 2 * B - 1
            featsT, tneg = W[f"featsT_{L}"], W[f"tneg_{L}"]
            with contextlib.ExitStack() as s2:
                CH = 512
                ftp = RPool(s2, nc, "ftF", [33, CH], F32, 3)
                tnp_ = RPool(s2, nc, "tnF", [128, 4], F32, 5)
                ap_ = RPool(s2, nc, "aF", [64, CH], F32, 6)
                tq = RPool(s2, nc, "tF", [64, CH], F32, 4)
                hp_ = RPool(s2, nc, "hF", [64, CH], F32, 4)
                hbp = RPool(s2, nc, "hbF", [64, CH], BF16, 4)
                ep = RPool(s2, nc, "eF", [128, 512], F32, 3)
                kp = RPool(s2, nc, "kF", [128, 512], BF16, 4)
                p1 = RPool(s2, nc, "p1F", [64, CH], F32, 3, psum=True)
                p3 = RPool(s2, nc, "p3F", [128, 512], F32, 4, psum=True)

                def sin_layer(ps, bps, bias_col, n_, outp=None):
                    a, ba = ap_.next()
                    pr.add("dve", TS(a[:, :n_], ps[:, :n_], fv[:, 0:1], fv[:, bias_col:bias_col + 1], ALU.mult, ALU.add), reads=[bps, bfv], writes=[ba])
                    t, bt = tq.next()
                    pr.add("dve", TS(t[:, :n_], a[:, :n_], PI, -2.0 * PI, ALU.is_gt, ALU.mult), reads=[ba], writes=[bt])
                    pr.add("dve", TT(a[:, :n_], a[:, :n_], t[:, :n_], ALU.add), reads=[ba, bt], writes=[ba])
                    t, bt = tq.next()
                    pr.add("dve", TS(t[:, :n_], a[:, :n_], -PI, 2.0 * PI, ALU.is_lt, ALU.mult), reads=[ba], writes=[bt])
                    pr.add("dve", TT(a[:, :n_], a[:, :n_], t[:, :n_], ALU.add), reads=[ba, bt], writes=[ba])
                    h, bh = (outp or hp_).next()
                    pr.add("act", ACTF(h[:, :n_], a[:, :n_], AF.Sin), reads=[ba], writes=[bh])
                    return h, bh
                chunks = [(r0, min(CH, 2 * T - r0)) for r0 in range(0, 2 * T, CH)]

                def st1(c):
                    r0, n_ = chunks[c]
                    ft, bft = ftp.next()
                    pr.dma("sp", ft[:, :n_], rows2d(featsT, 2 * T, 0, 33, r0, n_), reads=[INB], writes=[bft])
                    tn, btn = tnp_.next()
                    for j in range(n_ // 128):
                        pr.dma("sp", tn[:, j:j + 1], col1(tneg, r0 + 128 * j), reads=[INB], writes=[btn])
                    ps, bps = p1.next()
                    pr.add("pe", MM(ps[:, :n_], fw1[:], ft[:, :n_], True, True), reads=[bw, bft], writes=[bps])
                    h1, bh1 = sin_layer(ps, bps, 3, n_)
                    return (h1, bh1, tn, btn)

                def st2(c, s1_):
                    r0, n_ = chunks[c]
                    h1, bh1, tn, btn = s1_
                    ps, bps = p1.next()
                    pr.add("pe", MM(ps[:, :n_], fw2[:], h1[:, :n_], True, True), reads=[bw, bh1], writes=[bps])
                    h2, bh2 = sin_layer(ps, bps, 4, n_, hbp)
                    return (h2, bh2, tn, btn)

                def st3(c, s2_):
                    r0, n_ = chunks[c]
                    h2, bh2, tn, btn = s2_
                    for j in range(n_ // 128):
                        dirn = 0 if (r0 + 128 * j - T) >= 0 else 1
                        for n in range(2):
                            cols = n * 1024 + dirn * 512
                            ps3, bp3 = p3.next()
                            pr.add("pe", MM(ps3[:], h2[:, j * 128:(j + 1) * 128], fw3b[:, cols:cols + 512], True, True), reads=[bw3b, bh2], writes=[bp3])
                            E, bE = ep.next()
                            pr.add("act", ACTF(E[:], dB[:, cols:cols + 512], AF.Exp, scale=tn[:, j:j + 1]), reads=[bdB, btn], writes=[bE])
                            kt, bkt = kp.next()
                            if r0 + 128 * j == T:
                                pr.add("dve", TT(E[:], ps3[:], E[:], ALU.mult), reads=[bp3, bE], writes=[bE])
                                pr.add("dve", TT(E[0:1, :], E[0:1, :], hb[0:1, n, :], ALU.add), reads=[bE, bhb], writes=[bE])
                                pr.add("dve", CP(kt[:], E[:]), reads=[bE], writes=[bkt])
                            else:
                                pr.add("dve", TT(kt[:], ps3[:], E[:], ALU.mult), reads=[bp3, bE], writes=[bkt])
                            nm = f"KTIME{n}_{L}"
                            pr.dma("pool", rows2d(sc[nm], 512, r0 + 128 * j, 128, 0, 512), kt[:], reads=[bkt], writes=[db[nm]])
                NCK = len(chunks)
                r1, r2 = {}, {}
                for i in range(NCK + 2):
                    if i < NCK:
                        r1[i] = st1(i)
                    if 0 <= i - 1 < NCK:
                        r2[i - 1] = st2(i - 1, r1.pop(i - 1))
                    if 0 <= i - 2 < NCK:
                        st3(i - 2, r2.pop(i - 2))
                pr.flush()
            S1F, bS1F = sc[f"S1F_{L}"], db[f"S1F_{L}"]
            for n in range(2):
                KT, bKT = sc[f"KTIME{n}_{L}"], db[f"KTIME{n}_{L}"]
                KH, bKH = sc[f"KHAT{n}_{L}"], db[f"KHAT{n}_{L}"]
                with contextlib.ExitStack() as s2:
                    kch = _split(PP)
                    fa = []
                    for p, nm in enumerate(("FA_re", "FA_im")):
                        lst = []
                        for ci, (r0_, rn_) in enumerate(kch):
                            t_ = s2.enter_context(nc.sbuf_tensor(U(f"FA{p}{ci}"), [128, 128], BF16))
                            b_ = Buf(f"FA{p}{ci}")
                            pr.add("dve", MSET(t_[:], 0.0), writes=[b_])
                            pr.dma("sp", t_[0:rn_, 0:B], rows2d(W[f"{nm}_{L}"], B, r0_, rn_, 0, B), reads=[INB], writes=[b_])
                            lst.append((t_, b_, rn_))
                        fa.append(lst)
                    EB = 8
                    xin = [RPool(s2, nc, f"xinF{ci}", [128, EB, 512], BF16, 2) for ci in range(len(kch))]
                    for xp_ in xin:
                        for t_, b_ in xp_.tiles:
                            pr.add("dve", MSET(t_[:].rearrange("p a c -> p (a c)"), 0.0), writes=[b_])
                    sop = [RPool(s2, nc, f"soF{p_}", [128, EB, 512], BF16, 2) for p_ in range(2)]
                    for sp_ in sop:
                        for t_, b_ in sp_.tiles:
                            pr.add("dve", MSET(t_[:].rearrange("p a c -> p (a c)"), 0.0), writes=[b_])
                    pp = RPool(s2, nc, "ppF", [128, 512], F32, 8, psum=True)
                    for e0 in range(0, NG, EB):
                        ne = min(EB, NG - e0)
                        xs = []
                        for ci, (d0, dn) in enumerate(kch):
                            xt_, bx = xin[ci].next()
                            pr.dma("sp", xt_[0:dn, 0:ne, :], DV(KT, (128 * d0 + e0 + 1) * 512, [[128 * 512, dn], [512, ne], [1, 512]]), reads=[bKT], writes=[bx])
                            xs.append((xt_, bx, dn))
                        sos = [sop[0].next(), sop[1].next()]
                        for ee in range(ne):
                            for part in range(2):
                                so, bso = sos[part]
                                ps, bps = pp.next()
                                for ci, (xt_, bx, dn) in enumerate(xs):
                                    ft_, bft_, rn = fa[part][ci]
                                    pr.add("pe", MM(ps[:, :], ft_[:, :], xt_[:, ee, :], ci == 0, ci == len(xs) - 1), reads=[bft_, bx], writes=[bps])
                                if part == 0:
                                    pr.add("act", ACTF(so[0:B, ee, :], ps[0:B, :], AF.Copy), reads=[bps], writes=[bso])
                                else:
                                    pr.add("dve", CP(so[0:B, ee, :], ps[0:B, :]), reads=[bps], writes=[bso])
                        for part in range(2):
                            so, bso = sos[part]
                            pr.dma("act" if part == 0 else "pool", DV(S1F, (part * BP * NG + e0) * 512, [[NG * 512, BP], [512, ne], [1, 512]]), so[0:BP, 0:ne, :], reads=[bso], writes=[bS1F])
                    pr.flush()
                with contextlib.ExitStack() as s2:
                    FBK = 4
                    xin = [RPool(s2, nc, f"xinG{ci}", [128, 2, FBK, 512], BF16, 2) for ci in range(2)]
                    kop = [RPool(s2, nc, f"koG{p_}", [128, FBK, 512], BF16, 3) for p_ in range(2)]
                    for kp_ in kop:
                        for t_, b_ in kp_.tiles:
                            pr.add("dve", MSET(t_[:].rearrange("p a c -> p (a c)"), 0.0), writes=[b_])
                    pp = RPool(s2, nc, "ppG", [128, 512], F32, 8, psum=True)
                    for f0 in range(0, B, FBK):
                        nf = min(FBK, B - f0)
                        xs = []
                        for ci, (e0c, en) in enumerate(GCH):
                            xt_, bx = xin[ci].next()
                            for p_ in range(2):
                                pr.dma("sp", xt_[0:128, p_, 0:nf, :], DV(S1F, (p_ * BP * NG + f0 * NG + e0c) * 512, [[512, 128], [NG * 512, nf], [1, 512]]),
                                       reads=[bS1F], writes=[bx])
                            xs.append((xt_, bx, en))
                        for gc, (g0, gn) in enumerate(GCH):
                            kos = [kop[0].next(), kop[1].next()]
                            for ff in range(nf):
                                for part in range(2):
                                    ko, bko = kos[part]
                                    ps, bps = pp.next()
                                    terms = []
                                    for ci, (xt_, bx, en) in enumerate(xs):
                                        if part == 0:
                                            terms += [("FBc", ci, 0), ("FBs", ci, 1)]
                                        else:
                                            terms += [("FBc", ci, 1), ("FBsn", ci, 0)]
                                    for ti, (nm, ci, src) in enumerate(terms):
                                        tb, btb, rn = FB_[nm][ci]
                                        xt_, bx, en = xs[ci]
                                        pr.add("pe", MM(ps[0:gn, :], tb[0:en, g0:g0 + gn], xt_[0:en, src, ff, :], ti == 0, ti == len(terms) - 1),
                                               reads=[btb, bx], writes=[bps])
                                    if part == 0:
                                        pr.add("act", ACTF(ko[0:gn, ff, :], ps[0:gn, :], AF.Copy), reads=[bps], writes=[bko])
                                    else:
                                        pr.add("dve", CP(ko[0:gn, ff, :], ps[0:gn, :]), reads=[bps], writes=[bko])
                            for p_ in range(2):
                                ko, bko = kos[p_]
                                pr.dma("act" if p_ == 0 else "pool", DV(KH, (gc * 128 * 2 * B + p_ * B + f0) * 512, [[2 * B * 512, 128], [512, nf], [1, 512]]),
                                       ko[0:128, 0:nf, :], reads=[bko], writes=[bKH])
                    pr.flush()


def phase_C(k):
    nc, pr, W, sc, db = k.nc, k.pr, k.din, k.sc, k.db
    INB = k.din_buf
    Bm = k.FD
    S1, S2 = sc["S1"], sc["S2"]
    with contextlib.ExitStack() as st:
        def sb(name, shape, dt):
            return st.enter_context(nc.sbuf_tensor(U(name), shape, dt))
        Bt = {}
        for nm in ("Bc", "Bs", "Bsn"):
            t = sb(f"C{nm}", [128, NG], BF16)
            b = Buf(f"C{nm}")
            pr.dma("sp", t[:], W[nm][:, :], reads=[INB], writes=[b])
            Bt[nm] = (t, b)
        IB = {nm: load_chunked(pr, st, nc, f"C{nm}", W[nm], NG, 128, GCH, INB) for nm in ("IBc", "IBs", "IBsn", "IBcn")}
        JB3 = 8
        zc, bzc = sb("zeroC", [128, 1536], BF16), Buf("zeroC")
        pr.add("dve", MSET(zc[:], 0.0), writes=[bzc])
        Tt_ = k.Ttot
        for i_ in range(16):
            pr.dma("pool", rows2d(sc["UC"], 1536, Tt_ + 128 * i_, 128, 0, 1536), zc[:, :], reads=[bzc], writes=[db["UC"]])
            pr.dma("pool", rows2d(sc["Z1"], 512, Tt_ + 128 * i_, 128, 0, 512), zc[:, 0:512], reads=[bzc], writes=[db["Z1"]])
        bemin = min(u["Be"] for u in k.units)
        for p_ in range(2):
            for f_ in range(bemin, Bm):
                pr.dma("pool", rows2d(S2, 512, (p_ * Bm + f_) * 128, 128, 0, 512), zc[:, 0:512], reads=[bzc], writes=[db["S2"]])
        At = {}
        for ui, u in enumerate(k.units):
            Be = u["Be"]
            for nm in ("A_re", "A_im", "IA_re", "IA_im"):
                t = sb(f"C{nm}u{ui}", [128, 128], BF16)
                b = Buf(f"C{nm}u{ui}")
                pr.dma("sp", t[:, :], W[f"u{ui}_{nm}"][:, :], reads=[INB], writes=[b])
                At[(nm, ui)] = (t, b)
        pr.flush()
        for ui, u in enumerate(k.units):
            L, B, nb, toff, Be = u["L"], u["B"], u["nb"], u["toff"], u["Be"]
            BP = prow(Be)
            for n in range(2):
                if n == 0:
                    zsrc, bz, zrs, zco = sc["UC"], db["UC"], 1536, 1024
                    zdst, bzd = sc["Z1"], db["Z1"]
                else:
                    zsrc, bz, zrs, zco = sc["Z1"], db["Z1"], 512, 0
                    zdst, bzd = sc["Z2"], db["Z2"]
                KH, bKH = sc[f"KHAT{n}_{L}"], db[f"KHAT{n}_{L}"]
                with contextlib.ExitStack() as s2:
                    JB = 16
                    zbp = RPool(s2, nc, "zbC", [128, JB, 512], BF16, 3)
                    for ti_, (t_, b_) in enumerate(zbp.tiles):
                        pr.add("dve", MSET(t_[:].rearrange("p a c -> p (a c)"), 0.0), writes=[b_])
                    sop = [RPool(s2, nc, f"soC{p_}", [128, JB, 512], BF16, 2) for p_ in range(2)]
                    for sp_ in sop:
                        for t_, b_ in sp_.tiles:
                            pr.add("dve", MSET(t_[:].rearrange("p a c -> p (a c)"), 0.0), writes=[b_])
                    pp = RPool(s2, nc, "ppC", [128, 512], F32, 8, psum=True)
                    for j0 in range(0, 128, JB):
                        zb, bzb = zbp.next()
                        pr.dma("sp", zb[0:BP, :, :], DV(zsrc, (toff + j0) * zrs + zco, [[128 * zrs, BP], [zrs, JB], [1, 512]]), reads=[bz], writes=[bzb])
                        sos = [sop[0].next(), sop[1].next()]
                        for jj in range(JB):
                            for part, nm in enumerate(("A_re", "A_im")):
                                at, bat = At[(nm, ui)]
                                so, bso = sos[part]
                                ps, bps = pp.next()
                                pr.add("pe", MM(ps[:, :], at[:, :], zb[:, jj, :], True, True), reads=[bat, bzb], writes=[bps])
                                if part == 0:
                                    pr.add("act", ACTF(so[0:Be, jj, :], ps[0:Be, :], AF.Copy), reads=[bps], writes=[bso])
                                else:
                                    pr.add("dve", CP(so[0:Be, jj, :], ps[0:Be, :]), reads=[bps], writes=[bso])
                        for part in range(2):
                            so, bso = sos[part]
                            pr.dma("act" if part == 0 else "pool", DV(S1, (part * Bm * 128 + j0) * 512, [[128 * 512, BP], [512, JB], [1, 512]]), so[0:BP, :, :], reads=[bso], writes=[db["S1"]])
                    pr.flush()
                with contextlib.ExitStack() as s2:
                    FBK = 4
                    xin = RPool(s2, nc, "xinC2", [128, 2, FBK, 512], BF16, 3)
                    khp = [RPool(s2, nc, f"khC2{gc}", [128, 2, FBK, 512], BF16, 3) for gc in range(2)]
                    sop = RPool(s2, nc, "soC2", [128, 2, FBK, 512], BF16, 2)
                    abp = RPool(s2, nc, "abC2", [128, 2, 512], F32, 5)
                    t12p = RPool(s2, nc, "t13C2", [128, 2, 512], BF16, 5)
                    t34p = RPool(s2, nc, "t24C2", [128, 2, 512], BF16, 5)
                    yrp = RPool(s2, nc, "yrC2", [128, 512], BF16, 5)
                    yip = RPool(s2, nc, "yiC2", [128, 512], BF16, 5)
                    pab = RPool(s2, nc, "pabC2", [128, 512], F32, 4, psum=True)
                    por = RPool(s2, nc, "porC2", [128, 512], F32, 4, psum=True)
                    for sbi in range(nb):
                        for f0 in range(0, B, FBK):
                            nf = min(FBK, B - f0)
                            fe0 = sbi * B + f0
                            xi, bxi = xin.next()
                            for p_ in range(2):
                                pr.dma("sp", xi[:, p_, 0:nf, :], DV(S1, (p_ * Bm * 128 + fe0 * 128) * 512, [[512, 128], [128 * 512, nf], [1, 512]]),
                                       reads=[db["S1"]], writes=[bxi])
                            khs = []
                            for gc, (g0, gn) in enumerate(GCH):
                                kh, bkh = khp[gc].next()
                                for p_ in range(2):
                                    pr.dma("sp", kh[0:128, p_, 0:nf, :], DV(KH, (gc * 128 * 2 * B + p_ * B + f0) * 512, [[2 * B * 512, 128], [512, nf], [1, 512]]),
                                           reads=[bKH], writes=[bkh])
                                khs.append((kh, bkh))
                            so, bso = sop.next()
                            items = [(ff, gc) for ff in range(nf) for gc in range(2)]

                            def fwd(item):
                                ff, gc = item
                                g0, gn = GCH[gc]
                                pa, bpa = pab.next()
                                pb, bpb = pab.next()
                                pr.add("pe", MM(pa[0:gn, :], Bt["Bc"][0][:, g0:g0 + gn], xi[:, 0, ff, :], True, False), reads=[Bt["Bc"][1], bxi], writes=[bpa])
                                pr.add("pe", MM(pa[0:gn, :], Bt["Bs"][0][:, g0:g0 + gn], xi[:, 1, ff, :], False, True), reads=[Bt["Bs"][1], bxi], writes=[bpa])
                                pr.add("pe", MM(pb[0:gn, :], Bt["Bc"][0][:, g0:g0 + gn], xi[:, 1, ff, :], True, False), reads=[Bt["Bc"][1], bxi], writes=[bpb])
                                pr.add("pe", MM(pb[0:gn, :], Bt["Bsn"][0][:, g0:g0 + gn], xi[:, 0, ff, :], False, True), reads=[Bt["Bsn"][1], bxi], writes=[bpb])
                                ab, bab = abp.next()
                                pr.add("act", ACTF(ab[0:gn, 0, :], pa[0:gn, :], AF.Copy), reads=[bpa], writes=[bab])
                                pr.add("act", ACTF(ab[0:gn, 1, :], pb[0:gn, :], AF.Copy), reads=[bpb], writes=[bab])
                                kh, bkh = khs[gc]
                                t13, bt13 = t12p.next()
                                t24, bt24 = t34p.next()
                                khv = kh[0:gn, :, ff, :]
                                pr.add("dve", TT(t13[0:gn, :, :], ab[0:gn, 0:1, :].broadcast_to([gn, 2, 512]), khv, ALU.mult), reads=[bab, bkh], writes=[bt13])
                                pr.add("dve", TT(t24[0:gn, :, :], ab[0:gn, 1:2, :].broadcast_to([gn, 2, 512]), khv, ALU.mult), reads=[bab, bkh], writes=[bt24])
                                return (t13, bt13, t24, bt24)
                            state = {}

                            def inv(item, fw):
                                ff, gc = item
                                g0, gn = GCH[gc]
                                t13, bt13, t24, bt24 = fw
                                if gc == 0:
                                    state["re"] = por.next()
                                    state["im"] = por.next()
                                pre, bre = state["re"]
                                pim, bim = state["im"]
                                ic, ibc, _ = IB["IBc"][gc]
                                icn, ibcn, _ = IB["IBcn"][gc]
                                is_, ibs, _ = IB["IBs"][gc]
                                isn, ibsn, _ = IB["IBsn"][gc]
                                ap_, aq_, bp_, bq_ = t13[0:gn, 0, :], t13[0:gn, 1, :], t24[0:gn, 0, :], t24[0:gn, 1, :]
                                pr.add("pe", MM(pre[:], ic[0:gn, :], ap_, gc == 0, False), reads=[ibc, bt13], writes=[bre])
                                pr.add("pe", MM(pre[:], icn[0:gn, :], bq_, False, False), reads=[ibcn, bt24], writes=[bre])
                                pr.add("pe", MM(pre[:], isn[0:gn, :], aq_, False, False), reads=[ibsn, bt13], writes=[bre])
                                pr.add("pe", MM(pre[:], isn[0:gn, :], bp_, False, gc == 1), reads=[ibsn, bt24], writes=[bre])
                                pr.add("pe", MM(pim[:], is_[0:gn, :], ap_, gc == 0, False), reads=[ibs, bt13], writes=[bim])
                                pr.add("pe", MM(pim[:], isn[0:gn, :], bq_, False, False), reads=[ibsn, bt24], writes=[bim])
                                pr.add("pe", MM(pim[:], ic[0:gn, :], aq_, False, False), reads=[ibc, bt13], writes=[bim])
                                pr.add("pe", MM(pim[:], ic[0:gn, :], bp_, False, gc == 1), reads=[ibc, bt24], writes=[bim])
                                if gc == 1:
                                    pr.add("act", ACTF(so[:, 0, ff, :], pre[:], AF.Copy), reads=[bre], writes=[bso])
                                    pr.add("act", ACTF(so[:, 1, ff, :], pim[:], AF.Copy), reads=[bim], writes=[bso])
                            LA = 2
                            fq = [fwd(items[i_]) for i_ in range(min(LA, len(items)))]
                            for ii, item in enumerate(items):
                                if ii + LA < len(items):
                                    fq.append(fwd(items[ii + LA]))
                                inv(item, fq.pop(0))
                            for p_ in range(2):
                                pr.dma("act", DV(S2, (p_ * Bm * 128 + fe0 * 128) * 512, [[512, 128], [128 * 512, nf], [1, 512]]), so[:, p_, 0:nf, :],
                                       reads=[bso], writes=[db["S2"]])
                    pr.flush()
                with contextlib.ExitStack() as s2:
                    JB = JB3
                    xin = RPool(s2, nc, "xinC3", [128, 2, JB, 512], BF16, 2)
                    for ti_, (t_, b_) in enumerate(xin.tiles):
                        pr.add("dve", MSET(t_[:].rearrange("p a b c -> p (a b c)"), 0.0), writes=[b_])
                    xp3 = RPool(s2, nc, "xC3", [128, JB, 512], BF16, 3)
                    op3 = RPool(s2, nc, "oC3", [128, JB, 512], BF16, 3)
                    for ti_, (t_, b_) in enumerate(op3.tiles):
                        pr.add("dve", MSET(t_[:].rearrange("p a c -> p (a c)"), 0.0), writes=[b_])
                    pp = RPool(s2, nc, "ppC3", [128, 512], F32, 8, psum=True)
                    are, bare = At[("IA_re", ui)]
                    aim, baim = At[("IA_im", ui)]
                    for j0 in range(0, 128, JB):
                        xi, bxi = xin.next()
                        for p_ in range(2):
                            pr.dma("sp", xi[0:BP, p_, :, :], DV(S2, (p_ * Bm * 128 + j0) * 512, [[128 * 512, BP], [512, JB], [1, 512]]), reads=[db["S2"]], writes=[bxi])
                        xt_, bxt = xp3.next()
                        pr.dma("sp", xt_[0:BP, :, :], DV(sc["UC"], (toff + j0) * 1536 + n * 512, [[128 * 1536, BP], [1536, JB], [1, 512]]), reads=[db["UC"]], writes=[bxt])
                        o, bo = op3.next()
                        for jj in range(JB):
                            ps, bps = pp.next()
                            pr.add("pe", MM(ps[:, :], are[:, :], xi[:, 0, jj, :], True, False), reads=[bare, bxi], writes=[bps])
                            pr.add("pe", MM(ps[:, :], aim[:, :], xi[:, 1, jj, :], False, True), reads=[baim, bxi], writes=[bps])
                            pr.add("dve", TT(o[0:Be, jj, :], xt_[0:Be, jj, :], ps[0:Be, :], ALU.mult), reads=[bxt, bps], writes=[bo])
                        pr.dma("pool", DV(zdst, (toff + j0) * 512, [[128 * 512, BP], [512, JB], [1, 512]]), o[0:BP, :, :], reads=[bo], writes=[bzd])
                    pr.flush()


NCORES = 8
_CACHE = {}


def kernel(**inputs):
    xp = np.ascontiguousarray(np.asarray(inputs["x_prompt"], dtype=np.float32))
    xs = np.ascontiguousarray(np.asarray(inputs["x_sample"], dtype=np.float32))
    nbp, Lxp = xp.shape[0], xp.shape[1]
    nbs, Lxs = xs.shape[0], xs.shape[1]
    assert nbp % NCORES == 0 and nbs % NCORES == 0
    npp, nps = nbp // NCORES, nbs // NCORES
    key = (npp, Lxp, nps, Lxs)
    if key not in _CACHE:
        specs = [("p", i, Lxp) for i in range(npp)] + [("s", i, Lxs) for i in range(nps)]
        _CACHE[key] = build_program(specs, npp, Lxp, nps, Lxs)
    nc, k = _CACHE[key]
    shared = {}
    for nme in ("meta_tokens", "pre_mix_g", "w_in", "q_norm_g", "w_uq", "kv_norm_g", "w_ukv", "conv_w", "f_w1", "f_b1",
                "f_freq", "f_w2", "f_b2", "f_w3", "f_decay", "hy_bias", "attn_out_g", "hy_out_g", "w_o", "post_mix_g",
                "pre_mlp_g", "w_ff1", "w_ff2", "post_mlp_g"):
        shared[nme] = np.ascontiguousarray(np.asarray(inputs[nme], dtype=np.float32))
    for nme, arr in k.host.items():
        shared[nme] = arr
    in_maps = []
    for c in range(NCORES):
        m = dict(shared)
        m["xp"] = xp[c * npp:(c + 1) * npp]
        m["xs"] = xs[c * nps:(c + 1) * nps]
        in_maps.append(m)
    res = run_bass_kernel_spmd(nc, in_maps, core_ids=list(range(NCORES)))
    yp = np.concatenate([np.asarray(r["yp"], dtype=np.float32) for r in res.results], axis=0)
    ys = np.concatenate([np.asarray(r["ys"], dtype=np.float32) for r in res.results], axis=0)
    return (yp, ys)
```

```python
import contextlib
import numpy as np
import ml_dtypes
import concourse.bass as bass
import concourse.mybir as mybir
from concourse.bass_utils import run_bass_kernel_spmd

F32 = mybir.dt.float32
BF16 = mybir.dt.bfloat16
AF = mybir.ActivationFunctionType
ALU = mybir.AluOpType
AX = mybir.AxisListType
NPBF = ml_dtypes.bfloat16


class Buf:
    __slots__ = ("name", "multi", "w", "r")

    def __init__(self, name, multi=False):
        self.name = name
        self.multi = multi
        self.w = {}
        self.r = {}


class Op:
    __slots__ = ("eng", "fn", "deps", "dma", "chan", "sem", "val", "signal", "selfwait")

    def __init__(self, eng, fn, dma, chan):
        self.eng = eng
        self.fn = fn
        self.dma = dma
        self.chan = chan
        self.deps = []
        self.sem = None
        self.val = 0
        self.signal = dma
        self.selfwait = 0


class Prog:
    ENGS = ("pe", "act", "dve", "pool", "sp")
    NROT = {"sp": 14, "act": 10, "pool": 12}

    def __init__(self, nc):
        self.nc = nc
        self.ops = []
        self.esem = {e: nc.alloc_semaphore(name=f"s_{e}") for e in ("pe", "act", "dve", "pool")}
        self.ecnt = {e: 0 for e in self.esem}
        self.dsem = {}
        self.dval = {}
        for q, n in self.NROT.items():
            for k in range(n):
                s = nc.alloc_semaphore(name=f"d_{q}{k}")
                self.dsem[(q, k)] = s
                self.dval[(q, k)] = 0
        self.dcnt = {q: 0 for q in self.NROT}
        self.seen = {e: {} for e in self.ENGS}

    def add(self, eng, fn, reads=(), writes=(), dma=False):
        if dma:
            k = self.dcnt[eng] % self.NROT[eng]
            self.dcnt[eng] += 1
            chan = (eng, k)
        else:
            chan = eng
        op = Op(eng, fn, dma, chan)
        deps = {}

        def need_raw(p):
            return not (p.eng == "pe" and eng == "pe" and not dma and not p.dma)

        def need_w(p):
            return p.dma or dma or p.eng != eng

        for b in reads:
            for p in b.w.values():
                if need_raw(p):
                    deps[id(p)] = p
        for b in writes:
            for p in b.w.values():
                if need_w(p):
                    deps[id(p)] = p
            for p in b.r.values():
                if need_w(p) and p is not op:
                    deps[id(p)] = p
        for b in reads:
            b.r[chan] = op
        for b in writes:
            if b.multi:
                b.w[chan] = op
            else:
                b.w = {chan: op}
                b.r = {}
        op.deps = list(deps.values())
        for p in op.deps:
            p.signal = True
        self.ops.append(op)
        return op

    def dma(self, q, out, in_, reads=(), writes=()):
        return self.add(q, lambda e: e.dma_start(out=out, in_=in_), reads=reads, writes=writes, dma=True)

    def _eng(self, e):
        nc = self.nc
        return {"pe": nc.tensor, "act": nc.scalar, "dve": nc.vector, "pool": nc.gpsimd, "sp": nc.sync}[e]

    def flush(self, final=False):
        nc = self.nc
        ops = self.ops
        self.ops = []
        last = {}
        for op in ops:
            if not op.dma:
                last[op.eng] = op
        for op in last.values():
            op.signal = True
        for op in ops:
            if op.dma:
                op.sem = self.dsem[op.chan]
                op.selfwait = self.dval[op.chan]
                self.dval[op.chan] += 16
                op.val = self.dval[op.chan]
            elif op.signal:
                op.sem = self.esem[op.eng]
                self.ecnt[op.eng] += 1
                op.val = self.ecnt[op.eng]
        streams = {e: [op for op in ops if op.eng == e] for e in self.ENGS}
        bar = [(self.esem[e], self.ecnt[e], e) for e in self.esem] + \
              [(self.dsem[c], self.dval[c], c) for c in self.dsem]

        def run(ename, eng):
            seen = self.seen[ename]
            for op in streams[ename]:
                for p in op.deps:
                    if seen.get(p.chan, 0) < p.val:
                        eng.wait_ge(p.sem, p.val)
                        seen[p.chan] = p.val
                if op.dma and op.selfwait > 0 and seen.get(op.chan, 0) < op.selfwait:
                    eng.wait_ge(op.sem, op.selfwait)
                    seen[op.chan] = op.selfwait
                inst = op.fn(eng)
                if op.signal:
                    inst.then_inc(op.sem, 16 if op.dma else 1)
            for sem, val, chan in bar:
                if val > 0 and seen.get(chan, 0) < val:
                    eng.wait_ge(sem, val)
                    seen[chan] = val

        with nc.Block() as block:
            @block.tensor
            def _(e):
                run("pe", e)

            @block.scalar
            def _(e):
                run("act", e)

            @block.vector
            def _(e):
                run("dve", e)

            @block.gpsimd
            def _(e):
                run("pool", e)

            @block.sync
            def _(e):
                run("sp", e)


D = 1024
NMETA = 16
QL = 256
KVL = 128
ROPE = 64
NH = 4
HC = 512
INC = 1984
DFF = 4096
EPS = 1e-6
NG = 255
SCALE = float((128 + 64) ** -0.5)
PI = float(np.pi)


def _split(n, m=128):
    k = -(-n // m)
    base, rem = divmod(n, k)
    out, s = [], 0
    for i in range(k):
        c = base + (1 if i < rem else 0)
        out.append((s, c))
        s += c
    return out


def prow(n):
    m = n
    while m <= 128:
        if any(m % d == 0 for d in (16, 15, 14, 13, 12, 11)):
            return m
        m += 1
    return n


def _bf(a):
    return np.ascontiguousarray(np.asarray(a, dtype=np.float64).astype(np.float32).astype(NPBF))


def seq_tables(L):
    B = (L - NMETA) // 128 + 1
    PP = 2 * B - 1
    T = 128 * B
    t = {}
    inv = 1.0 / (10000.0 ** (np.arange(0, ROPE, 2, dtype=np.float64) / ROPE))
    ang = np.arange(L, dtype=np.float64)[:, None] * inv[None, :]
    ang = np.concatenate([ang, ang], -1)
    t["cosT"] = np.ascontiguousarray(np.cos(ang).T.astype(np.float32))
    t["sinT"] = np.ascontiguousarray(np.sin(ang).T.astype(np.float32))
    J = np.arange(B, dtype=np.float64)
    f = np.arange(B, dtype=np.float64)
    th = 2 * np.pi * np.outer(J, f) / PP
    t["A_re"] = _bf(np.cos(th))
    t["A_im"] = _bf(-np.sin(th))
    wf = np.where(f == 0, 1.0, 2.0)[:, None]
    thi = 2 * np.pi * np.outer(f, J) / PP
    t["IA_re"] = _bf(wf * np.cos(thi) / PP)
    t["IA_im"] = _bf(-wf * np.sin(thi) / PP)
    d = np.arange(-(B - 1), B, dtype=np.float64)
    thd = 2 * np.pi * np.outer(d, f) / PP
    t["FA_re"] = _bf(np.cos(thd))
    t["FA_im"] = _bf(-np.sin(thd))
    r = np.arange(2 * T, dtype=np.float64)
    tt = np.abs(r - T)
    tv = (tt / (L - 1)).astype(np.float32).astype(np.float64)
    w = 2.0 * np.pi * tt / L
    fb = np.linspace(1e-4, 15.0, 16)
    feats = np.concatenate([tv[:, None], np.cos(w[:, None] * fb[None]), -np.sin(w[:, None] * fb[None])], -1)
    t["featsT"] = np.ascontiguousarray(feats.T.astype(np.float32))
    t["tneg"] = np.ascontiguousarray((-tv).astype(np.float32)[:, None])
    return t


def common_tables():
    t = {}
    j = np.arange(128, dtype=np.float64)
    g = np.arange(NG, dtype=np.float64)
    th = 2 * np.pi * np.outer(j, g) / NG
    t["Bc"], t["Bs"], t["Bsn"] = _bf(np.cos(th)), _bf(np.sin(th)), _bf(-np.sin(th))
    e = np.arange(-127, 128, dtype=np.float64)
    the = 2 * np.pi * np.outer(e, g) / NG
    t["FBc"], t["FBs"], t["FBsn"] = _bf(np.cos(the)), _bf(np.sin(the)), _bf(-np.sin(the))
    thi = 2 * np.pi * np.outer(g, j) / NG
    t["IBc"], t["IBs"], t["IBsn"] = _bf(np.cos(thi) / NG), _bf(np.sin(thi) / NG), _bf(-np.sin(thi) / NG)
    t["IBcn"] = _bf(-np.cos(thi) / NG)
    t["ident"] = np.eye(128).astype(NPBF)
    t["ones"] = np.ones((128, 128)).astype(NPBF)
    t["zeros"] = np.zeros((128, 1536), NPBF)
    return t


_uid = [0]


def U(name):
    _uid[0] += 1
    return f"{name}_{_uid[0]}"


class RPool:
    def __init__(self, stack, nc, name, shape, dtype, n, psum=False):
        self.tiles = []
        for i in range(n):
            if psum:
                t = stack.enter_context(nc.psum_tensor(U(f"{name}{i}"), shape, dtype))
            else:
                t = stack.enter_context(nc.sbuf_tensor(U(f"{name}{i}"), shape, dtype))
            self.tiles.append((t, Buf(f"{name}{i}")))
        self.i = 0

    def next(self):
        t = self.tiles[self.i % len(self.tiles)]
        self.i += 1
        return t


def MM(ps, lhsT, rhs, start, stop):
    return lambda e: e.matmul(ps, lhsT=lhsT, rhs=rhs, start=start, stop=stop)


def TR(out, in_, ident):
    return lambda e: e.transpose(out=out, in_=in_, identity=ident)


def ACTF(out, in_, func, **kw):
    return lambda e: e.activation(out=out, in_=in_, func=func, **kw)


def TT(out, a, b, op):
    return lambda e: e.tensor_tensor(out=out, in0=a, in1=b, op=op)


def TS(out, in0, s1, s2=None, op0=ALU.mult, op1=None):
    if op1 is None:
        return lambda e: e.tensor_scalar(out=out, in0=in0, scalar1=s1, scalar2=None, op0=op0)
    return lambda e: e.tensor_scalar(out=out, in0=in0, scalar1=s1, scalar2=s2, op0=op0, op1=op1)


def STT(out, in0, scalar, in1, op0, op1):
    return lambda e: e.scalar_tensor_tensor(out=out, in0=in0, scalar=scalar, in1=in1, op0=op0, op1=op1)


def CP(out, in_):
    return lambda e: e.tensor_copy(out=out, in_=in_)


def MSET(ap, v):
    return lambda e: e.memset(ap, v)


def RSUM(out, in_):
    return lambda e: e.reduce_sum(out=out, in_=in_, axis=AX.X)


def RECIP(out, in_):
    return lambda e: e.reciprocal(out=out, in_=in_)


def zero_init(pr, tile3, buf, writer, zsrc, zbuf):
    a_, c_ = tile3.shape[1], tile3.shape[2]
    if writer == "dve":
        pr.add("act", ACTF(tile3[:], zsrc[:, 0:c_].unsqueeze(1).broadcast_to([128, a_, c_]), AF.Copy), reads=[zbuf], writes=[buf])
    else:
        pr.add("dve", MSET(tile3[:].rearrange("p a c -> p (a c)"), 0.0), writes=[buf])


class SeqInfo:
    def __init__(self, kind, idx, Lx, toff):
        self.kind, self.idx, self.Lx = kind, idx, Lx
        self.L = Lx + NMETA
        self.n = Lx // 128
        self.B = self.n + 1
        self.T = 128 * self.B
        self.toff = toff
        self.groups = [(0, NMETA, [(0, NMETA, None)])]
        i = 0
        while i < self.n:
            k = min(4, self.n - i)
            self.groups.append((NMETA + 128 * i, 128 * k,
                                [(NMETA + 128 * (i + j), 128, 128 * (i + j)) for j in range(k)]))
            i += k


def DV(ap, offset, dims):
    return bass.AP(ap.tensor, int(offset), [list(map(int, d)) for d in dims])


def rows2d(ap, ncols, r0, nr, c0, ncl):
    return DV(ap, r0 * ncols + c0, [[ncols, nr], [1, ncl]])


class K:
    pass


def build_program(seq_specs, n_xp, Lxp, n_xs, Lxs, dbg=False, stop_after=None):
    nc = bass.Bass("TRN2", target_bir_lowering=False)
    pr = Prog(nc)
    k = K()
    k.nc, k.pr, k.dbg = nc, pr, dbg

    def din(name, shape, dt=F32):
        return nc.dram_tensor(name, list(shape), dt, kind="ExternalInput").ap()

    def dscr(name, shape, dt):
        return nc.dram_tensor(name, list(shape), dt, kind=("ExternalOutput" if dbg else "Internal")).ap()

    k.din = {}
    k.din["xp"] = din("xp", [max(n_xp, 1), Lxp, D])
    k.din["xs"] = din("xs", [max(n_xs, 1), Lxs, D])
    k.yp = nc.dram_tensor("yp", [max(n_xp, 1), Lxp, D], F32, kind="ExternalOutput").ap()
    k.ys = nc.dram_tensor("ys", [max(n_xs, 1), Lxs, D], F32, kind="ExternalOutput").ap()
    wshapes = dict(meta_tokens=[16, D], pre_mix_g=[1, D], w_in=[1, D, INC], q_norm_g=[1, QL], w_uq=[1, QL, 768],
                   kv_norm_g=[1, KVL], w_ukv=[1, KVL, 1024], conv_w=[1, 3, 1536], f_w1=[1, 33, 64], f_b1=[1, 64],
                   f_freq=[1, 64], f_w2=[1, 64, 64], f_b2=[1, 64], f_w3=[1, 64, 2048], f_decay=[1, 2048],
                   hy_bias=[1, 2, 512], attn_out_g=[1, 512], hy_out_g=[1, 512], w_o=[1, D, D], post_mix_g=[1, D],
                   pre_mlp_g=[1, D], w_ff1=[1, D, DFF], w_ff2=[1, DFF, D], post_mlp_g=[1, D])
    for nme, shp in wshapes.items():
        k.din[nme] = din(nme, shp)
    ct = common_tables()
    k.host = dict(ct)
    for nme, arr in ct.items():
        k.din[nme] = din(nme, arr.shape, BF16 if arr.dtype == NPBF else F32)
    seqs, toff = [], 0
    for kind, idx, Lx in seq_specs:
        s = SeqInfo(kind, idx, Lx, toff)
        toff += s.T
        seqs.append(s)
    k.seqs, k.Ttot = seqs, toff
    k.Ls = sorted({s.L for s in seqs})
    k.tab = {}
    for L in k.Ls:
        tb = seq_tables(L)
        for nme, arr in tb.items():
            key = f"{nme}_{L}"
            k.host[key] = arr
            k.din[key] = din(key, arr.shape, BF16 if arr.dtype == NPBF else F32)
    Tt = k.Ttot
    Bm = max(s.B for s in seqs)
    units = []
    for s_ in seqs:
        if units and units[-1]["L"] == s_.L and (units[-1]["nb"] + 1) * s_.B <= 128:
            units[-1]["nb"] += 1
        else:
            units.append(dict(L=s_.L, B=s_.B, nb=1, toff=s_.toff))
    for ui, u in enumerate(units):
        u["Be"] = u["B"] * u["nb"]
        tb = seq_tables(u["L"])
        for nm in ("A_re", "A_im", "IA_re", "IA_im"):
            arr0 = np.kron(np.eye(u["nb"], dtype=np.float32), tb[nm].astype(np.float32))
            arr = np.zeros((128, 128), np.float32)
            arr[:arr0.shape[0], :arr0.shape[1]] = arr0
            arr = arr.astype(NPBF)
            key = f"u{ui}_{nm}"
            k.host[key] = np.ascontiguousarray(arr)
            k.din[key] = din(key, arr.shape, BF16)
    k.units = units
    sc = {}
    sc["QT"] = dscr("QT", [NH, 192, Tt], BF16)
    sc["KN"] = dscr("KN", [NH, 128, Tt], BF16)
    sc["KR"] = dscr("KR", [64, Tt], BF16)
    sc["V"] = dscr("V", [Tt, 512], BF16)
    PADR = 128 * 16
    k.FD = max(prow(u["Be"]) for u in units)
    sc["UC"] = dscr("UC", [Tt + PADR, 1536], BF16)
    sc["UR"] = dscr("UR", [Tt + PADR, 1536], BF16)
    sc["Z1"] = dscr("Z1", [Tt + PADR, 512], BF16)
    sc["Z2"] = dscr("Z2", [Tt + PADR, 512], BF16)
    sc["Y"] = dscr("Y", [Tt + PADR, 512], F32)
    sc["S1"] = dscr("S1", [2, k.FD, 128, 512], BF16)
    sc["S2"] = dscr("S2", [2, k.FD, 128, 512], BF16)
    sc["AN"] = dscr("AN", [512, Tt], BF16)
    sc["H1"] = dscr("H1", [Tt, D], F32)
    for L in k.Ls:
        B = (L - NMETA) // 128 + 1
        for n in range(2):
            sc[f"KTIME{n}_{L}"] = dscr(f"KTIME{n}_{L}", [256 * B, 512], BF16)
            sc[f"KHAT{n}_{L}"] = dscr(f"KHAT{n}_{L}", [2, 128, 2, B, 512], BF16)
        sc[f"S1F_{L}"] = dscr(f"S1F_{L}", [2 * prow(B) * NG + 16, 512], BF16)
    k.sc = sc
    k.db = {nme: Buf(nme, multi=True) for nme in sc}
    k.dout = Buf("yout", multi=True)
    k.din_buf = Buf("inputs", multi=True)

    phases = [phase_F, phase_A, phase_B, phase_C, phase_D1, phase_D2]
    for ph in phases:
        ph(k)
        if stop_after == ph.__name__:
            break
    return nc, k


def xrows(k, s, xrow, nr):
    ap = k.din["xp"] if s.kind == "p" else k.din["xs"]
    Lx = s.Lx
    return DV(ap, (s.idx * Lx + xrow) * D, [[D, nr], [1, D]])


def yrows(k, s, xrow, nr):
    ap = k.yp if s.kind == "p" else k.ys
    return DV(ap, (s.idx * s.Lx + xrow) * D, [[D, nr], [1, D]])


def col1(ap, off, n=128):
    return DV(ap, off, [[1, n], [1, 1]])


def rstd_act(pr, small, junk_ap, junk_buf, src_ap, src_bufs, n_feat):
    st, sb_ = small.next()
    pr.add("act", lambda e: e.activation(out=junk_ap, in_=src_ap, func=AF.Square, accum_out=st[:, 0:1]), reads=list(src_bufs), writes=[junk_buf, sb_])
    pr.add("act", ACTF(st[:, 1:2], st[:, 0:1], AF.Ln, scale=1.0 / n_feat, bias=EPS), reads=[sb_], writes=[sb_])
    pr.add("act", ACTF(st[:, 2:3], st[:, 1:2], AF.Exp, scale=-0.5), reads=[sb_], writes=[sb_])
    return st[:, 2:3], sb_


def rstd_chain(pr, small, src_sq_ap, src_buf, n_feat, rows=128):
    st, sb_ = small.next()
    pr.add("dve", RSUM(st[0:rows, 0:1], src_sq_ap), reads=[src_buf], writes=[sb_])
    pr.add("act", ACTF(st[0:rows, 1:2], st[0:rows, 0:1], AF.Ln, scale=1.0 / n_feat, bias=EPS), reads=[sb_], writes=[sb_])
    pr.add("act", ACTF(st[0:rows, 2:3], st[0:rows, 1:2], AF.Exp, scale=-0.5), reads=[sb_], writes=[sb_])
    return st[0:rows, 2:3], sb_


def phase_A(k):
    nc, pr, W, sc, db = k.nc, k.pr, k.din, k.sc, k.db
    INB = k.din_buf
    Tt = k.Ttot
    with contextlib.ExitStack() as st:
        def sb(name, shape, dt):
            return st.enter_context(nc.sbuf_tensor(U(name), shape, dt))
        WinA, bWinA = sb("WinA", [128, 8, 512], BF16), Buf("WinA")
        Wu, bWu = sb("Wu", [128, 8, 1536], BF16), Buf("Wu")
        cwb, bcwb = sb("cwb", [128, 3, 1536], BF16), Buf("cwb")
        Wuq, bWuq = sb("Wuq", [128, 2, 768], BF16), Buf("Wuq")
        WuqR, bWuqR = sb("WuqR", [128, 2, 256], BF16), Buf("WuqR")
        Wukv, bWukv = sb("Wukv", [128, 1024], BF16), Buf("Wukv")
        WV, bWV = sb("WV", [128, 512], BF16), Buf("WV")
        gv, bgv = sb("gv", [128, 12], F32), Buf("gv")
        ident, bid = sb("identA", [128, 128], BF16), Buf("ident")
        ones, bon = sb("onesA", [128, 128], BF16), Buf("ones")
        zt, bzt = sb("zerosA", [128, 1536], BF16), Buf("zeros")
        pr.dma("sp", ident[:], W["ident"][:, :], reads=[INB], writes=[bid])
        pr.dma("sp", ones[:], W["ones"][:, :], reads=[INB], writes=[bon])
        pr.dma("sp", zt[:], W["zeros"][:, :], reads=[INB], writes=[bzt])
        for kc in range(8):
            pr.dma("sp", gv[:, kc:kc + 1], col1(W["pre_mix_g"], kc * 128), reads=[INB], writes=[bgv])
        for kc in range(2):
            pr.dma("sp", gv[:, 8 + kc:9 + kc], col1(W["q_norm_g"], kc * 128), reads=[INB], writes=[bgv])
        pr.dma("sp", gv[:, 10:11], col1(W["kv_norm_g"], 0), reads=[INB], writes=[bgv])
        with contextlib.ExitStack() as st2:
            cwB = st2.enter_context(nc.sbuf_tensor("cwB", [128, 3, 1536], F32))
            bcw = Buf("cwB")
            stg = RPool(st2, nc, "stgA", [128, INC], F32, 2)
            for i in range(3):
                pr.dma("sp", cwB[:, i, :], DV(W["conv_w"], i * 1536, [[0, 128], [1, 1536]]), reads=[INB], writes=[bcw])
            for kc in range(8):
                sg, bsg = stg.next()
                g1 = gv[:, kc:kc + 1]
                pr.dma("sp", sg[:], rows2d(W["w_in"], INC, kc * 128, 128, 0, INC), reads=[INB], writes=[bsg])
                pr.add("dve", TS(WinA[:, kc, 0:448], sg[:, 0:448], g1), reads=[bsg, bgv], writes=[bWinA])
                pr.add("dve", TS(WinA[:, kc, 448:480], sg[:, 416:448], g1, -1.0, ALU.mult, ALU.mult), reads=[bsg, bgv], writes=[bWinA])
                pr.add("dve", TS(WinA[:, kc, 480:512], sg[:, 384:416], g1), reads=[bsg, bgv], writes=[bWinA])
                pr.add("dve", TS(Wu[:, kc, :], sg[:, 448:INC], g1), reads=[bsg, bgv], writes=[bWu])
                if kc == 0:
                    pr.add("dve", CP(cwb[:].rearrange("p a c -> p (a c)"), cwB[:].rearrange("p a c -> p (a c)")), reads=[bcw], writes=[bcwb])
            for kc in range(2):
                sg, bsg = stg.next()
                g1 = gv[:, 8 + kc:9 + kc]
                pr.dma("sp", sg[:, 0:768], rows2d(W["w_uq"], 768, kc * 128, 128, 0, 768), reads=[INB], writes=[bsg])
                pr.add("dve", TS(Wuq[:, kc, :], sg[:, 0:768], g1), reads=[bsg, bgv], writes=[bWuq])
                for h in range(NH):
                    b0 = h * 192 + 128
                    pr.add("dve", TS(WuqR[:, kc, h * 64:h * 64 + 32], sg[:, b0 + 32:b0 + 64], g1, -1.0, ALU.mult, ALU.mult),
                           reads=[bsg, bgv], writes=[bWuqR])
                    pr.add("dve", TS(WuqR[:, kc, h * 64 + 32:h * 64 + 64], sg[:, b0:b0 + 32], g1), reads=[bsg, bgv], writes=[bWuqR])
            sg, bsg = stg.next()
            pr.dma("sp", sg[:, 0:1024], rows2d(W["w_ukv"], 1024, 0, 128, 0, 1024), reads=[INB], writes=[bsg])
            pr.add("dve", TS(Wukv[:], sg[:, 0:1024], gv[:, 10:11]), reads=[bsg, bgv], writes=[bWukv])
            for h in range(NH):
                pr.add("dve", TS(WV[:, h * 128:(h + 1) * 128], sg[:, h * 256 + 128:h * 256 + 256], gv[:, 10:11]),
                       reads=[bsg, bgv], writes=[bWV])
            pr.flush()
        xt = RPool(st, nc, "xtA", [128, D], F32, 3)
        sqp = RPool(st, nc, "sqA", [128, D], BF16, 2)
        small = RPool(st, nc, "smA", [128, 4], F32, 10)
        abf = RPool(st, nc, "abfA", [128, D], BF16, 2)
        aTp = RPool(st, nc, "aTA", [128, 8, 514], BF16, 3)
        tpp = RPool(st, nc, "tpA", [128, 1024], BF16, 2, psum=True)
        pm = RPool(st, nc, "pmA", [128, 512], F32, 6, psum=True)
        cqf = RPool(st, nc, "cqfA", [128, 2, 512], F32, 2)
        cqsq = RPool(st, nc, "cqsqA", [128, 2, 512], BF16, 1)
        rsp = RPool(st, nc, "rsA", [128, 512], F32, 3)
        cqn = RPool(st, nc, "cqnA", [128, 2, 512], BF16, 2)
        csp = RPool(st, nc, "csA", [64, 2, 512], F32, 2)
        rtp = RPool(st, nc, "rtA", [64, 512], F32, 4)
        oq = RPool(st, nc, "oqA", [128, 512], BF16, 4)
        orp = RPool(st, nc, "orA", [64, 512], BF16, 3)
        uev = RPool(st, nc, "uevA", [128, 1536], BF16, 2)

        ldp = RPool(st, nc, "ldA", [128, 1536], BF16, 6)
        cop = RPool(st, nc, "coA", [128, 1536], F32, 2)
        ctp = RPool(st, nc, "ctA", [128, 1536], F32, 2)
        cbp = RPool(st, nc, "cbA", [128, 1536], BF16, 2)

        def u_group(s, grp, aT_t, baT):
            t0, ntok, subs = grp
            for si, (t, nr, xr) in enumerate(subs):
                ue, bue = uev.next()
                for cg in range(3):
                    ps, bps = pm.next()
                    for kc in range(8):
                        pr.add("pe", MM(ps[0:nr, :], aT_t[:, kc, 1 + si * 128:1 + si * 128 + nr],
                                        Wu[:, kc, cg * 512:(cg + 1) * 512], kc == 0, kc == 7), reads=[baT, bWu], writes=[bps])
                    pr.add("act", ACTF(ue[0:nr, cg * 512:(cg + 1) * 512], ps[0:nr, :], AF.Copy), reads=[bps], writes=[bue])
                pr.dma("act", rows2d(sc["UR"], 1536, s.toff + t + 1, nr, 0, 1536), ue[0:nr, :], reads=[bue], writes=[db["UR"]])

        def conv_group(s, grp, aT_t=None, baT=None):
            t0, ntok, subs = grp
            for si, (t, nr, xr) in enumerate(subs):
                lds = []
                for i in range(3):
                    ld, bld = ldp.next()
                    pr.dma("sp", ld[0:nr, :], rows2d(sc["UR"], 1536, s.toff + t + i, nr, 0, 1536), reads=[db["UR"]], writes=[bld])
                    lds.append((ld, bld))
                co, bco = cop.next()
                ct, bct = ctp.next()
                cb, bcb = cbp.next()
                pr.add("dve", TT(co[0:nr, :], lds[0][0][0:nr, :], cwb[0:nr, 0, :], ALU.mult), reads=[lds[0][1], bcwb], writes=[bco])
                pr.add("dve", TT(ct[0:nr, :], lds[1][0][0:nr, :], cwb[0:nr, 1, :], ALU.mult), reads=[lds[1][1], bcwb], writes=[bct])
                pr.add("dve", TT(co[0:nr, :], co[0:nr, :], ct[0:nr, :], ALU.add), reads=[bco, bct], writes=[bco])
                pr.add("dve", TT(ct[0:nr, :], lds[2][0][0:nr, :], cwb[0:nr, 2, :], ALU.mult), reads=[lds[2][1], bcwb], writes=[bct])
                pr.add("dve", TT(cb[0:nr, :], co[0:nr, :], ct[0:nr, :], ALU.add), reads=[bco, bct], writes=[bcb])
                pr.dma("pool", rows2d(sc["UC"], 1536, s.toff + t, nr, 0, 1536), cb[0:nr, :], reads=[bcb], writes=[db["UC"]])

        def rope_out(ps1, b1, ps2, b2, cs, bcs, ntok, dst):
            t1, bt1 = rtp.next()
            t2, bt2 = rtp.next()
            pr.add("dve", TT(t1[:, :ntok], ps1[0:64, :ntok], cs[:, 0, :ntok], ALU.mult), reads=[b1, bcs], writes=[bt1])
            pr.add("dve", TT(t2[:, :ntok], ps2[0:64, :ntok], cs[:, 1, :ntok], ALU.mult), reads=[b2, bcs], writes=[bt2])
            o, bo = orp.next()
            pr.add("pool", TT(o[:, :ntok], t1[:, :ntok], t2[:, :ntok], ALU.add), reads=[bt1, bt2], writes=[bo])
            pr.dma("pool", dst, o[:, :ntok], reads=[bo], writes=[db["QT"], db["KR"]])

        def fm_norm(ps_list, nch, ntok, n_feat):
            cf, bcf = cqf.next()
            cs_, bcs_ = cqsq.next()
            for c, (ps, bps) in enumerate(ps_list):
                pr.add("act", ACTF(cf[:, c, :ntok], ps[:, :ntok], AF.Copy), reads=[bps], writes=[bcf])
                pr.add("act", ACTF(cs_[:, c, :ntok], ps[:, :ntok], AF.Square), reads=[bps], writes=[bcs_])
            pss, bpss = pm.next()
            for c in range(nch):
                pr.add("pe", MM(pss[:, :ntok], ones[:], cs_[:, c, :ntok], c == 0, c == nch - 1), reads=[bon, bcs_], writes=[bpss])
            l1, bl1 = rsp.next()
            pr.add("act", ACTF(l1[:, :ntok], pss[:, :ntok], AF.Ln, scale=1.0 / n_feat, bias=EPS), reads=[bpss], writes=[bl1])
            r1, br1 = rsp.next()
            pr.add("act", ACTF(r1[:, :ntok], l1[:, :ntok], AF.Exp, scale=-0.5), reads=[bl1], writes=[br1])
            cn, bcn = cqn.next()
            for c in range(nch):
                pr.add("dve", TT(cn[:, c, :ntok], cf[:, c, :ntok], r1[:, :ntok], ALU.mult), reads=[bcf, br1], writes=[bcn])
            return cn, bcn

        for s in k.seqs:
            L = s.L
            cosT, sinT = W[f"cosT_{L}"], W[f"sinT_{L}"]
            pr.dma("act", rows2d(sc["UC"], 1536, s.toff + L, s.T - L, 0, 1536), zt[0:s.T - L, :], reads=[bzt], writes=[db["UC"]])
            pr.dma("act", rows2d(sc["UR"], 1536, s.toff, 1, 0, 1536), zt[0:1, :], reads=[bzt], writes=[db["UR"]])
            pr.dma("act", rows2d(sc["UR"], 1536, s.toff + L + 1, 1, 0, 1536), zt[0:1, :], reads=[bzt], writes=[db["UR"]])
            ng = len(s.groups)

            def SX(gi):
                t0, ntok, subs = s.groups[gi]
                aT_t, baT = aTp.next()
                for si, (t, nr, xr) in enumerate(subs):
                    x_t, bx = xt.next()
                    if xr is None:
                        pr.add("pool", MSET(x_t[:], 0.0), writes=[bx])
                        pr.dma("sp", x_t[0:NMETA, :], W["meta_tokens"][:, :], reads=[INB], writes=[bx])
                    else:
                        pr.dma("sp", x_t[:], xrows(k, s, xr, 128), reads=[INB], writes=[bx])
                    sq_t, bsq = sqp.next()
                    rstd, brs = rstd_act(pr, small, sq_t[:], bsq, x_t[:], [bx], D)
                    a_t, ba = abf.next()
                    pr.add("act", ACTF(a_t[:], x_t[:], AF.Copy, scale=rstd), reads=[bx, brs], writes=[ba])
                    tp, btp = tpp.next()
                    for kc in range(8):
                        pr.add("pe", TR(tp[:, kc * 128:(kc + 1) * 128], a_t[:, kc * 128:(kc + 1) * 128], ident[:]),
                               reads=[ba, bid], writes=[btp])
                    ncol = nr if xr is None else 128
                    pr.add("dve", CP(aT_t[:, :, 1 + si * 128:1 + si * 128 + ncol],
                                     tp[:].rearrange("p (k t) -> p k t", k=8)[:, :, 0:ncol]), reads=[btp], writes=[baT])
                cs, bcs = csp.next()
                pr.dma("sp", cs[:, 0, :ntok], rows2d(cosT, L, 0, 64, t0, ntok), reads=[INB], writes=[bcs])
                pr.dma("sp", cs[:, 1, :ntok], rows2d(sinT, L, 0, 64, t0, ntok), reads=[INB], writes=[bcs])
                return (aT_t, baT, cs, bcs)

            def SP(gi, X, nextX):
                t0, ntok, subs = s.groups[gi]
                aT_t, baT, cs, bcs = X
                rhsA = [aT_t[:, kc, 1:1 + ntok] for kc in range(8)]
                col0 = s.toff + t0

                def proj(c0, m):
                    ps, bps = pm.next()
                    for kc in range(8):
                        pr.add("pe", MM(ps[0:m, :ntok], WinA[:, kc, c0:c0 + m], rhsA[kc], kc == 0, kc == 7),
                               reads=[bWinA, baT], writes=[bps])
                    return ps, bps
                pcq = [proj(0, 128), proj(128, 128)]
                pkv = [proj(256, 128)]
                pk1, bk1 = proj(384, 64)
                pk2, bk2 = proj(448, 64)
                Xn = nextX() if nextX is not None else None
                cn, bcn = fm_norm(pcq, 2, ntok, QL)
                kn, bkn = fm_norm(pkv, 1, ntok, KVL)
                rope_out(pk1, bk1, pk2, bk2, cs, bcs, ntok, DV(sc["KR"], col0, [[Tt, 64], [1, ntok]]))
                u_group(s, s.groups[gi], aT_t, baT)
                for h in range(NH):
                    ps, bps = pm.next()
                    for kc in range(2):
                        pr.add("pe", MM(ps[:, :ntok], Wuq[:, kc, h * 192:h * 192 + 128], cn[:, kc, :ntok], kc == 0, kc == 1),
                               reads=[bWuq, bcn], writes=[bps])
                    o, bo = oq.next()
                    pr.add("act", ACTF(o[:, :ntok], ps[:, :ntok], AF.Copy), reads=[bps], writes=[bo])
                    pr.dma("act", DV(sc["QT"], (h * 192) * Tt + col0, [[Tt, 128], [1, ntok]]), o[:, :ntok], reads=[bo], writes=[db["QT"]])
                    ps1, b1 = pm.next()
                    for kc in range(2):
                        pr.add("pe", MM(ps1[0:64, :ntok], Wuq[:, kc, h * 192 + 128:h * 192 + 192], cn[:, kc, :ntok], kc == 0, kc == 1),
                               reads=[bWuq, bcn], writes=[b1])
                    ps2, b2 = pm.next()
                    for kc in range(2):
                        pr.add("pe", MM(ps2[0:64, :ntok], WuqR[:, kc, h * 64:(h + 1) * 64], cn[:, kc, :ntok], kc == 0, kc == 1),
                               reads=[bWuqR, bcn], writes=[b2])
                    rope_out(ps1, b1, ps2, b2, cs, bcs, ntok, DV(sc["QT"], (h * 192 + 128) * Tt + col0, [[Tt, 64], [1, ntok]]))
                for h in range(NH):
                    ps, bps = pm.next()
                    pr.add("pe", MM(ps[:, :ntok], Wukv[:, h * 256:h * 256 + 128], kn[:, 0, :ntok], True, True),
                           reads=[bWukv, bkn], writes=[bps])
                    o, bo = oq.next()
                    pr.add("act", ACTF(o[:, :ntok], ps[:, :ntok], AF.Copy), reads=[bps], writes=[bo])
                    pr.dma("act", DV(sc["KN"], (h * 128) * Tt + col0, [[Tt, 128], [1, ntok]]), o[:, :ntok], reads=[bo], writes=[db["KN"]])
                for si, (t, nr, xr) in enumerate(subs):
                    ps, bps = pm.next()
                    pr.add("pe", MM(ps[0:nr, :], kn[:, 0, si * 128:si * 128 + nr], WV[:], True, True), reads=[bWV, bkn], writes=[bps])
                    o, bo = oq.next()
                    pr.add("act", ACTF(o[0:nr, :], ps[0:nr, :], AF.Copy), reads=[bps], writes=[bo])
                    pr.dma("act", rows2d(sc["V"], 512, s.toff + t, nr, 0, 512), o[0:nr, :], reads=[bo], writes=[db["V"]])
                return Xn
            X = SX(0)
            for gi in range(ng):
                X = SP(gi, X, (lambda g=gi: SX(g + 1)) if gi + 1 < ng else None)
                if gi >= 1:
                    conv_group(s, s.groups[gi - 1])
            conv_group(s, s.groups[ng - 1])
        pr.flush()


def phase_B(k):
    nc, pr, W, sc, db = k.nc, k.pr, k.din, k.sc, k.db
    INB = k.din_buf
    Tt = k.Ttot
    Lm = max(s.L for s in k.seqs)
    Bm = max(s.B for s in k.seqs)
    with contextlib.ExitStack() as st:
        def sb(name, shape, dt):
            return st.enter_context(nc.sbuf_tensor(U(name), shape, dt))
        KNt, bKN = sb("KNt", [128, NH, Lm], BF16), Buf("KNt")
        KRt, bKR = sb("KRt", [128, Lm], BF16), Buf("KRt")
        Vt, bV = sb("Vt", [128, Bm, 512], BF16), Buf("Vt")
        ones, bon = sb("onesB", [128, 128], BF16), Buf("ones")
        pr.dma("sp", ones[:], W["ones"][:, :], reads=[INB], writes=[bon])
        onesf, bonf = sb("onesfB", [128, 128], F32), Buf("onesf")
        pr.add("pool", MSET(onesf[:], 1.0), writes=[bonf])
        accp = RPool(st, nc, "accB", [128, 2, 512], F32, 2)
        qnp = RPool(st, nc, "qnB", [128, 512], BF16, 2)
        qrp = RPool(st, nc, "qrB", [128, 512], BF16, 2)
        pr.add("dve", MSET(KRt[64:128, :], 0.0), writes=[bKR])
        for t_, b_ in qrp.tiles:
            pr.add("dve", MSET(t_[64:128, :], 0.0), writes=[b_])
        ptp = RPool(st, nc, "ptB", [128, 2, 512], BF16, 4)
        ptb2 = {id(t_): Buf("ptslot1") for t_, b_ in ptp.tiles}
        rsp = RPool(st, nc, "rsB", [128, 512], F32, 2)
        ohp = RPool(st, nc, "ohB", [128, NH, 512], F32, 2)
        sqp = RPool(st, nc, "sqB", [128, NH, 512], BF16, 1)
        anp = RPool(st, nc, "anB", [128, NH, 512], BF16, 2)
        pss = RPool(st, nc, "pssB", [128, 512], F32, 4, psum=True)
        pop = RPool(st, nc, "poB", [128, 512], F32, 2, psum=True)
        psm = RPool(st, nc, "psmB", [128, 512], F32, 1, psum=True)
        pep = RPool(st, nc, "pepB", [128, 512], F32, 1, psum=True)
        for s in k.seqs:
            L, n = s.L, s.n
            for h in range(NH):
                pr.dma("sp", KNt[:, h, 0:L], DV(sc["KN"], h * 128 * Tt + s.toff, [[Tt, 128], [1, L]]), reads=[db["KN"]], writes=[bKN])
            pr.dma("sp", KRt[0:64, 0:L], DV(sc["KR"], s.toff, [[Tt, 64], [1, L]]), reads=[db["KR"]], writes=[bKR])
            pr.dma("sp", Vt[0:NMETA, 0, :], rows2d(sc["V"], 512, s.toff, NMETA, 0, 512), reads=[db["V"]], writes=[bV])
            i = 0
            while i < n:
                c = min(16, n - i)
                pr.dma("sp", Vt[:, 1 + i:1 + i + c, :], DV(sc["V"], (s.toff + NMETA + 128 * i) * 512, [[512, 128], [128 * 512, c], [1, 512]]),
                       reads=[db["V"]], writes=[bV])
                i += c
            ktiles = [(0, NMETA)] + [(NMETA + 128 * i, 128) for i in range(n)]
            for (t0, ntok, subs) in s.groups[1:]:
                col0 = s.toff + t0
                oh, boh = ohp.next()
                for h in range(NH):
                    qn, bqn = qnp.next()
                    qr, bqr = qrp.next()
                    pr.dma("sp", qn[:, :ntok], DV(sc["QT"], h * 192 * Tt + col0, [[Tt, 128], [1, ntok]]), reads=[db["QT"]], writes=[bqn])
                    pr.dma("sp", qr[0:64, :ntok], DV(sc["QT"], (h * 192 + 128) * Tt + col0, [[Tt, 64], [1, ntok]]), reads=[db["QT"]], writes=[bqr])
                    po, bpo = pop.next()
                    pm_, bpm = psm.next()
                    ac, bac = accp.next()
                    pr.add("pool", MSET(ac[:].rearrange("p a c -> p (a c)"), 0.0), writes=[bac])
                    pstate = {}

                    def s_mm(kt):
                        c0, nk = ktiles[kt]
                        ps, bps = pss.next()
                        pr.add("pe", MM(ps[0:nk, :ntok], KNt[:, h, c0:c0 + nk], qn[:, :ntok], True, False), reads=[bKN, bqn], writes=[bps])
                        pr.add("pe", MM(ps[0:nk, :ntok], KRt[:, c0:c0 + nk], qr[:, :ntok], False, True), reads=[bKR, bqr], writes=[bps])
                        slot = 0 if kt == 0 else (kt - 1) % 2
                        if kt == 0 or slot == 0:
                            pstate["cur"] = ptp.next()
                        ptt, bp0 = pstate["cur"]
                        bpt = bp0 if slot == 0 else ptb2[id(ptt)]
                        pt = ptt[:, slot, :]
                        pr.add("act", ACTF(pt[0:nk, :ntok], ps[0:nk, :ntok], AF.Exp, scale=SCALE), reads=[bps], writes=[bpt])
                        if kt == 0:
                            pr.add("dve", TT(ac[0:nk, 0, :ntok], ac[0:nk, 0, :ntok], pt[0:nk, :ntok], ALU.add), reads=[bac, bpt], writes=[bac])
                        elif slot == 1:
                            pr.add("dve", TT(ac[:, :, :ntok], ac[:, :, :ntok], ptt[:, :, :ntok], ALU.add), reads=[bac, bp0, bpt], writes=[bac])
                        elif kt == nkt - 1:
                            pr.add("dve", TT(ac[:, 0, :ntok], ac[:, 0, :ntok], pt[:, :ntok], ALU.add), reads=[bac, bpt], writes=[bac])
                        return pt, bpt, nk
                    nkt = len(ktiles)
                    LA = 2
                    q_ = [s_mm(i) for i in range(min(LA, nkt))]
                    for kt in range(nkt):
                        if kt + LA < nkt:
                            q_.append(s_mm(kt + LA))
                        pt, bpt, nk = q_.pop(0)
                        pr.add("pe", MM(po[:, :ntok], Vt[0:nk, kt, h * 128:(h + 1) * 128], pt[0:nk, :ntok], kt == 0, kt == nkt - 1),
                               reads=[bV, bpt], writes=[bpo])
                    for ai in range(2):
                        pr.add("pe", MM(pm_[:, :ntok], onesf[:], ac[:, ai, :ntok], ai == 0, ai == 1), reads=[bonf, bac], writes=[bpm])
                    rs, brs = rsp.next()
                    pr.add("dve", RECIP(rs[:, :ntok], pm_[:, :ntok]), reads=[bpm], writes=[brs])
                    pr.add("dve", TT(oh[:, h, :ntok], po[:, :ntok], rs[:, :ntok], ALU.mult), reads=[bpo, brs], writes=[boh])
                sq, bsq = sqp.next()
                pe_, bpe = pep.next()
                for h in range(NH):
                    pr.add("act", ACTF(sq[:, h, :ntok], oh[:, h, :ntok], AF.Square), reads=[boh], writes=[bsq])
                for h in range(NH):
                    pr.add("pe", MM(pe_[:, :ntok], ones[:], sq[:, h, :ntok], h == 0, h == NH - 1), reads=[bon, bsq], writes=[bpe])
                l1, bl1 = rsp.next()
                pr.add("act", ACTF(l1[:, :ntok], pe_[:, :ntok], AF.Ln, scale=1.0 / 512, bias=EPS), reads=[bpe], writes=[bl1])
                r1, br1 = rsp.next()
                pr.add("act", ACTF(r1[:, :ntok], l1[:, :ntok], AF.Exp, scale=-0.5), reads=[bl1], writes=[br1])
                an, ban = anp.next()
                for h in range(NH):
                    pr.add("dve" if h % 2 == 0 else "pool", TT(an[:, h, :ntok], oh[:, h, :ntok], r1[:, :ntok], ALU.mult), reads=[boh, br1], writes=[ban])
                pr.dma("pool", DV(sc["AN"], col0, [[Tt, 128], [128 * Tt, NH], [1, ntok]]), an[:, :, :ntok], reads=[ban], writes=[db["AN"]])
        pr.flush()


def phase_D1(k):
    nc, pr, W, sc, db = k.nc, k.pr, k.din, k.sc, k.db
    INB = k.din_buf
    Tt = k.Ttot
    with contextlib.ExitStack() as st:
        def sb(name, shape, dt):
            return st.enter_context(nc.sbuf_tensor(U(name), shape, dt))
        Wo, bWo = sb("Wo", [128, 8, D], BF16), Buf("Wo")
        gB, bgB = sb("gB1", [128, D], F32), Buf("gB1")
        gv, bgv = sb("gv1", [128, 8], F32), Buf("gv1")
        ident, bid = sb("identD1", [128, 128], BF16), Buf("ident")
        pr.dma("sp", ident[:], W["ident"][:, :], reads=[INB], writes=[bid])
        pr.dma("sp", gB[:], DV(W["post_mix_g"], 0, [[0, 128], [1, D]]), reads=[INB], writes=[bgB])
        for kc in range(4):
            pr.dma("sp", gv[:, kc:kc + 1], col1(W["attn_out_g"], kc * 128), reads=[INB], writes=[bgv])
            pr.dma("sp", gv[:, 4 + kc:5 + kc], col1(W["hy_out_g"], kc * 128), reads=[INB], writes=[bgv])
        stg = RPool(st, nc, "stgD1", [128, D], F32, 2)
        for kc in range(8):
            sg, bsg = stg.next()
            pr.dma("sp", sg[:], rows2d(W["w_o"], D, kc * 128, 128, 0, D), reads=[INB], writes=[bsg])
            pr.add("dve", TS(Wo[:, kc, :], sg[:], gv[:, kc:kc + 1]), reads=[bsg, bgv], writes=[bWo])
        anp = RPool(st, nc, "anD1", [128, NH, 512], BF16, 3)
        zp = RPool(st, nc, "zD1", [128, 512], BF16, 4)
        xp = RPool(st, nc, "xD1", [128, D], F32, 5)
        sqp = RPool(st, nc, "sqD1", [128, D], BF16, 3)
        small = RPool(st, nc, "smD1", [128, 4], F32, 12)
        hnp = RPool(st, nc, "hnD1", [128, 512], BF16, 3)
        hTp = RPool(st, nc, "hTD1", [128, 4, 128], BF16, 5)
        tpp = RPool(st, nc, "tpD1", [128, 512], BF16, 2, psum=True)
        pmm = RPool(st, nc, "pmD1", [128, 1024], F32, 3, psum=True)
        tp_ = RPool(st, nc, "tD1", [128, D], F32, 2)
        hp = RPool(st, nc, "hD1", [128, D], F32, 2)
        work = []
        for s in k.seqs:
            for (t0, ntok, subs) in s.groups[1:]:
                for si, sub in enumerate(subs):
                    work.append((s, t0, ntok, si, sub))

        def stX(w):
            s, t0, ntok, si, (t, nr, xr) = w
            col0 = s.toff + t0
            if si == 0:
                an, ban = anp.next()
                pr.dma("sp", an[:, :, :ntok], DV(sc["AN"], col0, [[Tt, 128], [128 * Tt, NH], [1, ntok]]), reads=[db["AN"]], writes=[ban])
                stX.an = (an, ban)
            an, ban = stX.an
            z, bz = zp.next()
            pr.dma("sp", z[:], rows2d(sc["Z2"], 512, s.toff + t, 128, 0, 512), reads=[db["Z2"]], writes=[bz])
            x_t, bx = xp.next()
            pr.dma("sp", x_t[:], xrows(k, s, xr, 128), reads=[INB], writes=[bx])
            sq, bsq = sqp.next()
            rstd, brs = rstd_act(pr, small, sq[:, 0:512], bsq, z[:], [bz], 512)
            hn, bhn = hnp.next()
            pr.add("act", ACTF(hn[:], z[:], AF.Copy, scale=rstd), reads=[bz, brs], writes=[bhn])
            tp, btp = tpp.next()
            for kc in range(4):
                pr.add("pe", TR(tp[:, kc * 128:(kc + 1) * 128], hn[:, kc * 128:(kc + 1) * 128], ident[:]), reads=[bhn, bid], writes=[btp])
            hT, bhT = hTp.next()
            pr.add("dve", CP(hT[:], tp[:].rearrange("p (k t) -> p k t", k=4)), reads=[btp], writes=[bhT])
            return (an, ban, hT, bhT, x_t, bx)

        def stYm(w, xs_):
            s, t0, ntok, si, (t, nr, xr) = w
            an, ban, hT, bhT, x_t, bx = xs_
            ps, bps = pmm.next()
            for half in range(2):
                for kc in range(4):
                    pr.add("pe", MM(ps[:, half * 512:(half + 1) * 512], an[:, kc, si * 128:(si + 1) * 128], Wo[:, kc, half * 512:(half + 1) * 512], kc == 0, False),
                           reads=[ban, bWo], writes=[bps])
                for kc in range(4):
                    pr.add("pe", MM(ps[:, half * 512:(half + 1) * 512], hT[:, kc, :], Wo[:, 4 + kc, half * 512:(half + 1) * 512], False, kc == 3),
                           reads=[bhT, bWo], writes=[bps])
            return ps, bps

        def stYe(w, xs_, pp_):
            s, t0, ntok, si, (t, nr, xr) = w
            an, ban, hT, bhT, x_t, bx = xs_
            ps, bps = pp_
            sq2, bsq2 = sqp.next()
            rstd2, brs2 = rstd_act(pr, small, sq2[:], bsq2, ps[:], [bps], D)
            tt, btt = tp_.next()
            pr.add("dve", STT(tt[:], ps[:], rstd2, gB[:], ALU.mult, ALU.mult), reads=[bps, brs2, bgB], writes=[btt])
            h1, bh1 = hp.next()
            pr.add("dve", TT(h1[:], tt[:], x_t[:], ALU.add), reads=[btt, bx], writes=[bh1])
            pr.dma("pool", rows2d(sc["H1"], D, s.toff + t, 128, 0, D), h1[:], reads=[bh1], writes=[db["H1"]])
        LOOK = 2
        pend = {}
        for i in range(min(LOOK, len(work))):
            pend[i] = stX(work[i])
        for i in range(len(work)):
            pp_ = stYm(work[i], pend[i])
            if i + LOOK < len(work):
                pend[i + LOOK] = stX(work[i + LOOK])
            stYe(work[i], pend.pop(i), pp_)
        pr.flush()


GD2 = 2


def phase_D2(k):
    nc, pr, W, sc, db = k.nc, k.pr, k.din, k.sc, k.db
    INB = k.din_buf
    NT = GD2 * 128
    with contextlib.ExitStack() as st:
        def sb(name, shape, dt):
            return st.enter_context(nc.sbuf_tensor(U(name), shape, dt))
        W1, bW1 = sb("W1", [128, 8, DFF], BF16), Buf("W1")
        W2, bW2 = sb("W2", [128, 32, D], BF16), Buf("W2")
        gB, bgB = sb("gB2", [128, D], F32), Buf("gB2")
        gv, bgv = sb("gv2", [128, 8], F32), Buf("gv2")
        ident, bid = sb("identD2", [128, 128], BF16), Buf("ident")
        pr.dma("sp", ident[:], W["ident"][:, :], reads=[INB], writes=[bid])
        pr.dma("sp", gB[:], DV(W["post_mlp_g"], 0, [[0, 128], [1, D]]), reads=[INB], writes=[bgB])
        for kc in range(8):
            pr.dma("sp", gv[:, kc:kc + 1], col1(W["pre_mlp_g"], kc * 128), reads=[INB], writes=[bgv])
        with contextlib.ExitStack() as st2:
            stg = RPool(st2, nc, "stgD2", [128, DFF], F32, 3)
            for kc in range(8):
                sg, bsg = stg.next()
                pr.dma("sp", sg[:], rows2d(W["w_ff1"], DFF, kc * 128, 128, 0, DFF), reads=[INB], writes=[bsg])
                if kc % 2 == 0:
                    pr.add("dve", TS(W1[:, kc, :], sg[:], gv[:, kc:kc + 1]), reads=[bsg, bgv], writes=[bW1])
                else:
                    pr.add("act", ACTF(W1[:, kc, :], sg[:], AF.Copy, scale=gv[:, kc:kc + 1]), reads=[bsg, bgv], writes=[bW1])
            for f4 in range(8):
                sg, bsg = stg.next()
                pr.dma("sp", sg[:].rearrange("p (a c) -> p a c", a=4),
                       DV(W["w_ff2"], f4 * 4 * 128 * D, [[D, 128], [128 * D, 4], [1, D]]), reads=[INB], writes=[bsg])
                eng = ("dve", "act")[f4 % 2]
                if eng == "act":
                    pr.add("act", ACTF(W2[:, f4 * 4:(f4 + 1) * 4, :], sg[:].rearrange("p (a c) -> p a c", a=4), AF.Copy), reads=[bsg], writes=[bW2])
                else:
                    pr.add(eng, CP(W2[:, f4 * 4:(f4 + 1) * 4, :], sg[:].rearrange("p (a c) -> p a c", a=4)), reads=[bsg], writes=[bW2])
            pr.flush()
        hp = RPool(st, nc, "hD2", [128, D], F32, 3)
        h2p = RPool(st, nc, "h2D2", [128, D], F32, 2)
        sqp = RPool(st, nc, "sqD2", [128, D], BF16, 2)
        small = RPool(st, nc, "smD2", [128, 4], F32, 12)
        abf = RPool(st, nc, "abfD2", [128, D], BF16, 4)
        aTp = RPool(st, nc, "aTD2", [128, 8, NT], BF16, 2)
        mT = sb("mT", [128, 32, NT], BF16)
        bmT = [Buf(f"mT{i}") for i in range(32)]
        rp = RPool(st, nc, "rD2", [128, NT], F32, 3)
        tpp = RPool(st, nc, "tpD2", [128, 1024], BF16, 1, psum=True)
        pmm = RPool(st, nc, "pmD2", [128, 512], F32, 3, psum=True)
        pm2 = RPool(st, nc, "pm2D2", [128, 1024], F32, 2, psum=True)
        tp_ = RPool(st, nc, "tD2", [128, D], F32, 2)
        yp_ = RPool(st, nc, "yD2", [128, D], F32, 2)
        work = []
        for s in k.seqs:
            subs_all = [sub for g in s.groups[1:] for sub in g[2]]
            for g0 in range(0, len(subs_all), GD2):
                work.append((s, subs_all[g0:g0 + GD2]))

        def stXa(w, sis=None):
            s, subs = w
            outs = []
            for si, (t, nr, xr) in enumerate(subs):
                if sis is not None and si not in sis:
                    continue
                h1, bh1 = hp.next()
                pr.dma("sp", h1[:], rows2d(sc["H1"], D, s.toff + t, 128, 0, D), reads=[db["H1"]], writes=[bh1])
                sq, bsq = sqp.next()
                rstd, brs = rstd_act(pr, small, sq[:], bsq, h1[:], [bh1], D)
                a_t, ba = abf.next()
                pr.add("act", ACTF(a_t[:], h1[:], AF.Copy, scale=rstd), reads=[bh1, brs], writes=[ba])
                outs.append((a_t, ba))
            return outs

        def stXb(w, outs):
            s, subs = w
            aT, baT = aTp.next()
            for si, (t, nr, xr) in enumerate(subs):
                a_t, ba = outs[si]
                tp, btp = tpp.next()
                for kc in range(8):
                    pr.add("pe", TR(tp[:, kc * 128:(kc + 1) * 128], a_t[:, kc * 128:(kc + 1) * 128], ident[:]), reads=[ba, bid], writes=[btp])
                pr.add("dve", CP(aT[:, :, si * 128:(si + 1) * 128], tp[:].rearrange("p (k t) -> p k t", k=8)), reads=[btp], writes=[baT])
            return aT, baT

        def stF1(w, aTs, fcs):
            s, subs = w
            aT, baT = aTs
            ntok = 128 * len(subs)
            for fc in fcs:
                ps, bps = pmm.next()
                for kc in range(8):
                    pr.add("pe", MM(ps[:, :ntok], W1[:, kc, fc * 128:(fc + 1) * 128], aT[:, kc, :ntok], kc == 0, kc == 7),
                           reads=[bW1, baT], writes=[bps])
                r, br = rp.next()
                pr.add("act", ACTF(r[:, :ntok], ps[:, :ntok], AF.Relu), reads=[bps], writes=[br])
                pr.add("dve" if fc % 4 != 3 else "pool", TT(mT[:, fc, :ntok], r[:, :ntok], r[:, :ntok], ALU.mult), reads=[br], writes=[bmT[fc]])

        def stF2(w):
            s, subs = w
            for si, (t, nr, xr) in enumerate(subs):
                ps, bps = pm2.next()
                for half in range(2):
                    for fc in range(32):
                        pr.add("pe", MM(ps[:, half * 512:(half + 1) * 512], mT[:, fc, si * 128:(si + 1) * 128], W2[:, fc, half * 512:(half + 1) * 512], fc == 0, fc == 31),
                               reads=[bmT[fc], bW2], writes=[bps])
                sq2, bsq2 = sqp.next()
                rstd2, brs2 = rstd_act(pr, small, sq2[:], bsq2, ps[:], [bps], D)
                tt, btt = tp_.next()
                pr.add("dve", STT(tt[:], ps[:], rstd2, gB[:], ALU.mult, ALU.mult), reads=[bps, brs2, bgB], writes=[btt])
                h2, bh2 = h2p.next()
                pr.dma("sp", h2[:], rows2d(sc["H1"], D, s.toff + t, 128, 0, D), reads=[db["H1"]], writes=[bh2])
                y, by = yp_.next()
                pr.add("dve", TT(y[:], tt[:], h2[:], ALU.add), reads=[btt, bh2], writes=[by])
                pr.dma("pool", yrows(k, s, xr, 128), y[:], reads=[by], writes=[k.dout])
        cur = stXb(work[0], stXa(work[0]))
        for i, w in enumerate(work):
            has_next = i + 1 < len(work)
            nsub = len(work[i + 1][1]) if has_next else 0
            stF1(w, cur, range(0, 5))
            nxa = stXa(work[i + 1], [0]) if has_next else []
            stF1(w, cur, range(5, 11))
            if has_next and nsub > 1:
                nxa = nxa + stXa(work[i + 1], list(range(1, nsub)))
            stF1(w, cur, range(11, 20))
            nxt = stXb(work[i + 1], nxa) if has_next else None
            stF1(w, cur, range(20, 32))
            stF2(w)
            cur = nxt
        pr.flush()


GCH = [(0, 128), (128, 127)]


def load_chunked(pr, st, nc, name, src, nrows, ncols, chunks, INB):
    out = []
    for ci, (r0, rn) in enumerate(chunks):
        t = st.enter_context(nc.sbuf_tensor(U(f"{name}{ci}"), [128, ncols], BF16))
        b = Buf(f"{name}{ci}")
        pr.dma("sp", t[0:rn, :], rows2d(src, ncols, r0, rn, 0, ncols), reads=[INB], writes=[b])
        out.append((t, b, rn))
    return out


def phase_F(k):
    nc, pr, W, sc, db = k.nc, k.pr, k.din, k.sc, k.db
    INB = k.din_buf
    with contextlib.ExitStack() as st:
        def sb(name, shape, dt):
            return st.enter_context(nc.sbuf_tensor(U(name), shape, dt))
        fw1, fw2, fw3 = sb("fw1", [33, 64], F32), sb("fw2", [64, 64], F32), sb("fw3", [64, 2048], F32)
        fv, dB = sb("fv", [64, 8], F32), sb("dB", [128, 2048], F32)
        bw, bfv, bdB = Buf("fw"), Buf("fv"), Buf("dB")
        pr.dma("sp", fw1[:], rows2d(W["f_w1"], 64, 0, 33, 0, 64), reads=[INB], writes=[bw])
        pr.dma("sp", fw2[:], rows2d(W["f_w2"], 64, 0, 64, 0, 64), reads=[INB], writes=[bw])
        pr.dma("sp", fw3[:], rows2d(W["f_w3"], 2048, 0, 64, 0, 2048), reads=[INB], writes=[bw])
        fw3b, bw3b = sb("fw3b", [64, 2048], BF16), Buf("fw3b")
        pr.add("dve", CP(fw3b[:], fw3[:]), reads=[bw], writes=[bw3b])
        pr.dma("sp", fv[:, 0:1], col1(W["f_freq"], 0, 64), reads=[INB], writes=[bfv])
        pr.dma("sp", fv[:, 1:2], col1(W["f_b1"], 0, 64), reads=[INB], writes=[bfv])
        pr.dma("sp", fv[:, 2:3], col1(W["f_b2"], 0, 64), reads=[INB], writes=[bfv])
        pr.add("dve", TT(fv[:, 3:4], fv[:, 1:2], fv[:, 0:1], ALU.mult), reads=[bfv], writes=[bfv])
        pr.add("dve", TT(fv[:, 4:5], fv[:, 2:3], fv[:, 0:1], ALU.mult), reads=[bfv], writes=[bfv])
        pr.dma("sp", dB[:], DV(W["f_decay"], 0, [[0, 128], [1, 2048]]), reads=[INB], writes=[bdB])
        pr.add("act", ACTF(dB[:], dB[:], AF.Abs), reads=[bdB], writes=[bdB])
        zF, bzF = sb("zeroF", [128, 512], BF16), Buf("zeroF")
        pr.add("pool", MSET(zF[:], 0.0), writes=[bzF])
        hb, bhb = sb("hbias", [1, 2, 512], F32), Buf("hbias")
        pr.dma("sp", hb[:], DV(W["hy_bias"], 0, [[0, 1], [512, 2], [1, 512]]), reads=[INB], writes=[bhb])
        FB_ = {nm: load_chunked(pr, st, nc, f"F{nm}", W[nm], NG, NG, GCH, INB) for nm in ("FBc", "FBs", "FBsn")}
        for L in k.Ls:
            B = (L - NMETA) // 128 + 1
            BP = prow(B)
            T = 128 * B
            PP = 2 * B - 1
            featsT, tneg = W[f"featsT_{L}"], W[f"tneg_{L}"]
            with contextlib.ExitStack() as s2:
                CH = 512
                ftp = RPool(s2, nc, "ftF", [33, CH], F32, 3)
                tnp_ = RPool(s2, nc, "tnF", [128, 4], F32, 5)
                ap_ = RPool(s2, nc, "aF", [64, CH], F32, 6)
                tq = RPool(s2, nc, "tF", [64, CH], F32, 4)
                hp_ = RPool(s2, nc, "hF", [64, CH], F32, 4)
                hbp = RPool(s2, nc, "hbF", [64, CH], BF16, 4)
                ep = RPool(s2, nc, "eF", [128, 512], F32, 3)
                kp = RPool(s2, nc, "kF", [128, 512], BF16, 4)
                p1 = RPool(s2, nc, "p1F", [64, CH], F32, 3, psum=True)
                p3 = RPool(s2, nc, "p3F", [128, 512], F32, 4, psum=True)

                def sin_layer(ps, bps, bias_col, n_, outp=None):
                    a, ba = ap_.next()
                    pr.add("dve", TS(a[:, :n_], ps[:, :n_], fv[:, 0:1], fv[:, bias_col:bias_col + 1], ALU.mult, ALU.add), reads=[bps, bfv], writes=[ba])
                    t, bt = tq.next()
                    pr.add("dve", TS(t[:, :n_], a[:, :n_], PI, -2.0 * PI, ALU.is_gt, ALU.mult), reads=[ba], writes=[bt])
                    pr.add("dve", TT(a[:, :n_], a[:, :n_], t[:, :n_], ALU.add), reads=[ba, bt], writes=[ba])
                    t, bt = tq.next()
                    pr.add("dve", TS(t[:, :n_], a[:, :n_], -PI, 2.0 * PI, ALU.is_lt, ALU.mult), reads=[ba], writes=[bt])
                    pr.add("dve", TT(a[:, :n_], a[:, :n_], t[:, :n_], ALU.add), reads=[ba, bt], writes=[ba])
                    h, bh = (outp or hp_).next()
                    pr.add("act", ACTF(h[:, :n_], a[:, :n_], AF.Sin), reads=[ba], writes=[bh])
                    return h, bh
                chunks = [(r0, min(CH, 2 * T - r0)) for r0 in range(0, 2 * T, CH)]

                def st1(c):
                    r0, n_ = chunks[c]
                    ft, bft = ftp.next()
                    pr.dma("sp", ft[:, :n_], rows2d(featsT, 2 * T, 0, 33, r0, n_), reads=[INB], writes=[bft])
                    tn, btn = tnp_.next()
                    for j in range(n_ // 128):
                        pr.dma("sp", tn[:, j:j + 1], col1(tneg, r0 + 128 * j), reads=[INB], writes=[btn])
                    ps, bps = p1.next()
                    pr.add("pe", MM(ps[:, :n_], fw1[:], ft[:, :n_], True, True), reads=[bw, bft], writes=[bps])
                    h1, bh1 = sin_layer(ps, bps, 3, n_)
                    return (h1, bh1, tn, btn)

                def st2(c, s1_):
                    r0, n_ = chunks[c]
                    h1, bh1, tn, btn = s1_
                    ps, bps = p1.next()
                    pr.add("pe", MM(ps[:, :n_], fw2[:], h1[:, :n_], True, True), reads=[bw, bh1], writes=[bps])
                    h2, bh2 = sin_layer(ps, bps, 4, n_, hbp)
                    return (h2, bh2, tn, btn)

                def st3(c, s2_):
                    r0, n_ = chunks[c]
                    h2, bh2, tn, btn = s2_
                    for j in range(n_ // 128):
                        dirn = 0 if (r0 + 128 * j - T) >= 0 else 1
                        for n in range(2):
                            cols = n * 1024 + dirn * 512
                            ps3, bp3 = p3.next()
                            pr.add("pe", MM(ps3[:], h2[:, j * 128:(j + 1) * 128], fw3b[:, cols:cols + 512], True, True), reads=[bw3b, bh2], writes=[bp3])
                            E, bE = ep.next()
                            pr.add("act", ACTF(E[:], dB[:, cols:cols + 512], AF.Exp, scale=tn[:, j:j + 1]), reads=[bdB, btn], writes=[bE])
                            kt, bkt = kp.next()
                            if r0 + 128 * j == T:
                                pr.add("dve", TT(E[:], ps3[:], E[:], ALU.mult), reads=[bp3, bE], writes=[bE])
                                pr.add("dve", TT(E[0:1, :], E[0:1, :], hb[0:1, n, :], ALU.add), reads=[bE, bhb], writes=[bE])
                                pr.add("dve", CP(kt[:], E[:]), reads=[bE], writes=[bkt])
                            else:
                                pr.add("dve", TT(kt[:], ps3[:], E[:], ALU.mult), reads=[bp3, bE], writes=[bkt])
                            nm = f"KTIME{n}_{L}"
                            pr.dma("pool", rows2d(sc[nm], 512, r0 + 128 * j, 128, 0, 512), kt[:], reads=[bkt], writes=[db[nm]])
                NCK = len(chunks)
                r1, r2 = {}, {}
                for i in range(NCK + 2):
                    if i < NCK:
                        r1[i] = st1(i)
                    if 0 <= i - 1 < NCK:
                        r2[i - 1] = st2(i - 1, r1.pop(i - 1))
                    if 0 <= i - 2 < NCK:
                        st3(i - 2, r2.pop(i - 2))
                pr.flush()
            S1F, bS1F = sc[f"S1F_{L}"], db[f"S1F_{L}"]
            for n in range(2):
                KT, bKT = sc[f"KTIME{n}_{L}"], db[f"KTIME{n}_{L}"]
                KH, bKH = sc[f"KHAT{n}_{L}"], db[f"KHAT{n}_{L}"]
                with contextlib.ExitStack() as s2:
                    kch = _split(PP)
                    fa = []
                    for p, nm in enumerate(("FA_re", "FA_im")):
                        lst = []
                        for ci, (r0_, rn_) in enumerate(kch):
                            t_ = s2.enter_context(nc.sbuf_tensor(U(f"FA{p}{ci}"), [128, 128], BF16))
                            b_ = Buf(f"FA{p}{ci}")
                            pr.add("dve", MSET(t_[:], 0.0), writes=[b_])
                            pr.dma("sp", t_[0:rn_, 0:B], rows2d(W[f"{nm}_{L}"], B, r0_, rn_, 0, B), reads=[INB], writes=[b_])
                            lst.append((t_, b_, rn_))
                        fa.append(lst)
                    EB = 8
                    xin = [RPool(s2, nc, f"xinF{ci}", [128, EB, 512], BF16, 2) for ci in range(len(kch))]
                    for xp_ in xin:
                        for t_, b_ in xp_.tiles:
                            pr.add("dve", MSET(t_[:].rearrange("p a c -> p (a c)"), 0.0), writes=[b_])
                    sop = [RPool(s2, nc, f"soF{p_}", [128, EB, 512], BF16, 2) for p_ in range(2)]
                    for pi_, sp_ in enumerate(sop):
                        for t_, b_ in sp_.tiles:
                            zero_init(pr, t_, b_, "act" if pi_ == 0 else "dve", zF, bzF)
                    pp = RPool(s2, nc, "ppF", [128, 512], F32, 8, psum=True)
                    for e0 in range(0, NG, EB):
                        ne = min(EB, NG - e0)
                        xs = []
                        for ci, (d0, dn) in enumerate(kch):
                            xt_, bx = xin[ci].next()
                            pr.dma("sp", xt_[0:dn, 0:ne, :], DV(KT, (128 * d0 + e0 + 1) * 512, [[128 * 512, dn], [512, ne], [1, 512]]), reads=[bKT], writes=[bx])
                            xs.append((xt_, bx, dn))
                        sos = [sop[0].next(), sop[1].next()]
                        for ee in range(ne):
                            for part in range(2):
                                so, bso = sos[part]
                                ps, bps = pp.next()
                                for ci, (xt_, bx, dn) in enumerate(xs):
                                    ft_, bft_, rn = fa[part][ci]
                                    pr.add("pe", MM(ps[:, :], ft_[:, :], xt_[:, ee, :], ci == 0, ci == len(xs) - 1), reads=[bft_, bx], writes=[bps])
                                if part == 0:
                                    pr.add("act", ACTF(so[0:B, ee, :], ps[0:B, :], AF.Copy), reads=[bps], writes=[bso])
                                else:
                                    pr.add("dve", CP(so[0:B, ee, :], ps[0:B, :]), reads=[bps], writes=[bso])
                        for part in range(2):
                            so, bso = sos[part]
                            pr.dma("act" if part == 0 else "pool", DV(S1F, (part * BP * NG + e0) * 512, [[NG * 512, BP], [512, ne], [1, 512]]), so[0:BP, 0:ne, :], reads=[bso], writes=[bS1F])
                    pr.flush()
                with contextlib.ExitStack() as s2:
                    FBK = 4
                    xin = [RPool(s2, nc, f"xinG{ci}", [128, 2, FBK, 512], BF16, 2) for ci in range(2)]
                    kop = [RPool(s2, nc, f"koG{p_}", [128, FBK, 512], BF16, 3) for p_ in range(2)]
                    for pi_, kp_ in enumerate(kop):
                        for t_, b_ in kp_.tiles:
                            zero_init(pr, t_, b_, "act" if pi_ == 0 else "dve", zF, bzF)
                    pp = RPool(s2, nc, "ppG", [128, 512], F32, 8, psum=True)
                    for f0 in range(0, B, FBK):
                        nf = min(FBK, B - f0)
                        xs = []
                        for ci, (e0c, en) in enumerate(GCH):
                            xt_, bx = xin[ci].next()
                            for p_ in range(2):
                                pr.dma("sp", xt_[0:128, p_, 0:nf, :], DV(S1F, (p_ * BP * NG + f0 * NG + e0c) * 512, [[512, 128], [NG * 512, nf], [1, 512]]),
                                       reads=[bS1F], writes=[bx])
                            xs.append((xt_, bx, en))
                        for gc, (g0, gn) in enumerate(GCH):
                            kos = [kop[0].next(), kop[1].next()]
                            for ff in range(nf):
                                for part in range(2):
                                    ko, bko = kos[part]
                                    ps, bps = pp.next()
                                    terms = []
                                    for ci, (xt_, bx, en) in enumerate(xs):
                                        if part == 0:
                                            terms += [("FBc", ci, 0), ("FBs", ci, 1)]
                                        else:
                                            terms += [("FBc", ci, 1), ("FBsn", ci, 0)]
                                    for ti, (nm, ci, src) in enumerate(terms):
                                        tb, btb, rn = FB_[nm][ci]
                                        xt_, bx, en = xs[ci]
                                        pr.add("pe", MM(ps[0:gn, :], tb[0:en, g0:g0 + gn], xt_[0:en, src, ff, :], ti == 0, ti == len(terms) - 1),
                                               reads=[btb, bx], writes=[bps])
                                    if part == 0:
                                        pr.add("act", ACTF(ko[0:gn, ff, :], ps[0:gn, :], AF.Copy), reads=[bps], writes=[bko])
                                    else:
                                        pr.add("dve", CP(ko[0:gn, ff, :], ps[0:gn, :]), reads=[bps], writes=[bko])
                            for p_ in range(2):
                                ko, bko = kos[p_]
                                pr.dma("act" if p_ == 0 else "pool", DV(KH, (gc * 128 * 2 * B + p_ * B + f0) * 512, [[2 * B * 512, 128], [512, nf], [1, 512]]),
                                       ko[0:128, 0:nf, :], reads=[bko], writes=[bKH])
                    pr.flush()


def phase_C(k):
    nc, pr, W, sc, db = k.nc, k.pr, k.din, k.sc, k.db
    INB = k.din_buf
    Bm = k.FD
    S1, S2 = sc["S1"], sc["S2"]
    with contextlib.ExitStack() as st:
        def sb(name, shape, dt):
            return st.enter_context(nc.sbuf_tensor(U(name), shape, dt))
        Bt = {}
        for nm in ("Bc", "Bs", "Bsn"):
            t = sb(f"C{nm}", [128, NG], BF16)
            b = Buf(f"C{nm}")
            pr.dma("sp", t[:], W[nm][:, :], reads=[INB], writes=[b])
            Bt[nm] = (t, b)
        IB = {nm: load_chunked(pr, st, nc, f"C{nm}", W[nm], NG, 128, GCH, INB) for nm in ("IBc", "IBs", "IBsn", "IBcn")}
        JB3 = 8
        zc, bzc = sb("zeroC", [128, 1536], BF16), Buf("zeroC")
        pr.add("dve", MSET(zc[:], 0.0), writes=[bzc])
        Tt_ = k.Ttot
        for i_ in range(16):
            pr.dma("pool", rows2d(sc["UC"], 1536, Tt_ + 128 * i_, 128, 0, 1536), zc[:, :], reads=[bzc], writes=[db["UC"]])
            pr.dma("pool", rows2d(sc["Z1"], 512, Tt_ + 128 * i_, 128, 0, 512), zc[:, 0:512], reads=[bzc], writes=[db["Z1"]])
        bemin = min(u["Be"] for u in k.units)
        for p_ in range(2):
            for f_ in range(bemin, Bm):
                pr.dma("pool", rows2d(S2, 512, (p_ * Bm + f_) * 128, 128, 0, 512), zc[:, 0:512], reads=[bzc], writes=[db["S2"]])
        At = {}
        for ui, u in enumerate(k.units):
            Be = u["Be"]
            for nm in ("A_re", "A_im", "IA_re", "IA_im"):
                t = sb(f"C{nm}u{ui}", [128, 128], BF16)
                b = Buf(f"C{nm}u{ui}")
                pr.dma("sp", t[:, :], W[f"u{ui}_{nm}"][:, :], reads=[INB], writes=[b])
                At[(nm, ui)] = (t, b)
        pr.flush()
        for ui, u in enumerate(k.units):
            L, B, nb, toff, Be = u["L"], u["B"], u["nb"], u["toff"], u["Be"]
            BP = prow(Be)
            for n in range(2):
                if n == 0:
                    zsrc, bz, zrs, zco = sc["UC"], db["UC"], 1536, 1024
                    zdst, bzd = sc["Z1"], db["Z1"]
                else:
                    zsrc, bz, zrs, zco = sc["Z1"], db["Z1"], 512, 0
                    zdst, bzd = sc["Z2"], db["Z2"]
                KH, bKH = sc[f"KHAT{n}_{L}"], db[f"KHAT{n}_{L}"]
                with contextlib.ExitStack() as s2:
                    JB = 16
                    zbp = RPool(s2, nc, "zbC", [128, JB, 512], BF16, 3)
                    for ti_, (t_, b_) in enumerate(zbp.tiles):
                        pr.add("dve", MSET(t_[:].rearrange("p a c -> p (a c)"), 0.0), writes=[b_])
                    sop = [RPool(s2, nc, f"soC{p_}", [128, JB, 512], BF16, 2) for p_ in range(2)]
                    for pi_, sp_ in enumerate(sop):
                        for t_, b_ in sp_.tiles:
                            zero_init(pr, t_, b_, "act" if pi_ == 0 else "dve", zc, bzc)
                    pp = RPool(s2, nc, "ppC", [128, 512], F32, 8, psum=True)
                    for j0 in range(0, 128, JB):
                        zb, bzb = zbp.next()
                        pr.dma("sp", zb[0:BP, :, :], DV(zsrc, (toff + j0) * zrs + zco, [[128 * zrs, BP], [zrs, JB], [1, 512]]), reads=[bz], writes=[bzb])
                        sos = [sop[0].next(), sop[1].next()]
                        for jj in range(JB):
                            for part, nm in enumerate(("A_re", "A_im")):
                                at, bat = At[(nm, ui)]
                                so, bso = sos[part]
                                ps, bps = pp.next()
                                pr.add("pe", MM(ps[:, :], at[:, :], zb[:, jj, :], True, True), reads=[bat, bzb], writes=[bps])
                                if part == 0:
                                    pr.add("act", ACTF(so[0:Be, jj, :], ps[0:Be, :], AF.Copy), reads=[bps], writes=[bso])
                                else:
                                    pr.add("dve", CP(so[0:Be, jj, :], ps[0:Be, :]), reads=[bps], writes=[bso])
                        for part in range(2):
                            so, bso = sos[part]
                            pr.dma("act" if part == 0 else "pool", DV(S1, (part * Bm * 128 + j0) * 512, [[128 * 512, BP], [512, JB], [1, 512]]), so[0:BP, :, :], reads=[bso], writes=[db["S1"]])
                    pr.flush()
                with contextlib.ExitStack() as s2:
                    FBK = 4
                    xin = RPool(s2, nc, "xinC2", [128, 2, FBK, 512], BF16, 3)
                    khp = [RPool(s2, nc, f"khC2{gc}", [128, 2, FBK, 512], BF16, 3) for gc in range(2)]
                    sop = RPool(s2, nc, "soC2", [128, 2, FBK, 512], BF16, 2)
                    abp = RPool(s2, nc, "abC2", [128, 2, 512], F32, 5)
                    t12p = RPool(s2, nc, "t13C2", [128, 2, 512], BF16, 5)
                    t34p = RPool(s2, nc, "t24C2", [128, 2, 512], BF16, 5)
                    yrp = RPool(s2, nc, "yrC2", [128, 512], BF16, 5)
                    yip = RPool(s2, nc, "yiC2", [128, 512], BF16, 5)
                    pab = RPool(s2, nc, "pabC2", [128, 512], F32, 4, psum=True)
                    por = RPool(s2, nc, "porC2", [128, 512], F32, 4, psum=True)
                    for sbi in range(nb):
                        for f0 in range(0, B, FBK):
                            nf = min(FBK, B - f0)
                            fe0 = sbi * B + f0
                            xi, bxi = xin.next()
                            for p_ in range(2):
                                pr.dma("sp", xi[:, p_, 0:nf, :], DV(S1, (p_ * Bm * 128 + fe0 * 128) * 512, [[512, 128], [128 * 512, nf], [1, 512]]),
                                       reads=[db["S1"]], writes=[bxi])
                            khs = []
                            for gc, (g0, gn) in enumerate(GCH):
                                kh, bkh = khp[gc].next()
                                for p_ in range(2):
                                    pr.dma("sp", kh[0:128, p_, 0:nf, :], DV(KH, (gc * 128 * 2 * B + p_ * B + f0) * 512, [[2 * B * 512, 128], [512, nf], [1, 512]]),
                                           reads=[bKH], writes=[bkh])
                                khs.append((kh, bkh))
                            so, bso = sop.next()
                            items = [(ff, gc) for ff in range(nf) for gc in range(2)]

                            def fwd(item):
                                ff, gc = item
                                g0, gn = GCH[gc]
                                pa, bpa = pab.next()
                                pb, bpb = pab.next()
                                pr.add("pe", MM(pa[0:gn, :], Bt["Bc"][0][:, g0:g0 + gn], xi[:, 0, ff, :], True, False), reads=[Bt["Bc"][1], bxi], writes=[bpa])
                                pr.add("pe", MM(pa[0:gn, :], Bt["Bs"][0][:, g0:g0 + gn], xi[:, 1, ff, :], False, True), reads=[Bt["Bs"][1], bxi], writes=[bpa])
                                pr.add("pe", MM(pb[0:gn, :], Bt["Bc"][0][:, g0:g0 + gn], xi[:, 1, ff, :], True, False), reads=[Bt["Bc"][1], bxi], writes=[bpb])
                                pr.add("pe", MM(pb[0:gn, :], Bt["Bsn"][0][:, g0:g0 + gn], xi[:, 0, ff, :], False, True), reads=[Bt["Bsn"][1], bxi], writes=[bpb])
                                ab, bab = abp.next()
                                pr.add("act", ACTF(ab[0:gn, 0, :], pa[0:gn, :], AF.Copy), reads=[bpa], writes=[bab])
                                pr.add("act", ACTF(ab[0:gn, 1, :], pb[0:gn, :], AF.Copy), reads=[bpb], writes=[bab])
                                kh, bkh = khs[gc]
                                t13, bt13 = t12p.next()
                                t24, bt24 = t34p.next()
                                khv = kh[0:gn, :, ff, :]
                                pr.add("dve", TT(t13[0:gn, :, :], ab[0:gn, 0:1, :].broadcast_to([gn, 2, 512]), khv, ALU.mult), reads=[bab, bkh], writes=[bt13])
                                pr.add("dve", TT(t24[0:gn, :, :], ab[0:gn, 1:2, :].broadcast_to([gn, 2, 512]), khv, ALU.mult), reads=[bab, bkh], writes=[bt24])
                                return (t13, bt13, t24, bt24)
                            state = {}

                            def inv(item, fw):
                                ff, gc = item
                                g0, gn = GCH[gc]
                                t13, bt13, t24, bt24 = fw
                                if gc == 0:
                                    state["re"] = por.next()
                                    state["im"] = por.next()
                                pre, bre = state["re"]
                                pim, bim = state["im"]
                                ic, ibc, _ = IB["IBc"][gc]
                                icn, ibcn, _ = IB["IBcn"][gc]
                                is_, ibs, _ = IB["IBs"][gc]
                                isn, ibsn, _ = IB["IBsn"][gc]
                                ap_, aq_, bp_, bq_ = t13[0:gn, 0, :], t13[0:gn, 1, :], t24[0:gn, 0, :], t24[0:gn, 1, :]
                                pr.add("pe", MM(pre[:], ic[0:gn, :], ap_, gc == 0, False), reads=[ibc, bt13], writes=[bre])
                                pr.add("pe", MM(pre[:], icn[0:gn, :], bq_, False, False), reads=[ibcn, bt24], writes=[bre])
                                pr.add("pe", MM(pre[:], isn[0:gn, :], aq_, False, False), reads=[ibsn, bt13], writes=[bre])
                                pr.add("pe", MM(pre[:], isn[0:gn, :], bp_, False, gc == 1), reads=[ibsn, bt24], writes=[bre])
                                pr.add("pe", MM(pim[:], is_[0:gn, :], ap_, gc == 0, False), reads=[ibs, bt13], writes=[bim])
                                pr.add("pe", MM(pim[:], isn[0:gn, :], bq_, False, False), reads=[ibsn, bt24], writes=[bim])
                                pr.add("pe", MM(pim[:], ic[0:gn, :], aq_, False, False), reads=[ibc, bt13], writes=[bim])
                                pr.add("pe", MM(pim[:], ic[0:gn, :], bp_, False, gc == 1), reads=[ibc, bt24], writes=[bim])
                                if gc == 1:
                                    pr.add("act", ACTF(so[:, 0, ff, :], pre[:], AF.Copy), reads=[bre], writes=[bso])
                                    pr.add("act", ACTF(so[:, 1, ff, :], pim[:], AF.Copy), reads=[bim], writes=[bso])
                            LA = 2
                            fq = [fwd(items[i_]) for i_ in range(min(LA, len(items)))]
                            for ii, item in enumerate(items):
                                if ii + LA < len(items):
                                    fq.append(fwd(items[ii + LA]))
                                inv(item, fq.pop(0))
                            for p_ in range(2):
                                pr.dma("act", DV(S2, (p_ * Bm * 128 + fe0 * 128) * 512, [[512, 128], [128 * 512, nf], [1, 512]]), so[:, p_, 0:nf, :],
                                       reads=[bso], writes=[db["S2"]])
                    pr.flush()
                with contextlib.ExitStack() as s2:
                    JB = JB3
                    xin = RPool(s2, nc, "xinC3", [128, 2, JB, 512], BF16, 2)
                    for ti_, (t_, b_) in enumerate(xin.tiles):
                        pr.add("dve", MSET(t_[:].rearrange("p a b c -> p (a b c)"), 0.0), writes=[b_])
                    xp3 = RPool(s2, nc, "xC3", [128, JB, 512], BF16, 3)
                    op3 = RPool(s2, nc, "oC3", [128, JB, 512], BF16, 3)
                    for ti_, (t_, b_) in enumerate(op3.tiles):
                        zero_init(pr, t_, b_, "dve", zc, bzc)
                    pp = RPool(s2, nc, "ppC3", [128, 512], F32, 8, psum=True)
                    are, bare = At[("IA_re", ui)]
                    aim, baim = At[("IA_im", ui)]
                    for j0 in range(0, 128, JB):
                        xi, bxi = xin.next()
                        for p_ in range(2):
                            pr.dma("sp", xi[0:BP, p_, :, :], DV(S2, (p_ * Bm * 128 + j0) * 512, [[128 * 512, BP], [512, JB], [1, 512]]), reads=[db["S2"]], writes=[bxi])
                        xt_, bxt = xp3.next()
                        pr.dma("sp", xt_[0:BP, :, :], DV(sc["UC"], (toff + j0) * 1536 + n * 512, [[128 * 1536, BP], [1536, JB], [1, 512]]), reads=[db["UC"]], writes=[bxt])
                        o, bo = op3.next()
                        for jj in range(JB):
                            ps, bps = pp.next()
                            pr.add("pe", MM(ps[:, :], are[:, :], xi[:, 0, jj, :], True, False), reads=[bare, bxi], writes=[bps])
                            pr.add("pe", MM(ps[:, :], aim[:, :], xi[:, 1, jj, :], False, True), reads=[baim, bxi], writes=[bps])
                            pr.add("dve", TT(o[0:Be, jj, :], xt_[0:Be, jj, :], ps[0:Be, :], ALU.mult), reads=[bxt, bps], writes=[bo])
                        pr.dma("pool", DV(zdst, (toff + j0) * 512, [[128 * 512, BP], [512, JB], [1, 512]]), o[0:BP, :, :], reads=[bo], writes=[bzd])
                    pr.flush()


NCORES = 8
_CACHE = {}


def kernel(**inputs):
    xp = np.ascontiguousarray(np.asarray(inputs["x_prompt"], dtype=np.float32))
    xs = np.ascontiguousarray(np.asarray(inputs["x_sample"], dtype=np.float32))
    nbp, Lxp = xp.shape[0], xp.shape[1]
    nbs, Lxs = xs.shape[0], xs.shape[1]
    assert nbp % NCORES == 0 and nbs % NCORES == 0
    npp, nps = nbp // NCORES, nbs // NCORES
    key = (npp, Lxp, nps, Lxs)
    if key not in _CACHE:
        specs = [("p", i, Lxp) for i in range(npp)] + [("s", i, Lxs) for i in range(nps)]
        _CACHE[key] = build_program(specs, npp, Lxp, nps, Lxs)
    nc, k = _CACHE[key]
    shared = {}
    for nme in ("meta_tokens", "pre_mix_g", "w_in", "q_norm_g", "w_uq", "kv_norm_g", "w_ukv", "conv_w", "f_w1", "f_b1",
                "f_freq", "f_w2", "f_b2", "f_w3", "f_decay", "hy_bias", "attn_out_g", "hy_out_g", "w_o", "post_mix_g",
                "pre_mlp_g", "w_ff1", "w_ff2", "post_mlp_g"):
        shared[nme] = np.ascontiguousarray(np.asarray(inputs[nme], dtype=np.float32))
    for nme, arr in k.host.items():
        shared[nme] = arr
    in_maps = []
    for c in range(NCORES):
        m = dict(shared)
        m["xp"] = xp[c * npp:(c + 1) * npp]
        m["xs"] = xs[c * nps:(c + 1) * nps]
        in_maps.append(m)
    res = run_bass_kernel_spmd(nc, in_maps, core_ids=list(range(NCORES)))
    yp = np.concatenate([np.asarray(r["yp"], dtype=np.float32) for r in res.results], axis=0)
    ys = np.concatenate([np.asarray(r["ys"], dtype=np.float32) for r in res.results], axis=0)
    return (yp, ys)
```
